# Optimizing a Trainium2 kernel written in Bass

```python
import jax, jax.numpy as jnp
from jax import lax
import numpy as np

D_MODEL = 2048
BATCH = 8
SEQ = 4096
DEPTH = 4

D_MIX = D_MODEL
HEAD_DIM = 128
D_ATTN = D_MIX // 2
D_GMLP = D_MIX - D_ATTN
N_HEADS = D_ATTN // HEAD_DIM
GMLP_GROUP = 128
N_GMLP_GROUPS = D_GMLP // GMLP_GROUP
CHUNK = 128
DILATED_PATTERNS = ((128, 1), (512, 4), (2048, 16))
BLK = 64
ROPE_THETA = 10000.0
D_FF = -(-(8 * D_MODEL) // (3 * 256)) * 256
D_IN = 3 * D_ATTN + 2 * D_GMLP
EPS = 1e-6
NEG = -1e30

kernel_name = "hybrid_dilated_attn_gmlp_encoder"


def rmsnorm(x, g):
    xf = x.astype(jnp.float32)
    y = xf * lax.rsqrt(jnp.mean(xf * xf, axis=-1, keepdims=True) + EPS)
    return (y * g.astype(jnp.float32)).astype(x.dtype)


def layernorm(x, g):
    xf = x.astype(jnp.float32)
    mu = jnp.mean(xf, axis=-1, keepdims=True)
    xc = xf - mu
    y = xc * lax.rsqrt(jnp.mean(xc * xc, axis=-1, keepdims=True) + EPS)
    return (y * g.astype(jnp.float32)).astype(x.dtype)


def rope_tables(seq):
    pos = jnp.arange(seq, dtype=jnp.float32)
    inv = ROPE_THETA ** (-jnp.arange(0, HEAD_DIM, 2, dtype=jnp.float32) / HEAD_DIM)
    ang = pos[:, None] * inv[None, :]
    return jnp.cos(ang), jnp.sin(ang)


def apply_rope(t, cos, sin):
    tf = t.astype(jnp.float32)
    t1, t2 = jnp.split(tf, 2, axis=-1)
    c = cos[None, :, None, :]
    s = sin[None, :, None, :]
    out = jnp.concatenate([t1 * c - t2 * s, t1 * s + t2 * c], axis=-1)
    return out.astype(t.dtype)


def _band_blocks(t, nb):
    return jnp.concatenate([t[:, :, 0:nb], t[:, :, 1:nb + 1], t[:, :, 2:nb + 2]], axis=3)


def dilated_window_branch(q, k, v, window, dilation):
    B, S, H, Dh = q.shape
    n_side = window // (2 * dilation)
    span = dilation * BLK
    s_pad = -(-S // span) * span
    L = s_pad // dilation
    nb = L // BLK

    def to_sub(t):
        t = jnp.pad(t, ((0, 0), (0, s_pad - S), (0, 0), (0, 0)))
        return t.reshape(B, L, dilation, H, Dh).transpose(0, 2, 1, 3, 4)

    qs, ks, vs = to_sub(q), to_sub(k), to_sub(v)
    valid = (jnp.arange(s_pad) < S).reshape(L, dilation).T

    qb = qs.reshape(B, dilation, nb, BLK, H, Dh)
    pad_kv = ((0, 0), (0, 0), (BLK, BLK), (0, 0), (0, 0))
    kb = _band_blocks(jnp.pad(ks, pad_kv).reshape(B, dilation, nb + 2, BLK, H, Dh), nb)
    vb = _band_blocks(jnp.pad(vs, pad_kv).reshape(B, dilation, nb + 2, BLK, H, Dh), nb)
    vk = jnp.pad(valid, ((0, 0), (BLK, BLK))).reshape(dilation, nb + 2, BLK)
    vk = jnp.concatenate([vk[:, 0:nb], vk[:, 1:nb + 1], vk[:, 2:nb + 2]], axis=2)

    scale = HEAD_DIM ** -0.5
    s = jnp.einsum('brnqhd,brnkhd->brnhqk', qb, kb,
                   preferred_element_type=jnp.float32) * scale
    rel = jnp.arange(3 * BLK)[None, :] - BLK - jnp.arange(BLK)[:, None]
    band = jnp.abs(rel) <= n_side
    mask = band[None, None, None, None] & vk[None, :, :, None, None, :]
    s = jnp.where(mask, s, NEG)
    m = jnp.max(s, axis=-1, keepdims=True)
    p = jnp.exp(s - m)
    denom = jnp.sum(p, axis=-1)
    o = jnp.einsum('brnhqk,brnkhd->brnqhd', p, vb.astype(jnp.float32))
    o = o / jnp.moveaxis(denom, -1, -2)[..., None]
    lse = jnp.moveaxis(m[..., 0] + jnp.log(denom), -1, -2)

    o = o.reshape(B, dilation, L, H, Dh).transpose(0, 2, 1, 3, 4).reshape(B, s_pad, H, Dh)[:, :S]
    lse = lse.reshape(B, dilation, L, H).transpose(0, 2, 1, 3).reshape(B, s_pad, H)[:, :S]
    return o, lse


def dilated_attention(q, k, v):
    outs, lses = [], []
    for window, dilation in DILATED_PATTERNS:
        o, l = dilated_window_branch(q, k, v, window, dilation)
        outs.append(o)
        lses.append(l)
    w = jax.nn.softmax(jnp.stack(lses, axis=0), axis=0)
    o = jnp.sum(w[..., None] * jnp.stack(outs, axis=0), axis=0)
    B, S = q.shape[0], q.shape[1]
    return o.reshape(B, S, D_ATTN).astype(q.dtype)


def chunked_spatial_gating(uv, ln_g, w_s, b_s):
    uv = jax.nn.gelu(uv, approximate=False)
    u, v = jnp.split(uv, 2, axis=-1)
    v = layernorm(v, ln_g)
    B, S, _ = v.shape
    vg = v.reshape(B, S // CHUNK, CHUNK, N_GMLP_GROUPS, GMLP_GROUP)
    mixed = jnp.einsum('gpq,bnqgc->bnpgc', w_s, vg) + b_s.T[None, None, :, :, None]
    return u * mixed.reshape(B, S, D_GMLP)


def setup_inputs(seed: int = 0) -> dict:
    key = jax.random.key(seed)
    ks = jax.random.split(key, 14)
    f32 = jnp.float32
    nrm = lambda k, shape, scale: jax.random.normal(k, shape, f32) * scale
    gain = lambda k, shape: 1.0 + 0.02 * jax.random.normal(k, shape, f32)
    return {
        "x": jax.random.normal(ks[0], (BATCH, SEQ, D_MODEL), f32),
        "norm1_g": gain(ks[1], (DEPTH, D_MODEL)),
        "w_in": nrm(ks[2], (DEPTH, D_MODEL, D_IN), D_MODEL ** -0.5),
        "gmlp_ln_g": gain(ks[3], (DEPTH, D_GMLP)),
        "w_spatial": nrm(ks[4], (DEPTH, N_GMLP_GROUPS, CHUNK, CHUNK), CHUNK ** -0.5),
        "b_spatial": gain(ks[5], (DEPTH, N_GMLP_GROUPS, CHUNK)),
        "mix_norm_attn_g": gain(ks[6], (DEPTH, D_ATTN)),
        "mix_norm_gmlp_g": gain(ks[7], (DEPTH, D_GMLP)),
        "w_out": nrm(ks[8], (DEPTH, D_MIX, D_MODEL), D_MIX ** -0.5),
        "norm2_g": gain(ks[9], (DEPTH, D_MODEL)),
        "w_gate": nrm(ks[10], (DEPTH, D_MODEL, D_FF), D_MODEL ** -0.5),
        "w_up": nrm(ks[11], (DEPTH, D_MODEL, D_FF), D_MODEL ** -0.5),
        "w_down": nrm(ks[12], (DEPTH, D_FF, D_MODEL), D_FF ** -0.5),
        "final_g": gain(ks[13], (D_MODEL,)),
    }


def reference(x, norm1_g, w_in, gmlp_ln_g, w_spatial, b_spatial, mix_norm_attn_g,
              mix_norm_gmlp_g, w_out, norm2_g, w_gate, w_up, w_down, final_g):
    B, S, _ = x.shape
    cos, sin = rope_tables(S)
    for l in range(DEPTH):
        h = rmsnorm(x, norm1_g[l])
        proj = jnp.einsum('bsd,de->bse', h, w_in[l])
        q = proj[..., 0:D_ATTN].reshape(B, S, N_HEADS, HEAD_DIM)
        k = proj[..., D_ATTN:2 * D_ATTN].reshape(B, S, N_HEADS, HEAD_DIM)
        v = proj[..., 2 * D_ATTN:3 * D_ATTN].reshape(B, S, N_HEADS, HEAD_DIM)
        uv = proj[..., 3 * D_ATTN:]
        q = apply_rope(q, cos, sin)
        k = apply_rope(k, cos, sin)
        a = dilated_attention(q, k, v)
        g = chunked_spatial_gating(uv, gmlp_ln_g[l], w_spatial[l], b_spatial[l])
        mix = jnp.concatenate([rmsnorm(a, mix_norm_attn_g[l]),
                               rmsnorm(g, mix_norm_gmlp_g[l])], axis=-1)
        x = x + jnp.einsum('bse,ed->bsd', mix, w_out[l])
        h = rmsnorm(x, norm2_g[l])
        ff = jax.nn.silu(jnp.einsum('bsd,df->bsf', h, w_gate[l])) * jnp.einsum('bsd,df->bsf', h, w_up[l])
        x = x + jnp.einsum('bsf,fd->bsd', ff, w_down[l])
    return rmsnorm(x, final_g)
```

```python
import bisect
import math
from contextlib import ExitStack

import numpy as np
import ml_dtypes

import concourse.bass as bass
import concourse.mybir as mybir
from concourse.bass_utils import run_bass_kernel_spmd

F32 = mybir.dt.float32
BF16 = mybir.dt.bfloat16
ALU = mybir.AluOpType
AF = mybir.ActivationFunctionType

EPS = 1e-6
import os
PATTERNS = tuple(int(v) for v in os.environ.get("DBG_PAT", "1,4,16").split(","))
ENGS = ("pe", "act", "dve", "pool", "sp")
SEM_LIMIT = 30000


def derive(cfg):
    c = dict(cfg)
    D = c["D"]
    c["DA"] = D // 2
    c["DG"] = D - c["DA"]
    c["H"] = c["DA"] // 128
    c["G"] = c["DG"] // 128
    c["DFF"] = -(-(8 * D) // (3 * 256)) * 256
    c["KD"] = D // 128
    c["KF"] = c["DFF"] // 128
    c["CWV"] = min(512, c["DA"])
    c["NVT"] = c["DA"] // c["CWV"]
    c["NA"] = 2 * c["H"] + c["G"] + c["KD"] + 2 * c["KF"]
    c["NB"] = 2 * c["NVT"]
    return c


class Buf:
    __slots__ = ("ap", "w", "r", "rd", "dsem", "dval", "name")

    def __init__(self, ap, name=""):
        self.ap = ap
        self.w = None
        self.r = {}
        self.rd = []
        self.dsem = None
        self.dval = 0
        self.name = name


class _Rec:
    def __init__(self):
        self.call = None

    def __getattr__(self, name):
        def f(*a, **k):
            assert self.call is None
            self.call = (name, a, k)
            return self
        return f


class Sched:
    def __init__(self, sem_pool):
        self.pool = list(sem_pool)
        self.streams = {e: [] for e in ENGS}
        self.engsem = {}
        self.sig_seq = {e: [] for e in ENGS}
        self.sig_tok = {e: [] for e in ENGS}
        self.seq = {e: 0 for e in ENGS}
        self.seen = {e: {} for e in ENGS}
        self.dsems = []
        for e in ("pe", "act", "dve", "pool"):
            self.engsem[e] = [self.pool.pop(), 0]

    def _resolve(self, tok):
        if tok[0] == "d":
            return tok[1], tok[2]
        _, eng, seq = tok
        i = bisect.bisect_left(self.sig_seq[eng], seq)
        assert i < len(self.sig_seq[eng]), "dependency on unsignaled op"
        return self.sig_tok[eng][i]

    def _wait(self, eng, tok):
        sem, val = self._resolve(tok)
        key = id(sem)
        if self.seen[eng].get(key, 0) >= val:
            return
        self.seen[eng][key] = val
        self.streams[eng].append(("w", sem, val))

    def _deps(self, eng, reads, writes):
        for b in reads:
            if b.w is not None:
                self._dep(eng, b.w, "raw")
        for b in writes:
            if b.w is not None:
                self._dep(eng, b.w, "waw")
            for t in b.r.values():
                self._dep(eng, t, "war")
            for t in b.rd:
                self._dep(eng, t, "war")

    def _dep(self, eng, tok, kind):
        if tok[0] == "e" and tok[1] == eng:
            if eng == "pe" or kind == "war":
                return
        self._wait(eng, tok)

    def op(self, eng, fn, reads=(), writes=(), signal=True):
        self._deps(eng, reads, writes)
        seq = self.seq[eng]
        self.seq[eng] += 1
        tok = ("e", eng, seq)
        sem = None
        if signal:
            es = self.engsem[eng]
            if es[1] >= SEM_LIMIT:
                es[0] = self.pool.pop()
                es[1] = 0
            es[1] += 1
            sem = es[0]
            self.sig_seq[eng].append(seq)
            self.sig_tok[eng].append((sem, es[1]))
        rec = _Rec()
        fn(rec)
        assert rec.call is not None
        self.streams[eng].append(("o", rec.call, sem, 1))
        for b in reads:
            b.r[eng] = tok
        for b in writes:
            b.w = tok
            b.r = {}
            b.rd = []
        return tok

    def dma(self, out_ap, in_ap, sb, load, extra_reads=(), extra_writes=()):
        eng = "sp"
        if load:
            if sb.w is not None and sb.w[0] == "d" and sb.dsem is not None and sb.w[1] is sb.dsem \
                    and not sb.r and not sb.rd:
                pass
            else:
                self._deps(eng, extra_reads, [sb] + list(extra_writes))
        else:
            self._deps(eng, [sb] + list(extra_reads), extra_writes)
        if sb.dsem is None:
            sb.dsem = self.pool.pop()
            self.dsems.append(sb)
        sb.dval += 16
        assert sb.dval < 2 * SEM_LIMIT
        tok = ("d", sb.dsem, sb.dval)
        self.streams[eng].append(("d", out_ap, in_ap, sb.dsem))
        if load:
            sb.w = tok
            sb.r = {}
            sb.rd = []
        else:
            sb.rd.append(tok)
        return tok

    def fence_dma(self, engs=("sp",)):
        for e in engs:
            for b in self.dsems:
                if b.dval:
                    self._wait(e, ("d", b.dsem, b.dval))

    def barrier(self):
        for e in ENGS:
            for o in ("pe", "act", "dve", "pool"):
                if o != e and self.sig_tok[o]:
                    sem, val = self.sig_tok[o][-1]
                    assert self.sig_seq[o][-1] == self.seq[o] - 1, "last op must signal"
                    self._wait(e, ("d", sem, val))
        self.fence_dma(ENGS)

    def replay(self, eng, e):
        for it in self.streams[eng]:
            if it[0] == "w":
                e.wait_ge(it[1], it[2])
            elif it[0] == "o":
                name, a, k = it[1]
                ins = getattr(e, name)(*a, **k)
                if it[2] is not None:
                    ins.then_inc(it[2], 1)
            else:
                e.dma_start(out=it[1], in_=it[2]).then_inc(it[3], 16)


class Arena:
    def __init__(self, t, nwords):
        self.t = t
        self.n = nwords
        self.off = 0

    def reset(self):
        self.off = 0

    def alloc(self, free_shape, dtype, name=""):
        n = int(np.prod(free_shape))
        esz = 4 if dtype == F32 else 2
        nwords = (n * esz + 3) // 4
        nwords = (nwords + 7) // 8 * 8
        assert self.off + nwords <= self.n, f"arena overflow at {name}: {self.off + nwords} > {self.n}"
        ap = self.t[:, self.off:self.off + nwords]
        if dtype != F32:
            ap = ap.bitcast(dtype)
        ap = ap[:, 0:n]
        if len(free_shape) == 2:
            ap = ap.rearrange("p (a b) -> p a b", a=free_shape[0])
        elif len(free_shape) == 3:
            ap = ap.rearrange("p (a b c) -> p a b c", a=free_shape[0], b=free_shape[1])
        self.off += nwords
        return Buf(ap, name)


def build(cfg, debug_outputs=False):
    c = derive(cfg)
    S, D, L, T = c["S"], c["D"], c["DEPTH"], c["T"]
    DA, DG, H, G, DFF, KD, KF = c["DA"], c["DG"], c["H"], c["G"], c["DFF"], c["KD"], c["KF"]
    CWV, NVT, NA, NB = c["CWV"], c["NVT"], c["NA"], c["NB"]
    NST = S // T
    NH = T // 512
    assert T % 512 == 0 and S % T == 0
    TA = KD * 128
    TBc = KD * CWV
    TC = KF * 128
    scale_qk = 1.0 / math.sqrt(128.0)

    nc = bass.Bass("TRN2", target_bir_lowering=False)

    def din(name, shape, dt=F32):
        return nc.dram_tensor(name, list(shape), dt, kind="ExternalInput").ap()

    xT_in = din("xT", [D, S])
    wA_in = din("wA", [L * NA, 128, TA])
    wB_in = din("wB", [L * NB, 128, TBc])
    wC_in = din("wC", [L * KD, 128, TC])
    wsT_in = din("wsT", [128, L * G * 128])
    g1_in = din("g1", [128, L * KD])
    g2_in = din("g2", [128, L * KD])
    ga_in = din("ga", [128, L * H])
    gg_in = din("gg", [128, L * G])
    lng_in = din("lng", [128, L * G])
    bs_in = din("bs", [128, L * G * 128])
    gf_in = din("gf", [128, KD])
    cos_in = din("cosT", [128, S])
    sin_in = din("sinS", [128, S])
    band_in = din("band", [128, 384])
    outT = nc.dram_tensor("outT", [D, S], F32, kind="ExternalOutput").ap()

    dk = {}
    xT = nc.dram_tensor("xT_s", [D, S], F32, **dk).ap()
    wA = nc.dram_tensor("wA_b", [L * NA, 128, TA], BF16).ap()
    wB = nc.dram_tensor("wB_b", [L * NB, 128, TBc], BF16).ap()
    wC = nc.dram_tensor("wC_b", [L * KD, 128, TC], BF16).ap()
    qT = nc.dram_tensor("qT_s", [DA, S], BF16, **dk).ap()
    kT = nc.dram_tensor("kT_s", [DA, S], BF16, **dk).ap()
    vS = nc.dram_tensor("v_s", [S, DA], BF16, **dk).ap()
    aT = nc.dram_tensor("aT_s", [DA, S], BF16, **dk).ap()
    mgT = nc.dram_tensor("mgT_s", [DG, S], BF16, **dk).ap()

    es = ExitStack()
    with es:
        sems = [es.enter_context(nc.semaphore(f"s{i}")) for i in range(96)]
        ARENA_WORDS = (nc.SBUF_PARTITION_SIZE_BYTES - 16384 - 2048) // 4
        arena_t = es.enter_context(nc.sbuf_tensor("arena", [128, ARENA_WORDS], F32))
        banks = [Buf(es.enter_context(nc.psum_tensor(f"ps{i}", [128, 512], F32))[:, :], f"ps{i}")
                 for i in range(8)]
        sch = Sched(sems)
        ar = Arena(arena_t, ARENA_WORDS)

        ones_b = ar.alloc([128], BF16, "ones")
        ones_f = ar.alloc([128], F32, "onesf")
        band_b = ar.alloc([384], BF16, "band")
        band_f = ar.alloc([384], F32, "bandf")
        g1_b = ar.alloc([L * KD], F32, "g1")
        g2_b = ar.alloc([L * KD], F32, "g2")
        ga_b = ar.alloc([L * H], F32, "ga")
        gg_b = ar.alloc([L * G], F32, "gg")
        lng_b = ar.alloc([L * G], F32, "lng")
        gf_b = ar.alloc([KD], F32, "gf")
        wsT_b = ar.alloc([L * G * 128], BF16, "wsT")
        eps_b = ar.alloc([8], F32, "eps")
        const_end = ar.off

        def phase_begin():
            sch.barrier()
            ar.off = const_end

        ar.off = const_end
        tmpc = ar.alloc([L * G * 128], F32, "tmpc")
        for dst, src in ((g1_b, g1_in), (g2_b, g2_in), (ga_b, ga_in), (gg_b, gg_in),
                         (lng_b, lng_in), (gf_b, gf_in), (band_f, band_in), (tmpc, wsT_in)):
            sch.dma(dst.ap, src, dst, True)
        sch.op("dve", lambda e: e.memset(ones_b.ap, 1.0), writes=[ones_b])
        sch.op("dve", lambda e: e.memset(ones_f.ap, 1.0), writes=[ones_f])
        sch.op("dve", lambda e: e.memset(eps_b.ap, EPS), writes=[eps_b])
        sch.op("dve", lambda e: e.tensor_copy(out=band_b.ap, in_=band_f.ap), reads=[band_f], writes=[band_b])
        sch.op("dve", lambda e: e.tensor_copy(out=wsT_b.ap, in_=tmpc.ap), reads=[tmpc], writes=[wsT_b])

        PC = 2048
        cast_pieces = []
        cast_mark_A = []
        cast_mark_R = []

        def add_pieces(src, dst, t_lo, t_hi, tcols):
            for t in range(t_lo, t_hi):
                for c0 in range(0, tcols, PC):
                    cw = min(PC, tcols - c0)
                    cast_pieces.append((src[t, :, c0:c0 + cw], dst[t, :, c0:c0 + cw], cw))

        for l in range(L):
            add_pieces(wA_in, wA, l * NA, l * NA + 2 * H + G, TA)
            add_pieces(wB_in, wB, l * NB, (l + 1) * NB, TBc)
            cast_mark_A.append(len(cast_pieces))
            add_pieces(wA_in, wA, l * NA + 2 * H + G, (l + 1) * NA, TA)
            add_pieces(wC_in, wC, l * KD, (l + 1) * KD, TC)
            cast_mark_R.append(len(cast_pieces))
        cast_state = {"next": 0, "n": 0, "m": 0}
        cast_fl = []
        cast_engs = ("dve", "act")

        def emit_cast(upto, stg, stb, flush=False):
            def issue():
                src, dst, cw = cast_pieces[cast_state["next"]]
                cast_state["next"] += 1
                a = stg[cast_state["n"] % len(stg)]
                cast_state["n"] += 1
                sch.dma(a.ap[:, 0:cw], src, a, True)
                cast_fl.append((a, dst, cw))

            def finish():
                a, dst, cw = cast_fl.pop(0)
                b = stb[cast_state["m"] % len(stb)]
                ce = cast_engs[cast_state["m"] % 2]
                cast_state["m"] += 1
                if ce == "act":
                    sch.op("act", lambda e: e.copy(out=b.ap[:, 0:cw], in_=a.ap[:, 0:cw]), reads=[a], writes=[b])
                else:
                    sch.op(ce, lambda e: e.tensor_copy(out=b.ap[:, 0:cw], in_=a.ap[:, 0:cw]), reads=[a], writes=[b])
                sch.dma(dst, b.ap[:, 0:cw], b, False)

            while cast_state["next"] < upto:
                issue()
                if len(cast_fl) > 3:
                    finish()
            if flush:
                while cast_fl:
                    finish()

        phase_begin()
        stg0 = [ar.alloc([PC], F32, f"stg{i}") for i in range(5)]
        stb0 = [ar.alloc([PC], BF16, f"stb{i}") for i in range(4)]
        emit_cast(cast_mark_A[0], stg0, stb0, flush=True)

        def mm_group(ps_list, lhs_fn, rhs_fn, nk, reads):
            for hi, ps in enumerate(ps_list):
                for k in range(nk):
                    last = (k == nk - 1)
                    sch.op("pe",
                           lambda e, ps=ps, k=k, hi=hi, last=last: e.matmul(
                               ps.ap, lhsT=lhs_fn(k), rhs=rhs_fn(k, hi),
                               start=(k == 0), stop=last),
                           reads=reads(k), writes=[ps], signal=last)

        def rms_stats(chunk_bufs, sq_ring, stat_banks, rstd, sqt, dim):
            n = len(chunk_bufs)
            for k, cb in enumerate(chunk_bufs):
                sq = sq_ring[k % len(sq_ring)]
                cbuf, cap = cb if isinstance(cb, tuple) else (cb, cb.ap)
                sch.op("act", lambda e, cap=cap, sq=sq: e.activation(out=sq.ap, in_=cap, func=AF.Square),
                       reads=[cbuf], writes=[sq])
                for hi in range(NH):
                    ps = stat_banks[hi]
                    last = (k == n - 1)
                    sch.op("pe", lambda e, ps=ps, sq=sq, hi=hi, k=k, last=last: e.matmul(
                        ps.ap, lhsT=ones_b.ap, rhs=sq.ap[:, hi * 512:(hi + 1) * 512], start=(k == 0), stop=last),
                        reads=[sq, ones_b], writes=[ps], signal=(last or hi == NH - 1))
            for hi in range(NH):
                ps = stat_banks[hi]
                sch.op("act", lambda e, ps=ps, hi=hi: e.activation(
                    out=sqt.ap[:, hi * 512:(hi + 1) * 512], in_=ps.ap, func=AF.Sqrt, bias=eps_b.ap[:, 0:1], scale=1.0 / dim),
                    reads=[ps, eps_b], writes=[sqt])
            sch.op("dve", lambda e: e.reciprocal(out=rstd.ap, in_=sqt.ap), reads=[sqt], writes=[rstd])


        class Ring:
            def __init__(self, bl):
                self.b = bl
                self.i = 0
                self.pend = set()

            def take(self):
                b = self.b[self.i % len(self.b)]
                self.i += 1
                assert id(b) not in self.pend, "ring slot reused before its consumer was emitted"
                self.pend.add(id(b))
                return b

            def done(self, b):
                self.pend.discard(id(b))

        def run_jobs(jobs, d=1):
            n = len(jobs)
            loaded = [False] * n
            for i in range(n):
                if not loaded[i]:
                    if jobs[i][0] is not None:
                        jobs[i][0]()
                    loaded[i] = True
                for k in range(i + 1, min(n, i + 1 + d)):
                    if loaded[k]:
                        continue
                    if not jobs[k][2]:
                        break
                    if jobs[k][0] is not None:
                        jobs[k][0]()
                    loaded[k] = True
                jobs[i][1]()

        def norm_jobs(l, s, xsrc, g_b, hT, xring, sq_ring, stat_banks, rstd, sqt, extra_load=None):
            t0 = s * T
            jobs = []
            for kc in range(KD):
                st_ = {}

                def ld(kc=kc, st_=st_):
                    xb = xring.take()
                    st_["xb"] = xb
                    sch.dma(xb.ap, xsrc[kc * 128:(kc + 1) * 128, t0:t0 + T], xb, True)

                def cp(kc=kc, st_=st_):
                    xb = st_["xb"]
                    sq = sq_ring[kc % len(sq_ring)]
                    sch.op("act", lambda e: e.activation(out=sq.ap, in_=xb.ap, func=AF.Square),
                           reads=[xb], writes=[sq])
                    for hi in range(NH):
                        ps = stat_banks[hi]
                        last = (kc == KD - 1)
                        sch.op("pe", lambda e: e.matmul(
                            ps.ap, lhsT=ones_b.ap, rhs=sq.ap[:, hi * 512:(hi + 1) * 512], start=(kc == 0), stop=last),
                            reads=[sq, ones_b], writes=[ps], signal=(last or hi == NH - 1))
                    sch.op("dve", lambda e: e.tensor_scalar(
                        out=hT[kc].ap, in0=xb.ap, scalar1=g_b.ap[:, l * KD + kc:l * KD + kc + 1], scalar2=1.0,
                        op0=ALU.mult, op1=ALU.mult), reads=[xb, g_b], writes=[hT[kc]])
                    xring.done(xb)
                jobs.append((ld, cp, True))

            def fin():
                for hi in range(NH):
                    ps = stat_banks[hi]
                    sch.op("act", lambda e: e.activation(
                        out=sqt.ap[:, hi * 512:(hi + 1) * 512], in_=ps.ap, func=AF.Sqrt, bias=eps_b.ap[:, 0:1],
                        scale=1.0 / D), reads=[ps, eps_b], writes=[sqt])
                sch.op("dve", lambda e: e.reciprocal(out=rstd.ap, in_=sqt.ap), reads=[sqt], writes=[rstd])
                for kc in range(KD):
                    eng = "pool" if kc % 4 == 3 else "dve"
                    sch.op(eng, lambda e: e.tensor_tensor(out=hT[kc].ap, in0=hT[kc].ap, in1=rstd.ap, op=ALU.mult),
                           reads=[hT[kc], rstd], writes=[hT[kc]])
            jobs.append((extra_load, fin, False))
            return jobs

        def fm_tile_job(wring, w_src, pp_fn, rhs_bufs, nk, evac):
            st_ = {}

            def ld():
                w = wring.take()
                st_["w"] = w
                sch.dma(w.ap, w_src, w, True)

            def cp():
                w = st_["w"]
                pp = pp_fn()
                for hi in range(NH):
                    ps = pp[hi]
                    for k in range(nk):
                        last = (k == nk - 1)
                        rb, rap = rhs_bufs[k]
                        sch.op("pe", lambda e: e.matmul(
                            ps.ap, lhsT=w.ap[:, k * 128:(k + 1) * 128], rhs=rap[:, hi * 512:(hi + 1) * 512],
                            start=(k == 0), stop=last), reads=[w, rb], writes=[ps], signal=last)
                wring.done(w)
                evac(pp)
            return (ld, cp, True)

        _pcache = {}

        def bufs(name, fn):
            if name not in _pcache:
                ar.off = const_end
                _pcache[name] = fn()
            return _pcache[name]

        for l in range(L):
            phase_begin()
            def allocA():
                hT = [[ar.alloc([T], BF16, f"hT{i}_{k}") for k in range(KD)] for i in range(2)]
                xst = [ar.alloc([T], F32, f"xst{i}") for i in range(2)]
                sqr = [ar.alloc([T], BF16, f"sq{i}") for i in range(2)]
                rstd = ar.alloc([T], F32, "rstd")
                sqt = ar.alloc([T], F32, "sqt")
                wfm = [ar.alloc([TA], BF16, f"wfm{i}") for i in range(3)]
                wtm = [ar.alloc([TBc], BF16, f"wtm{i}") for i in range(2)]
                wvg = wtm
                assert NVT <= 2
                cosb = ar.alloc([T], F32, "cos")
                sinb = ar.alloc([T], F32, "sin")
                r1 = [ar.alloc([T], F32, "r1_0")] * 2
                r2 = [ar.alloc([T], F32, "r2_0")] * 2
                qko = [ar.alloc([T], BF16, f"qko{i}") for i in range(2)]
                vout = [ar.alloc([CWV], BF16, f"vout{i}") for i in range(3)]
                uT = [ar.alloc([T], BF16, f"uT{g}") for g in range(G)]
                vg = [ar.alloc([DG], F32, f"vg{i}") for i in range(2)]
                vn = [ar.alloc([DG], BF16, f"vn{i}") for i in range(2)]
                st = [ar.alloc([16], F32, f"st{i}") for i in range(2)]
                GT = uT
                gtmp = [ar.alloc([128], F32, f"gtmp{i}") for i in range(2)]
                mgo = [ar.alloc([T], BF16, f"mgo{i}") for i in range(2)]
                bsb = ar.alloc([G * 128], F32, "bsb")
                return dict(locals())
            nsA = bufs("A", allocA)
            hT, xst, sqr, rstd, sqt, wfm, wtm, wvg, cosb, sinb, r1, r2, qko, vout, uT, vg, vn, st, GT, gtmp, mgo, bsb = (
                nsA[k] for k in ("hT", "xst", "sqr", "rstd", "sqt", "wfm", "wtm", "wvg", "cosb", "sinb", "r1", "r2",
                                 "qko", "vout", "uT", "vg", "vn", "st", "GT", "gtmp", "mgo", "bsb"))
            sch.dma(bsb.ap, bs_in[:, l * G * 128:(l + 1) * G * 128], bsb, True)
            PA = [banks[0:2], banks[2:4]]
            PST = banks[4:6]
            PG = banks[6:8]
            xsrc = xT_in if l == 0 else xT
            xring = Ring(xst)
            wfm_r = Ring(wfm)
            wtm_r = Ring(wtm)
            pctr = {"pa": 0, "bi": 0}

            def next_pp():
                pp = PA[pctr["pa"] % 2]
                pctr["pa"] += 1
                return pp

            jobsA = []
            for s in range(NST):
                t0 = s * T

                def ld_cs(t0=t0):
                    sch.dma(cosb.ap, cos_in[:, t0:t0 + T], cosb, True)
                    sch.dma(sinb.ap, sin_in[:, t0:t0 + T], sinb, True)
                hTs = hT[s % 2]
                if s == 0:
                    jobsA += norm_jobs(l, 0, xsrc, g1_b, hTs, xring, sqr, PST, rstd, sqt, extra_load=ld_cs)
                if s + 1 < NST:
                    def ld_cs_n(t1=(s + 1) * T):
                        sch.dma(cosb.ap, cos_in[:, t1:t1 + T], cosb, True)
                        sch.dma(sinb.ap, sin_in[:, t1:t1 + T], sinb, True)
                    nxt = norm_jobs(l, s + 1, xsrc, g1_b, hT[(s + 1) % 2], xring, sqr, PST, rstd, sqt, extra_load=ld_cs_n)
                else:
                    nxt = []
                fmj = []
                hb = [(hTs[k], hTs[k].ap) for k in range(KD)]
                for j in range(2 * H):
                    def evac_qk(pp, j=j, t0=t0):
                        ra, rb, qo = r1[j % 2], r2[j % 2], qko[j % 2]
                        for hi in range(NH):
                            ps = pp[hi]
                            sl = slice(hi * 512, (hi + 1) * 512)
                            sch.op("dve", lambda e: e.tensor_tensor(
                                out=ra.ap[:, sl], in0=ps.ap, in1=cosb.ap[:, sl], op=ALU.mult),
                                reads=[ps, cosb], writes=[ra])
                            sch.op("dve", lambda e: e.tensor_tensor(
                                out=rb.ap[0:64, sl], in0=ps.ap[64:128, :], in1=sinb.ap[0:64, sl], op=ALU.mult),
                                reads=[ps, sinb], writes=[rb])
                            sch.op("dve", lambda e: e.tensor_tensor(
                                out=rb.ap[64:128, sl], in0=ps.ap[0:64, :], in1=sinb.ap[64:128, sl], op=ALU.mult),
                                reads=[ps, sinb], writes=[rb])
                        sch.op("pool", lambda e: e.tensor_tensor(
                            out=qo.ap, in0=ra.ap, in1=rb.ap, op=ALU.add), reads=[ra, rb], writes=[qo])
                        dst = qT if j < H else kT
                        hh = j % H
                        sch.dma(dst[hh * 128:(hh + 1) * 128, t0:t0 + T], qo.ap, qo, False)
                    fmj.append(fm_tile_job(wfm_r, wA[l * NA + j], next_pp, hb, KD, evac_qk))
                for g in range(G):
                    def evac_u(pp, g=g):
                        for hi in range(NH):
                            ps = pp[hi]
                            sch.op("act", lambda e: e.activation(
                                out=uT[g].ap[:, hi * 512:(hi + 1) * 512], in_=ps.ap, func=AF.Gelu),
                                reads=[ps], writes=[uT[g]])
                    fmj.append(fm_tile_job(wfm_r, wA[l * NA + 2 * H + g], next_pp, hb, KD, evac_u))
                assert len(fmj) >= KD
                for ji, jb in enumerate(fmj):
                    jobsA.append(jb)
                    if nxt and ji < KD:
                        jobsA.append(nxt[ji])
                jobsA += nxt[KD:]
                for nt in range(NVT):
                    stv = {}

                    def ld_v(nt=nt, stv=stv):
                        w = wtm_r.take()
                        stv["w"] = w
                        sch.dma(w.ap, wB[l * NB + nt], w, True)

                    def cp_v(nt=nt, stv=stv, t0=t0, hTs=hTs):
                        w = stv["w"]
                        for tt in range(T // 128):
                            ps = PA[pctr["bi"] % 2][0]
                            vo = vout[pctr["bi"] % 3]
                            pctr["bi"] += 1
                            for k in range(KD):
                                last = (k == KD - 1)
                                sch.op("pe", lambda e: e.matmul(
                                    ps.ap[:, 0:CWV], lhsT=hTs[k].ap[:, tt * 128:(tt + 1) * 128],
                                    rhs=w.ap[:, k * CWV:(k + 1) * CWV], start=(k == 0), stop=last),
                                    reads=[w, hTs[k]], writes=[ps], signal=last)
                            if tt % 2 == 0:
                                sch.op("act", lambda e: e.copy(out=vo.ap, in_=ps.ap[:, 0:CWV]), reads=[ps], writes=[vo])
                            else:
                                sch.op("dve", lambda e: e.tensor_copy(out=vo.ap, in_=ps.ap[:, 0:CWV]),
                                       reads=[ps], writes=[vo])
                            sch.dma(vS[t0 + tt * 128:t0 + (tt + 1) * 128, nt * CWV:(nt + 1) * CWV], vo.ap, vo, False)
                        wtm_r.done(w)
                    jobsA.append((ld_v, cp_v, True))
                stg_ = {}

                def ld_vg(stg_=stg_):
                    stg_["w"] = [wtm_r.take() for _ in range(NVT)]
                    for nt in range(NVT):
                        sch.dma(stg_["w"][nt].ap, wB[l * NB + NVT + nt], stg_["w"][nt], True)

                def cp_vg(stg_=stg_, t0=t0, hTs=hTs):
                    wv = stg_["w"]

                    def part_a(tt):
                        vgb, vnb, stb_ = vg[tt % 2], vn[tt % 2], st[tt % 2]
                        for nt in range(NVT):
                            ps = PA[pctr["bi"] % 2][1]
                            pctr["bi"] += 1
                            w = wv[nt]
                            for k in range(KD):
                                last = (k == KD - 1)
                                sch.op("pe", lambda e: e.matmul(
                                    ps.ap[:, 0:CWV], lhsT=hTs[k].ap[:, tt * 128:(tt + 1) * 128],
                                    rhs=w.ap[:, k * CWV:(k + 1) * CWV], start=(k == 0), stop=last),
                                    reads=[w, hTs[k]], writes=[ps], signal=last)
                            sch.op("act", lambda e: e.activation(
                                out=vgb.ap[:, nt * CWV:(nt + 1) * CWV], in_=ps.ap[:, 0:CWV], func=AF.Gelu,
                                accum_out=stb_.ap[:, nt:nt + 1]),
                                reads=[ps], writes=[vgb, stb_])
                        sch.op("act", lambda e: e.activation(
                            out=vnb.ap, in_=vgb.ap, func=AF.Square, accum_out=stb_.ap[:, 4:5]),
                            reads=[vgb], writes=[vnb, stb_])
                        if NVT == 2:
                            sch.op("dve", lambda e: e.tensor_tensor(
                                out=stb_.ap[:, 5:6], in0=stb_.ap[:, 0:1], in1=stb_.ap[:, 1:2], op=ALU.add),
                                reads=[stb_], writes=[stb_])
                        else:
                            sch.op("dve", lambda e: e.tensor_copy(out=stb_.ap[:, 5:6], in_=stb_.ap[:, 0:1]),
                                   reads=[stb_], writes=[stb_])
                        sch.op("dve", lambda e: e.tensor_scalar(
                            out=stb_.ap[:, 8:9], in0=stb_.ap[:, 5:6], scalar1=1.0 / DG, scalar2=None, op0=ALU.mult),
                            reads=[stb_], writes=[stb_])
                        sch.op("dve", lambda e: e.tensor_tensor(
                            out=stb_.ap[:, 6:7], in0=stb_.ap[:, 8:9], in1=stb_.ap[:, 8:9], op=ALU.mult),
                            reads=[stb_], writes=[stb_])
                        sch.op("dve", lambda e: e.scalar_tensor_tensor(
                            out=stb_.ap[:, 9:10], in0=stb_.ap[:, 4:5], scalar=1.0 / DG, in1=stb_.ap[:, 6:7],
                            op0=ALU.mult, op1=ALU.subtract),
                            reads=[stb_], writes=[stb_])
                        sch.op("act", lambda e: e.activation(
                            out=stb_.ap[:, 7:8], in_=stb_.ap[:, 9:10], func=AF.Sqrt, bias=eps_b.ap[:, 0:1], scale=1.0),
                            reads=[stb_, eps_b], writes=[stb_])
                        sch.op("dve", lambda e: e.reciprocal(out=stb_.ap[:, 10:11], in_=stb_.ap[:, 7:8]),
                               reads=[stb_], writes=[stb_])
                        sch.op("dve", lambda e: e.tensor_scalar(
                            out=vnb.ap, in0=vgb.ap, scalar1=stb_.ap[:, 8:9], scalar2=stb_.ap[:, 10:11],
                            op0=ALU.subtract, op1=ALU.mult),
                            reads=[vgb, stb_], writes=[vnb])

                    def part_b(tt):
                        vnb = vn[tt % 2]
                        for g in range(G):
                            ps = PG[(g // 4) % 2]
                            col = (g % 4) * 128
                            sch.op("pe", lambda e: e.matmul(
                                ps.ap[:, col:col + 128], lhsT=vnb.ap[:, g * 128:(g + 1) * 128],
                                rhs=wsT_b.ap[:, (l * G + g) * 128:(l * G + g + 1) * 128], start=True, stop=True),
                                reads=[vnb, wsT_b], writes=[ps])
                            gt = gtmp[g % 2]
                            sch.op("dve", lambda e: e.scalar_tensor_tensor(
                                out=gt.ap, in0=ps.ap[:, col:col + 128], scalar=lng_b.ap[:, l * G + g:l * G + g + 1],
                                in1=bsb.ap[:, g * 128:(g + 1) * 128], op0=ALU.mult, op1=ALU.add),
                                reads=[ps, lng_b, bsb], writes=[gt])
                            sch.op("pool", lambda e: e.tensor_tensor(
                                out=GT[g].ap[:, tt * 128:(tt + 1) * 128], in0=gt.ap,
                                in1=uT[g].ap[:, tt * 128:(tt + 1) * 128], op=ALU.mult),
                                reads=[gt, uT[g]], writes=[GT[g]])

                    ntt = T // 128
                    part_a(0)
                    for tt in range(ntt):
                        if tt + 1 < ntt:
                            part_a(tt + 1)
                        part_b(tt)
                    for w in wv:
                        wtm_r.done(w)
                    rms_stats(GT, sqr, PST, rstd, sqt, DG)
                    for g in range(G):
                        mo = mgo[g % 2]
                        sch.op("dve", lambda e: e.scalar_tensor_tensor(
                            out=mo.ap, in0=GT[g].ap, scalar=gg_b.ap[:, l * G + g:l * G + g + 1], in1=rstd.ap,
                            op0=ALU.mult, op1=ALU.mult),
                            reads=[GT[g], gg_b, rstd], writes=[mo])
                        sch.dma(mgT[g * 128:(g + 1) * 128, t0:t0 + T], mo.ap, mo, False)
                jobsA.append((ld_vg, cp_vg, False))
            run_jobs(jobsA, 1)

            phase_begin()
            def allocB1():
                qh = [ar.alloc([S], BF16, f"qh{i}") for i in range(2)]
                kh = [ar.alloc([S], BF16, f"kh{i}") for i in range(2)]
                Vr = [[ar.alloc([S // 128, 128], BF16, f"V{r}_{i}") for i in range(2)] for r in PATTERNS]
                num = ar.alloc([S], F32, "num")
                den = ar.alloc([S], F32, "den")
                oh = [ar.alloc([S], BF16, f"oh{i}") for i in range(2)]
                NPE, NPM = 4, 10
                pet = [ar.alloc([384], BF16, f"pet{i}") for i in range(NPE)]
                pmt = [ar.alloc([384], BF16, f"pmt{i}") for i in range(NPM)]
                cstg = [ar.alloc([PC], F32, f"cstg{i}") for i in range(5)]
                cstb = [ar.alloc([PC], BF16, f"cstb{i}") for i in range(3)]
                return dict(locals())
            nsB = bufs("B1", allocB1)
            qh, kh, Vr, num, den, oh, NPE, NPM, pet, pmt, cstg, cstb = (
                nsB[k] for k in ("qh", "kh", "Vr", "num", "den", "oh", "NPE", "NPM", "pet", "pmt", "cstg", "cstb"))
            cast_upto = cast_mark_A[l + 1] if l + 1 < L else cast_mark_R[l]
            cast_start = cast_state["next"]
            cast_n = cast_upto - cast_start
            PS_S = banks[0:3]
            PS_O = banks[3:5]
            PS_D = banks[5:7]
            LA = 3

            def load_head(h):
                i = h % 2
                sch.dma(qh[i].ap, qT[h * 128:(h + 1) * 128, :], qh[i], True)
                sch.dma(kh[i].ap, kT[h * 128:(h + 1) * 128, :], kh[i], True)
                for pi_, r in enumerate(PATTERNS):
                    vb = Vr[pi_][i]
                    nch = S // (128 * r)
                    if r == 1:
                        src = vS.rearrange("(c p) f -> p c f", p=128)[:, :, h * 128:(h + 1) * 128]
                        sch.dma(vb.ap, src, vb, True)
                    else:
                        for res in range(r):
                            src = vS.rearrange("(c p r) f -> p r c f", p=128, r=r)[:, res, :, h * 128:(h + 1) * 128]
                            sch.dma(vb.ap[:, res * nch:(res + 1) * nch, :], src, vb, True)

            load_head(0)
            if os.environ.get("DBG_BAR"):
                sch.barrier()
            sct = 0
            for h in range(H):
                i = h % 2
                if h + 1 < H and not os.environ.get("DBG_NOPF"):
                    load_head(h + 1)
                qb, kb = qh[i], kh[i]
                qk_jobs = []
                pv_jobs = []
                order = []
                for pi_, r in enumerate(PATTERNS):
                    nch = S // (128 * r)
                    nb = min(4, nch)
                    for res in range(r):
                        for cc in range(nch):
                            qk_jobs.append((pi_, r, res, cc, nch))
                            order.append(("qk", len(qk_jobs) - 1))
                            if cc >= 1:
                                pv_jobs.append((pi_, r, res, cc - 1, nch, nb))
                                order.append(("pv", len(pv_jobs) - 1))
                        pv_jobs.append((pi_, r, res, nch - 1, nch, nb))
                        order.append(("pv", len(pv_jobs) - 1))
                pm_of = {}
                qk_done = 0

                def emit_qk(idx):
                    nonlocal sct
                    pi_, r, res, cc, nch = qk_jobs[idx]
                    blo = max(0, cc - 1)
                    bhi = min(nch, cc + 2)
                    n = (bhi - blo) * 128
                    boff = (blo - (cc - 1)) * 128
                    ps = PS_S[sct % 3]
                    pe_t = pet[sct % NPE]
                    pm_t = pmt[sct % NPM]
                    sct += 1
                    k0 = 128 * cc * r + res
                    q0 = 128 * blo * r + res
                    sch.op("pe", lambda e, ps=ps, n=n, k0=k0, q0=q0, r=r: e.matmul(
                        ps.ap[:, 0:n], lhsT=kb.ap[:, k0:k0 + 127 * r + 1:r], rhs=qb.ap[:, q0:q0 + (n - 1) * r + 1:r],
                        start=True, stop=True), reads=[kb, qb], writes=[ps])
                    sch.op("act", lambda e, ps=ps, n=n, pe_t=pe_t: e.activation(
                        out=pe_t.ap[:, 0:n], in_=ps.ap[:, 0:n], func=AF.Exp, scale=scale_qk),
                        reads=[ps], writes=[pe_t])
                    sch.op("pool" if sct % 3 == 0 else "dve", lambda e, n=n, pe_t=pe_t, pm_t=pm_t, boff=boff: e.tensor_tensor(
                        out=pm_t.ap[:, 0:n], in0=pe_t.ap[:, 0:n], in1=band_b.ap[:, boff:boff + n], op=ALU.mult),
                        reads=[pe_t, band_b], writes=[pm_t])
                    pm_of[(pi_, res, cc)] = (pm_t, blo, sct)

                oct_ = [0]

                def emit_pv(idx):
                    pi_, r, res, b, nch, nb = pv_jobs[idx]
                    vb = Vr[pi_][i]
                    bslot = b % nb
                    if bslot == 0:
                        oct_[0] += 1
                    po = PS_O[oct_[0] % 2]
                    pd = PS_D[oct_[0] % 2]
                    cs = [cc for cc in (b - 1, b, b + 1) if 0 <= cc < nch]
                    for ci_, cc in enumerate(cs):
                        pm_t, blo, sct_at = pm_of[(pi_, res, cc)]
                        assert sct - sct_at < NPM - 1, "pm ring too small"
                        col = (b - blo) * 128
                        first = (ci_ == 0)
                        last = (ci_ == len(cs) - 1)
                        vidx = res * nch + cc if r > 1 else cc
                        sch.op("pe", lambda e, po=po, pm_t=pm_t, col=col, vidx=vidx, first=first, last=last, bslot=bslot, vb=vb: e.matmul(
                            po.ap[:, bslot * 128:(bslot + 1) * 128], lhsT=vb.ap[:, vidx, :], rhs=pm_t.ap[:, col:col + 128],
                            start=first, stop=last), reads=[vb, pm_t], writes=[po], signal=False)
                        sch.op("pe", lambda e, pd=pd, pm_t=pm_t, col=col, first=first, last=last, bslot=bslot: e.matmul(
                            pd.ap[0:1, bslot * 128:(bslot + 1) * 128], lhsT=ones_b.ap[:, 0:1], rhs=pm_t.ap[:, col:col + 128],
                            start=first, stop=last), reads=[ones_b, pm_t], writes=[pd], signal=True)
                    if bslot == nb - 1:
                        b0 = b - (nb - 1)
                        w = nb * 128
                        if r == 1:
                            nv = num.ap[:, b0 * 128:b0 * 128 + w]
                            dv = den.ap[0:1, b0 * 128:b0 * 128 + w]
                        else:
                            nv = num.ap.rearrange("p (m r) -> p r m", r=r)[:, res, b0 * 128:b0 * 128 + w]
                            dv = den.ap.rearrange("p (m r) -> p r m", r=r)[0:1, res, b0 * 128:b0 * 128 + w]
                        if pi_ == 0:
                            sch.op("dve", lambda e, nv=nv, po=po, w=w: e.tensor_copy(out=nv, in_=po.ap[:, 0:w]),
                                   reads=[po], writes=[num])
                            sch.op("dve", lambda e, dv=dv, pd=pd, w=w: e.tensor_copy(out=dv, in_=pd.ap[0:1, 0:w]),
                                   reads=[pd], writes=[den])
                        else:
                            sch.op("dve", lambda e, nv=nv, po=po, w=w: e.tensor_tensor(
                                out=nv, in0=po.ap[:, 0:w], in1=nv, op=ALU.add), reads=[po, num], writes=[num])
                            sch.op("dve", lambda e, dv=dv, pd=pd, w=w: e.tensor_tensor(
                                out=dv, in0=pd.ap[0:1, 0:w], in1=dv, op=ALU.add), reads=[pd, den], writes=[den])

                qk_positions = [k for k, (kind, _) in enumerate(order) if kind == "qk"]
                for pos, (kind, idx) in enumerate(order):
                    if kind == "qk":
                        while qk_done <= min(idx + LA, len(qk_jobs) - 1):
                            emit_qk(qk_done)
                            qk_done += 1
                    else:
                        emit_pv(idx)
                    tgt = cast_start + (cast_n * (h * len(order) + pos + 1)) // (H * len(order))
                    emit_cast(min(tgt, cast_upto), cstg, cstb)
                ob = oh[i]
                sch.op("dve", lambda e: e.reciprocal(out=den.ap[0:1, :], in_=den.ap[0:1, :]), reads=[den], writes=[den])
                pbc = banks[7]
                for c8 in range(S // 512):
                    sl = slice(c8 * 512, (c8 + 1) * 512)
                    sch.op("pe", lambda e: e.matmul(pbc.ap, lhsT=ones_f.ap[0:1, :], rhs=den.ap[0:1, sl],
                                                    start=True, stop=True), reads=[ones_f, den], writes=[pbc])
                    sch.op("dve", lambda e: e.tensor_tensor(out=ob.ap[:, sl], in0=num.ap[:, sl], in1=pbc.ap, op=ALU.mult),
                           reads=[num, pbc], writes=[ob])
                sch.dma(aT[h * 128:(h + 1) * 128, :], ob.ap, ob, False)
                if h == H - 1:
                    emit_cast(cast_upto, cstg, cstb, flush=True)
                if h + 1 < H and os.environ.get("DBG_NOPF"):
                    load_head(h + 1)
                if debug_outputs and h == 0 and l == 0:
                    for nm, bb, dt in (("d_num", num, F32), ("d_rden", num, F32), ("d_qh", qb, BF16), ("d_kh", kb, BF16),
                                       ("d_V1", Vr[0][0], BF16), ("d_V4", Vr[min(1, len(Vr) - 1)][0], BF16), ("d_V16", Vr[-1][0], BF16)):
                        dd = nc.dram_tensor(nm, [128, S], dt, kind="ExternalOutput").ap()
                        src_ap = bb.ap if len(bb.ap.shape) == 2 else bb.ap.rearrange("p a b -> p (a b)")
                        sch.dma(dd, src_ap, bb, False)

            phase_begin()
            def allocB2():
                aall = ar.alloc([H, T], BF16, "aall")
                mall = ar.alloc([G, T], BF16, "mall")
                mixa = [ar.alloc([T], BF16, f"mx{h}") for h in range(H)]
                sqr = [ar.alloc([T], BF16, f"sq{i}") for i in range(2)]
                rstd = ar.alloc([T], F32, "rstd")
                sqt = ar.alloc([T], F32, "sqt")
                wfm = [ar.alloc([TA], BF16, f"wfm{i}") for i in range(3)]
                xold = [ar.alloc([T], F32, f"xo{i}") for i in range(3)]
                return dict(locals())
            nsC = bufs("B2", allocB2)
            aall, mall, mixa, sqr, rstd, sqt, wfm, xold = (
                nsC[k] for k in ("aall", "mall", "mixa", "sqr", "rstd", "sqt", "wfm", "xold"))
            PA = [banks[0:2], banks[2:4], banks[4:6]]
            PST = banks[6:8]
            xsrc = xT_in if l == 0 else xT
            wfm_r = Ring(wfm)
            xo_r = Ring(xold)
            pctr = {"pa": 0}

            def next_pp3():
                pp = PA[pctr["pa"] % 3]
                pctr["pa"] += 1
                return pp

            jobsB = []
            for s in range(NST):
                t0 = s * T

                def ld_am(t0=t0):
                    sch.dma(aall.ap, aT[:, t0:t0 + T].rearrange("(h p) t -> p h t", p=128), aall, True)
                    sch.dma(mall.ap, mgT[:, t0:t0 + T].rearrange("(g p) t -> p g t", p=128), mall, True)

                def cp_am():
                    rms_stats([(aall, aall.ap[:, h, :]) for h in range(H)], sqr, PST, rstd, sqt, DA)
                    for h in range(H):
                        sch.op("dve", lambda e: e.scalar_tensor_tensor(
                            out=mixa[h].ap, in0=aall.ap[:, h, :], scalar=ga_b.ap[:, l * H + h:l * H + h + 1],
                            in1=rstd.ap, op0=ALU.mult, op1=ALU.mult), reads=[aall, ga_b, rstd], writes=[mixa[h]])
                jobsB.append((ld_am, cp_am, False))
                mix = [(mixa[h], mixa[h].ap) for h in range(H)] + [(mall, mall.ap[:, g, :]) for g in range(G)]
                for j in range(KD):
                    stx = {}

                    def ev_o(pp, j=j, stx=stx, t0=t0):
                        xo = stx["xo"]
                        for hi in range(NH):
                            ps = pp[hi]
                            sl = slice(hi * 512, (hi + 1) * 512)
                            sch.op("dve", lambda e: e.tensor_tensor(
                                out=xo.ap[:, sl], in0=ps.ap, in1=xo.ap[:, sl], op=ALU.add), reads=[ps, xo], writes=[xo])
                        sch.dma(xT[j * 128:(j + 1) * 128, t0:t0 + T], xo.ap, xo, False)
                        xo_r.done(xo)
                    tj2 = fm_tile_job(wfm_r, wA[l * NA + 2 * H + G + j], next_pp3, mix, KD, ev_o)

                    def ld_o(j=j, stx=stx, tj2=tj2, t0=t0):
                        tj2[0]()
                        xo = xo_r.take()
                        stx["xo"] = xo
                        sch.dma(xo.ap, xsrc[j * 128:(j + 1) * 128, t0:t0 + T], xo, True)
                    jobsB.append((ld_o, tj2[1], True))
            run_jobs(jobsB, 1)

            phase_begin()
            def allocC():
                hT = [ar.alloc([T], BF16, f"h2T{k}") for k in range(KD)]
                ffT = [ar.alloc([T], BF16, f"ffT{f}") for f in range(KF)]
                xst = [ar.alloc([T], F32, f"xst{i}") for i in range(2)]
                xo2 = [ar.alloc([T], F32, f"xo{i}") for i in range(2)]
                sqr = [ar.alloc([T], BF16, f"sq{i}") for i in range(2)]
                rstd = ar.alloc([T], F32, "rstd")
                sqt = ar.alloc([T], F32, "sqt")
                wgs = [ar.alloc([TA], BF16, f"wg{i}") for i in range(2)]
                wus = [ar.alloc([TA], BF16, f"wu{i}") for i in range(2)]
                wds = [ar.alloc([TC], BF16, f"wd{i}") for i in range(2)]
                sg = [ar.alloc([T], BF16, f"sg{i}") for i in range(2)]
                return dict(locals())
            nsD = bufs("C", allocC)
            hT, ffT, xst, xo2, sqr, rstd, sqt, wgs, wus, wds, sg = (
                nsD[k] for k in ("hT", "ffT", "xst", "xo2", "sqr", "rstd", "sqt", "wgs", "wus", "wds", "sg"))
            PGt = [banks[0:2], banks[2:4]]
            PUp = [banks[4:6], banks[6:8]]
            PD = [banks[0:2], banks[2:4], banks[4:6]]
            PST = banks[6:8]
            xring = Ring(xst)
            xo_r = Ring(xo2)
            wg_r, wu_r, wd_r = Ring(wgs), Ring(wus), Ring(wds)
            pctr = {"pd": 0}

            def next_pd():
                pp = PD[pctr["pd"] % 3]
                pctr["pd"] += 1
                return pp

            def gu_job(f):
                st_ = {}

                def ld():
                    st_["wg"] = wg_r.take()
                    st_["wu"] = wu_r.take()
                    sch.dma(st_["wg"].ap, wA[l * NA + 2 * H + G + KD + f], st_["wg"], True)
                    sch.dma(st_["wu"].ap, wA[l * NA + 2 * H + G + KD + KF + f], st_["wu"], True)

                def cp():
                    wg_, wu_ = st_["wg"], st_["wu"]
                    pg, pu = PGt[f % 2], PUp[f % 2]
                    for w, pp in ((wg_, pg), (wu_, pu)):
                        for hi in range(NH):
                            ps = pp[hi]
                            for k in range(KD):
                                last = (k == KD - 1)
                                sch.op("pe", lambda e: e.matmul(
                                    ps.ap, lhsT=w.ap[:, k * 128:(k + 1) * 128], rhs=hT[k].ap[:, hi * 512:(hi + 1) * 512],
                                    start=(k == 0), stop=last), reads=[w, hT[k]], writes=[ps], signal=last)
                    wg_r.done(wg_)
                    wu_r.done(wu_)
                    sgb = sg[f % 2]
                    for hi in range(NH):
                        sl = slice(hi * 512, (hi + 1) * 512)
                        sch.op("act", lambda e: e.activation(
                            out=sgb.ap[:, sl], in_=pg[hi].ap, func=AF.Silu), reads=[pg[hi]], writes=[sgb])
                        sch.op("dve", lambda e: e.tensor_tensor(
                            out=ffT[f].ap[:, sl], in0=pu[hi].ap, in1=sgb.ap[:, sl], op=ALU.mult),
                            reads=[pu[hi], sgb], writes=[ffT[f]])
                return (ld, cp, True)

            def down_job(j, t0):
                stx = {}
                fb = [(ffT[k], ffT[k].ap) for k in range(KF)]

                def ev(pp):
                    xo = stx["xo"]
                    for hi in range(NH):
                        ps = pp[hi]
                        sl = slice(hi * 512, (hi + 1) * 512)
                        sch.op("dve", lambda e: e.tensor_tensor(
                            out=xo.ap[:, sl], in0=ps.ap, in1=xo.ap[:, sl], op=ALU.add), reads=[ps, xo], writes=[xo])
                    sch.dma(xT[j * 128:(j + 1) * 128, t0:t0 + T], xo.ap, xo, False)
                    xo_r.done(xo)
                tj = fm_tile_job(wd_r, wC[l * KD + j], next_pd, fb, KF, ev)

                def ld():
                    tj[0]()
                    xo = xo_r.take()
                    stx["xo"] = xo
                    sch.dma(xo.ap, xT[j * 128:(j + 1) * 128, t0:t0 + T], xo, True)
                return (ld, tj[1], True)

            jobsC = []
            nj = [norm_jobs(l, s, xT, g2_b, hT, xring, sqr, PST, rstd, sqt) for s in range(NST)]
            jobsC += nj[0]
            for s in range(NST):
                t0 = s * T
                for f in range(KF):
                    jobsC.append(gu_job(f))
                for j in range(KD):
                    jobsC.append(down_job(j, t0))
                    if s + 1 < NST:
                        jobsC.append(nj[s + 1][j])
                if s + 1 < NST:
                    jobsC += nj[s + 1][KD:]
            run_jobs(jobsC, 1)

        phase_begin()
        xall = ar.alloc([KD, T], F32, "xall")
        sqr = [ar.alloc([T], BF16, f"sq{i}") for i in range(2)]
        rstd = ar.alloc([T], F32, "rstd")
        sqt = ar.alloc([T], F32, "sqt")
        PST = banks[0:2]
        for s in range(NST):
            t0 = s * T
            sch.dma(xall.ap, xT[:, t0:t0 + T].rearrange("(k p) t -> p k t", p=128), xall, True)
            rms_stats([(xall, xall.ap[:, kc, :]) for kc in range(KD)], sqr, PST, rstd, sqt, D)
            for kc in range(KD):
                eng = "dve"
                sch.op(eng, lambda e, kc=kc: e.scalar_tensor_tensor(
                    out=xall.ap[:, kc, :], in0=xall.ap[:, kc, :], scalar=gf_b.ap[:, kc:kc + 1], in1=rstd.ap,
                    op0=ALU.mult, op1=ALU.mult), reads=[xall, gf_b, rstd], writes=[xall])
            sch.dma(outT[:, t0:t0 + T].rearrange("(k p) t -> p k t", p=128), xall.ap, xall, False)
        if debug_outputs:
            phase_begin()
            for nm, src, dt in (("d_xT", xT, F32), ("d_qT", qT, BF16), ("d_kT", kT, BF16), ("d_v", vS, BF16),
                                ("d_aT", aT, BF16), ("d_mgT", mgT, BF16)):
                rows, cols = src.shape
                dst = nc.dram_tensor(nm, [rows, cols], dt, kind="ExternalOutput").ap()
                ar.off = const_end
                dbuf = [ar.alloc([cols], dt, f"dbg{nm}{i}") for i in range(2)]
                for rr in range(rows // 128):
                    b = dbuf[rr % 2]
                    sch.dma(b.ap, src[rr * 128:(rr + 1) * 128, :], b, True)
                    sch.dma(dst[rr * 128:(rr + 1) * 128, :], b.ap, b, False)
                sch.barrier()
        sch.barrier()

        with nc.Block() as block:
            @block.sync
            def _(e):
                sch.replay("sp", e)

            @block.tensor
            def _(e):
                sch.replay("pe", e)

            @block.scalar
            def _(e):
                sch.replay("act", e)

            @block.vector
            def _(e):
                sch.replay("dve", e)

            @block.gpsimd
            def _(e):
                sch.replay("pool", e)
    return nc, {k: len(v) for k, v in sch.streams.items()}


def tile_w(W, cw):
    K, N = W.shape
    return np.ascontiguousarray(
        W.reshape(K // 128, 128, N // cw, cw).transpose(2, 1, 0, 3)).reshape(N // cw, 128, (K // 128) * cw)


def host_prep(cfg, p):
    c = derive(cfg)
    S, D, L = c["S"], c["D"], c["DEPTH"]
    DA, DG, H, G, KD, KF, CWV = c["DA"], c["DG"], c["H"], c["G"], c["KD"], c["KF"], c["CWV"]
    wA, wB, wC = [], [], []
    for l in range(L):
        w_in = p["w_in"][l]
        tiles = [tile_w(w_in[:, 0:2 * DA], 128),
                 tile_w(w_in[:, 3 * DA:3 * DA + DG], 128),
                 tile_w(p["w_out"][l], 128),
                 tile_w(p["w_gate"][l], 128),
                 tile_w(p["w_up"][l], 128)]
        wA.append(np.concatenate(tiles, 0))
        wB.append(np.concatenate([tile_w(w_in[:, 2 * DA:3 * DA], CWV),
                                  tile_w(w_in[:, 3 * DA + DG:3 * DA + 2 * DG], CWV)], 0))
        wC.append(tile_w(p["w_down"][l], 128))
    shared = {
        "wA": np.concatenate(wA, 0), "wB": np.concatenate(wB, 0), "wC": np.concatenate(wC, 0),
    }

    def pp(v, n):
        return np.ascontiguousarray(v.reshape(L, n, 128).transpose(2, 0, 1)).reshape(128, L * n)

    shared["g1"] = pp(p["norm1_g"], KD)
    shared["g2"] = pp(p["norm2_g"], KD)
    shared["ga"] = pp(p["mix_norm_attn_g"], H)
    shared["gg"] = pp(p["mix_norm_gmlp_g"], G)
    shared["lng"] = pp(p["gmlp_ln_g"], G)
    shared["gf"] = np.ascontiguousarray(p["final_g"].reshape(KD, 128).T)
    shared["wsT"] = np.ascontiguousarray(p["w_spatial"].transpose(3, 0, 1, 2)).reshape(128, L * G * 128)
    shared["bs"] = np.ascontiguousarray(np.broadcast_to(p["b_spatial"].reshape(1, L * G * 128), (128, L * G * 128)))
    pos = np.arange(S, dtype=np.float32)
    inv = (np.float32(10000.0) ** (-np.arange(0, 128, 2, dtype=np.float32) / np.float32(128))).astype(np.float32)
    ang = (pos[None, :] * inv[:, None]).astype(np.float32)
    cosv = np.cos(ang).astype(np.float32)
    sinv = np.sin(ang).astype(np.float32)
    shared["cosT"] = np.ascontiguousarray(np.concatenate([cosv, cosv], 0))
    shared["sinS"] = np.ascontiguousarray(np.concatenate([-sinv, sinv], 0))
    kk = np.arange(128)[:, None]
    jj = np.arange(384)[None, :]
    shared["band"] = (np.abs(kk + 128 - jj) <= 64).astype(np.float32)
    return shared


FULL_CFG = dict(S=4096, D=2048, DEPTH=4, T=1024)
_CACHE = {}


def run(cfg, inputs, n_cores):
    p = {k: np.asarray(v, dtype=np.float32) for k, v in inputs.items()}
    shared = host_prep(cfg, p)
    key = tuple(sorted(cfg.items()))
    if key not in _CACHE:
        _CACHE[key] = build(cfg)[0]
    nc = _CACHE[key]
    x = p["x"]
    in_maps = []
    for b in range(n_cores):
        m = dict(shared)
        m["xT"] = np.ascontiguousarray(x[b].T)
        in_maps.append(m)
    res = run_bass_kernel_spmd(nc, in_maps, core_ids=list(range(n_cores)))
    out = np.stack([np.ascontiguousarray(res.results[b]["outT"].T) for b in range(n_cores)], 0)
    return out.astype(np.float32)


def kernel(**inputs):
    return run(FULL_CFG, inputs, 8)
```

```python
import bisect
import math
from contextlib import ExitStack

import numpy as np
import ml_dtypes

import concourse.bass as bass
import concourse.mybir as mybir
from concourse.bass_utils import run_bass_kernel_spmd

F32 = mybir.dt.float32
BF16 = mybir.dt.bfloat16
ALU = mybir.AluOpType
AF = mybir.ActivationFunctionType

EPS = 1e-6
import os
PATTERNS = tuple(int(v) for v in os.environ.get("DBG_PAT", "1,4,16").split(","))
ENGS = ("pe", "act", "dve", "pool", "sp")
SEM_LIMIT = 30000


def derive(cfg):
    c = dict(cfg)
    D = c["D"]
    c["DA"] = D // 2
    c["DG"] = D - c["DA"]
    c["H"] = c["DA"] // 128
    c["G"] = c["DG"] // 128
    c["DFF"] = -(-(8 * D) // (3 * 256)) * 256
    c["KD"] = D // 128
    c["KF"] = c["DFF"] // 128
    c["CWV"] = min(512, c["DA"])
    c["NVT"] = c["DA"] // c["CWV"]
    c["NA"] = 2 * c["H"] + c["G"] + c["KD"] + 2 * c["KF"]
    c["NB"] = 2 * c["NVT"]
    return c


class Buf:
    __slots__ = ("ap", "w", "r", "rd", "dsem", "dval", "name")

    def __init__(self, ap, name=""):
        self.ap = ap
        self.w = None
        self.r = {}
        self.rd = []
        self.dsem = None
        self.dval = 0
        self.name = name


class _Rec:
    def __init__(self):
        self.call = None

    def __getattr__(self, name):
        def f(*a, **k):
            assert self.call is None
            self.call = (name, a, k)
            return self
        return f


class Sched:
    def __init__(self, sem_pool):
        self.pool = list(sem_pool)
        self.streams = {e: [] for e in ENGS}
        self.engsem = {}
        self.sig_seq = {e: [] for e in ENGS}
        self.sig_tok = {e: [] for e in ENGS}
        self.seq = {e: 0 for e in ENGS}
        self.seen = {e: {} for e in ENGS}
        self.dsems = []
        for e in ("pe", "act", "dve", "pool"):
            self.engsem[e] = [self.pool.pop(), 0]

    def _resolve(self, tok):
        if tok[0] == "d":
            return tok[1], tok[2]
        _, eng, seq = tok
        i = bisect.bisect_left(self.sig_seq[eng], seq)
        assert i < len(self.sig_seq[eng]), "dependency on unsignaled op"
        return self.sig_tok[eng][i]

    def _wait(self, eng, tok):
        sem, val = self._resolve(tok)
        key = id(sem)
        if self.seen[eng].get(key, 0) >= val:
            return
        self.seen[eng][key] = val
        self.streams[eng].append(("w", sem, val))

    def _deps(self, eng, reads, writes):
        for b in reads:
            if b.w is not None:
                self._dep(eng, b.w, "raw")
        for b in writes:
            if b.w is not None:
                self._dep(eng, b.w, "waw")
            for t in b.r.values():
                self._dep(eng, t, "war")
            for t in b.rd:
                self._dep(eng, t, "war")

    def _dep(self, eng, tok, kind):
        if tok[0] == "e" and tok[1] == eng:
            if eng == "pe" or kind == "war":
                return
        self._wait(eng, tok)

    def op(self, eng, fn, reads=(), writes=(), signal=True):
        self._deps(eng, reads, writes)
        seq = self.seq[eng]
        self.seq[eng] += 1
        tok = ("e", eng, seq)
        sem = None
        if signal:
            es = self.engsem[eng]
            if es[1] >= SEM_LIMIT:
                es[0] = self.pool.pop()
                es[1] = 0
            es[1] += 1
            sem = es[0]
            self.sig_seq[eng].append(seq)
            self.sig_tok[eng].append((sem, es[1]))
        rec = _Rec()
        fn(rec)
        assert rec.call is not None
        self.streams[eng].append(("o", rec.call, sem, 1))
        for b in reads:
            b.r[eng] = tok
        for b in writes:
            b.w = tok
            b.r = {}
            b.rd = []
        return tok

    def dma(self, out_ap, in_ap, sb, load, extra_reads=(), extra_writes=()):
        eng = "sp"
        if load:
            if sb.w is not None and sb.w[0] == "d" and sb.dsem is not None and sb.w[1] is sb.dsem \
                    and not sb.r and not sb.rd:
                pass
            else:
                self._deps(eng, extra_reads, [sb] + list(extra_writes))
        else:
            self._deps(eng, [sb] + list(extra_reads), extra_writes)
        if sb.dsem is None:
            sb.dsem = self.pool.pop()
            self.dsems.append(sb)
        sb.dval += 16
        assert sb.dval < 2 * SEM_LIMIT
        tok = ("d", sb.dsem, sb.dval)
        self.streams[eng].append(("d", out_ap, in_ap, sb.dsem))
        if load:
            sb.w = tok
            sb.r = {}
            sb.rd = []
        else:
            sb.rd.append(tok)
        return tok

    def fence_dma(self, engs=("sp",)):
        for e in engs:
            for b in self.dsems:
                if b.dval:
                    self._wait(e, ("d", b.dsem, b.dval))

    def barrier(self):
        for e in ENGS:
            for o in ("pe", "act", "dve", "pool"):
                if o != e and self.sig_tok[o]:
                    sem, val = self.sig_tok[o][-1]
                    assert self.sig_seq[o][-1] == self.seq[o] - 1, "last op must signal"
                    self._wait(e, ("d", sem, val))
        self.fence_dma(ENGS)

    def replay(self, eng, e):
        for it in self.streams[eng]:
            if it[0] == "w":
                e.wait_ge(it[1], it[2])
            elif it[0] == "o":
                name, a, k = it[1]
                ins = getattr(e, name)(*a, **k)
                if it[2] is not None:
                    ins.then_inc(it[2], 1)
            else:
                e.dma_start(out=it[1], in_=it[2]).then_inc(it[3], 16)


class Arena:
    def __init__(self, t, nwords):
        self.t = t
        self.n = nwords
        self.off = 0

    def reset(self):
        self.off = 0

    def alloc(self, free_shape, dtype, name=""):
        n = int(np.prod(free_shape))
        esz = 4 if dtype == F32 else 2
        nwords = (n * esz + 3) // 4
        nwords = (nwords + 7) // 8 * 8
        assert self.off + nwords <= self.n, f"arena overflow at {name}: {self.off + nwords} > {self.n}"
        ap = self.t[:, self.off:self.off + nwords]
        if dtype != F32:
            ap = ap.bitcast(dtype)
        ap = ap[:, 0:n]
        if len(free_shape) == 2:
            ap = ap.rearrange("p (a b) -> p a b", a=free_shape[0])
        elif len(free_shape) == 3:
            ap = ap.rearrange("p (a b c) -> p a b c", a=free_shape[0], b=free_shape[1])
        self.off += nwords
        return Buf(ap, name)


def build(cfg, debug_outputs=False):
    c = derive(cfg)
    S, D, L, T = c["S"], c["D"], c["DEPTH"], c["T"]
    DA, DG, H, G, DFF, KD, KF = c["DA"], c["DG"], c["H"], c["G"], c["DFF"], c["KD"], c["KF"]
    CWV, NVT, NA, NB = c["CWV"], c["NVT"], c["NA"], c["NB"]
    NST = S // T
    NH = T // 512
    assert T % 512 == 0 and S % T == 0
    TA = KD * 128
    TBc = KD * CWV
    TC = KF * 128
    scale_qk = 1.0 / math.sqrt(128.0)

    nc = bass.Bass("TRN2", target_bir_lowering=False)

    def din(name, shape, dt=F32):
        return nc.dram_tensor(name, list(shape), dt, kind="ExternalInput").ap()

    xT_in = din("xT", [D, S])
    wA_in = din("wA", [L * NA, 128, TA])
    wB_in = din("wB", [L * NB, 128, TBc])
    wC_in = din("wC", [L * KD, 128, TC])
    wsT_in = din("wsT", [128, L * G * 128])
    g1_in = din("g1", [128, L * KD])
    g2_in = din("g2", [128, L * KD])
    ga_in = din("ga", [128, L * H])
    gg_in = din("gg", [128, L * G])
    lng_in = din("lng", [128, L * G])
    bs_in = din("bs", [128, L * G * 128])
    gf_in = din("gf", [128, KD])
    cos_in = din("cosT", [128, S])
    sin_in = din("sinS", [128, S])
    band_in = din("band", [128, 384])
    outT = nc.dram_tensor("outT", [D, S], F32, kind="ExternalOutput").ap()

    dk = {}
    xT = nc.dram_tensor("xT_s", [D, S], F32, **dk).ap()
    wA = nc.dram_tensor("wA_b", [L * NA, 128, TA], BF16).ap()
    wB = nc.dram_tensor("wB_b", [L * NB, 128, TBc], BF16).ap()
    wC = nc.dram_tensor("wC_b", [L * KD, 128, TC], BF16).ap()
    qT = nc.dram_tensor("qT_s", [DA, S], BF16, **dk).ap()
    kT = nc.dram_tensor("kT_s", [DA, S], BF16, **dk).ap()
    vS = nc.dram_tensor("v_s", [S, DA], BF16, **dk).ap()
    aT = nc.dram_tensor("aT_s", [DA, S], BF16, **dk).ap()
    mgT = nc.dram_tensor("mgT_s", [DG, S], BF16, **dk).ap()

    es = ExitStack()
    with es:
        sems = [es.enter_context(nc.semaphore(f"s{i}")) for i in range(96)]
        ARENA_WORDS = (nc.SBUF_PARTITION_SIZE_BYTES - 16384 - 2048) // 4
        arena_t = es.enter_context(nc.sbuf_tensor("arena", [128, ARENA_WORDS], F32))
        banks = [Buf(es.enter_context(nc.psum_tensor(f"ps{i}", [128, 512], F32))[:, :], f"ps{i}")
                 for i in range(8)]
        sch = Sched(sems)
        ar = Arena(arena_t, ARENA_WORDS)

        ones_b = ar.alloc([128], BF16, "ones")
        band_b = ar.alloc([384], BF16, "band")
        band_f = ar.alloc([384], F32, "bandf")
        g1_b = ar.alloc([L * KD], F32, "g1")
        g2_b = ar.alloc([L * KD], F32, "g2")
        ga_b = ar.alloc([L * H], F32, "ga")
        gg_b = ar.alloc([L * G], F32, "gg")
        lng_b = ar.alloc([L * G], F32, "lng")
        gf_b = ar.alloc([KD], F32, "gf")
        wsT_b = ar.alloc([L * G * 128], BF16, "wsT")
        eps_b = ar.alloc([8], F32, "eps")
        const_end = ar.off

        def phase_begin():
            sch.barrier()
            ar.off = const_end

        ar.off = const_end
        tmpc = ar.alloc([L * G * 128], F32, "tmpc")
        for dst, src in ((g1_b, g1_in), (g2_b, g2_in), (ga_b, ga_in), (gg_b, gg_in),
                         (lng_b, lng_in), (gf_b, gf_in), (band_f, band_in), (tmpc, wsT_in)):
            sch.dma(dst.ap, src, dst, True)
        sch.op("dve", lambda e: e.memset(ones_b.ap, 1.0), writes=[ones_b])
        sch.op("dve", lambda e: e.memset(eps_b.ap, EPS), writes=[eps_b])
        sch.op("dve", lambda e: e.tensor_copy(out=band_b.ap, in_=band_f.ap), reads=[band_f], writes=[band_b])
        sch.op("dve", lambda e: e.tensor_copy(out=wsT_b.ap, in_=tmpc.ap), reads=[tmpc], writes=[wsT_b])

        PC = 2048
        cast_pieces = []
        cast_mark_A = []
        cast_mark_R = []

        def add_pieces(src, dst, t_lo, t_hi, tcols):
            for t in range(t_lo, t_hi):
                for c0 in range(0, tcols, PC):
                    cw = min(PC, tcols - c0)
                    cast_pieces.append((src[t, :, c0:c0 + cw], dst[t, :, c0:c0 + cw], cw))

        for l in range(L):
            add_pieces(wA_in, wA, l * NA, l * NA + 2 * H + G, TA)
            add_pieces(wB_in, wB, l * NB, (l + 1) * NB, TBc)
            cast_mark_A.append(len(cast_pieces))
            add_pieces(wA_in, wA, l * NA + 2 * H + G, (l + 1) * NA, TA)
            add_pieces(wC_in, wC, l * KD, (l + 1) * KD, TC)
            cast_mark_R.append(len(cast_pieces))
        cast_state = {"next": 0, "n": 0, "m": 0}
        cast_fl = []
        cast_engs = ("dve", "act")

        def emit_cast(upto, stg, stb, flush=False):
            def issue():
                src, dst, cw = cast_pieces[cast_state["next"]]
                cast_state["next"] += 1
                a = stg[cast_state["n"] % len(stg)]
                cast_state["n"] += 1
                sch.dma(a.ap[:, 0:cw], src, a, True)
                cast_fl.append((a, dst, cw))

            def finish():
                a, dst, cw = cast_fl.pop(0)
                b = stb[cast_state["m"] % len(stb)]
                ce = cast_engs[cast_state["m"] % 2]
                cast_state["m"] += 1
                if ce == "act":
                    sch.op("act", lambda e: e.copy(out=b.ap[:, 0:cw], in_=a.ap[:, 0:cw]), reads=[a], writes=[b])
                else:
                    sch.op(ce, lambda e: e.tensor_copy(out=b.ap[:, 0:cw], in_=a.ap[:, 0:cw]), reads=[a], writes=[b])
                sch.dma(dst, b.ap[:, 0:cw], b, False)

            while cast_state["next"] < upto:
                issue()
                if len(cast_fl) > 3:
                    finish()
            if flush:
                while cast_fl:
                    finish()

        phase_begin()
        stg0 = [ar.alloc([PC], F32, f"stg{i}") for i in range(5)]
        stb0 = [ar.alloc([PC], BF16, f"stb{i}") for i in range(4)]
        emit_cast(cast_mark_A[0], stg0, stb0, flush=True)

        def mm_group(ps_list, lhs_fn, rhs_fn, nk, reads):
            for hi, ps in enumerate(ps_list):
                for k in range(nk):
                    last = (k == nk - 1)
                    sch.op("pe",
                           lambda e, ps=ps, k=k, hi=hi, last=last: e.matmul(
                               ps.ap, lhsT=lhs_fn(k), rhs=rhs_fn(k, hi),
                               start=(k == 0), stop=last),
                           reads=reads(k), writes=[ps], signal=last)

        def rms_stats(chunk_bufs, sq_ring, stat_banks, rstd, sqt, dim):
            n = len(chunk_bufs)
            for k, cb in enumerate(chunk_bufs):
                sq = sq_ring[k % len(sq_ring)]
                cbuf, cap = cb if isinstance(cb, tuple) else (cb, cb.ap)
                sch.op("act", lambda e, cap=cap, sq=sq: e.activation(out=sq.ap, in_=cap, func=AF.Square),
                       reads=[cbuf], writes=[sq])
                for hi in range(NH):
                    ps = stat_banks[hi]
                    last = (k == n - 1)
                    sch.op("pe", lambda e, ps=ps, sq=sq, hi=hi, k=k, last=last: e.matmul(
                        ps.ap, lhsT=ones_b.ap, rhs=sq.ap[:, hi * 512:(hi + 1) * 512], start=(k == 0), stop=last),
                        reads=[sq, ones_b], writes=[ps], signal=(last or hi == NH - 1))
            for hi in range(NH):
                ps = stat_banks[hi]
                sch.op("act", lambda e, ps=ps, hi=hi: e.activation(
                    out=sqt.ap[:, hi * 512:(hi + 1) * 512], in_=ps.ap, func=AF.Sqrt, bias=eps_b.ap[:, 0:1], scale=1.0 / dim),
                    reads=[ps, eps_b], writes=[sqt])
            sch.op("dve", lambda e: e.reciprocal(out=rstd.ap, in_=sqt.ap), reads=[sqt], writes=[rstd])


        class Ring:
            def __init__(self, bl):
                self.b = bl
                self.i = 0
                self.pend = set()

            def take(self):
                b = self.b[self.i % len(self.b)]
                self.i += 1
                assert id(b) not in self.pend, "ring slot reused before its consumer was emitted"
                self.pend.add(id(b))
                return b

            def done(self, b):
                self.pend.discard(id(b))

        def run_jobs(jobs, d=1):
            n = len(jobs)
            loaded = [False] * n
            for i in range(n):
                if not loaded[i]:
                    if jobs[i][0] is not None:
                        jobs[i][0]()
                    loaded[i] = True
                for k in range(i + 1, min(n, i + 1 + d)):
                    if loaded[k]:
                        continue
                    if not jobs[k][2]:
                        break
                    if jobs[k][0] is not None:
                        jobs[k][0]()
                    loaded[k] = True
                jobs[i][1]()

        def norm_jobs(l, s, xsrc, g_b, hT, xring, sq_ring, stat_banks, rstd, sqt, extra_load=None):
            t0 = s * T
            jobs = []
            for kc in range(KD):
                st_ = {}

                def ld(kc=kc, st_=st_):
                    xb = xring.take()
                    st_["xb"] = xb
                    sch.dma(xb.ap, xsrc[kc * 128:(kc + 1) * 128, t0:t0 + T], xb, True)

                def cp(kc=kc, st_=st_):
                    xb = st_["xb"]
                    sq = sq_ring[kc % len(sq_ring)]
                    sch.op("act", lambda e: e.activation(out=sq.ap, in_=xb.ap, func=AF.Square),
                           reads=[xb], writes=[sq])
                    for hi in range(NH):
                        ps = stat_banks[hi]
                        last = (kc == KD - 1)
                        sch.op("pe", lambda e: e.matmul(
                            ps.ap, lhsT=ones_b.ap, rhs=sq.ap[:, hi * 512:(hi + 1) * 512], start=(kc == 0), stop=last),
                            reads=[sq, ones_b], writes=[ps], signal=(last or hi == NH - 1))
                    sch.op("dve", lambda e: e.tensor_scalar(
                        out=hT[kc].ap, in0=xb.ap, scalar1=g_b.ap[:, l * KD + kc:l * KD + kc + 1], scalar2=1.0,
                        op0=ALU.mult, op1=ALU.mult), reads=[xb, g_b], writes=[hT[kc]])
                    xring.done(xb)
                jobs.append((ld, cp, True))

            def fin():
                for hi in range(NH):
                    ps = stat_banks[hi]
                    sch.op("act", lambda e: e.activation(
                        out=sqt.ap[:, hi * 512:(hi + 1) * 512], in_=ps.ap, func=AF.Sqrt, bias=eps_b.ap[:, 0:1],
                        scale=1.0 / D), reads=[ps, eps_b], writes=[sqt])
                sch.op("dve", lambda e: e.reciprocal(out=rstd.ap, in_=sqt.ap), reads=[sqt], writes=[rstd])
                for kc in range(KD):
                    eng = "pool" if kc % 4 == 3 else "dve"
                    sch.op(eng, lambda e: e.tensor_tensor(out=hT[kc].ap, in0=hT[kc].ap, in1=rstd.ap, op=ALU.mult),
                           reads=[hT[kc], rstd], writes=[hT[kc]])
            jobs.append((extra_load, fin, False))
            return jobs

        def fm_tile_job(wring, w_src, pp_fn, rhs_bufs, nk, evac):
            st_ = {}

            def ld():
                w = wring.take()
                st_["w"] = w
                sch.dma(w.ap, w_src, w, True)

            def cp():
                w = st_["w"]
                pp = pp_fn()
                for hi in range(NH):
                    ps = pp[hi]
                    for k in range(nk):
                        last = (k == nk - 1)
                        rb, rap = rhs_bufs[k]
                        sch.op("pe", lambda e: e.matmul(
                            ps.ap, lhsT=w.ap[:, k * 128:(k + 1) * 128], rhs=rap[:, hi * 512:(hi + 1) * 512],
                            start=(k == 0), stop=last), reads=[w, rb], writes=[ps], signal=last)
                wring.done(w)
                evac(pp)
            return (ld, cp, True)

        _pcache = {}

        def bufs(name, fn):
            if name not in _pcache:
                ar.off = const_end
                _pcache[name] = fn()
            return _pcache[name]

        for l in range(L):
            phase_begin()
            def allocA():
                hT = [ar.alloc([T], BF16, f"hT{k}") for k in range(KD)]
                xst = [ar.alloc([T], F32, f"xst{i}") for i in range(3)]
                sqr = [ar.alloc([T], BF16, f"sq{i}") for i in range(2)]
                rstd = ar.alloc([T], F32, "rstd")
                sqt = ar.alloc([T], F32, "sqt")
                wfm = [ar.alloc([TA], BF16, f"wfm{i}") for i in range(3)]
                wtm = [ar.alloc([TBc], BF16, f"wtm{i}") for i in range(2)]
                wvg = wtm
                assert NVT <= 2
                cosb = ar.alloc([T], F32, "cos")
                sinb = ar.alloc([T], F32, "sin")
                r1 = [ar.alloc([T], F32, "r1_0")] * 2
                r2 = [ar.alloc([T], F32, "r2_0")] * 2
                qko = [ar.alloc([T], BF16, f"qko{i}") for i in range(2)]
                vout = ar.alloc([T // 128, DA], BF16, "vout")
                uT = [ar.alloc([T], BF16, f"uT{g}") for g in range(G)]
                vg = [ar.alloc([DG], F32, f"vg{i}") for i in range(2)]
                vn = [ar.alloc([DG], BF16, f"vn{i}") for i in range(2)]
                st = [ar.alloc([16], F32, f"st{i}") for i in range(2)]
                GT = [ar.alloc([T], BF16, f"GT{g}") for g in range(G)]
                gtmp = [ar.alloc([128], F32, f"gtmp{i}") for i in range(2)]
                mgo = [ar.alloc([T], BF16, f"mgo{i}") for i in range(2)]
                bsb = ar.alloc([G * 128], F32, "bsb")
                return dict(locals())
            nsA = bufs("A", allocA)
            hT, xst, sqr, rstd, sqt, wfm, wtm, wvg, cosb, sinb, r1, r2, qko, vout, uT, vg, vn, st, GT, gtmp, mgo, bsb = (
                nsA[k] for k in ("hT", "xst", "sqr", "rstd", "sqt", "wfm", "wtm", "wvg", "cosb", "sinb", "r1", "r2",
                                 "qko", "vout", "uT", "vg", "vn", "st", "GT", "gtmp", "mgo", "bsb"))
            sch.dma(bsb.ap, bs_in[:, l * G * 128:(l + 1) * G * 128], bsb, True)
            PA = [banks[0:2], banks[2:4]]
            PST = banks[4:6]
            PG = banks[6:8]
            xsrc = xT_in if l == 0 else xT
            xring = Ring(xst)
            wfm_r = Ring(wfm)
            wtm_r = Ring(wtm)
            pctr = {"pa": 0, "bi": 0}

            def next_pp():
                pp = PA[pctr["pa"] % 2]
                pctr["pa"] += 1
                return pp

            jobsA = []
            for s in range(NST):
                t0 = s * T

                def ld_cs(t0=t0):
                    sch.dma(cosb.ap, cos_in[:, t0:t0 + T], cosb, True)
                    sch.dma(sinb.ap, sin_in[:, t0:t0 + T], sinb, True)
                jobsA += norm_jobs(l, s, xsrc, g1_b, hT, xring, sqr, PST, rstd, sqt, extra_load=ld_cs)
                hb = [(hT[k], hT[k].ap) for k in range(KD)]
                for j in range(2 * H):
                    def evac_qk(pp, j=j, t0=t0):
                        ra, rb, qo = r1[j % 2], r2[j % 2], qko[j % 2]
                        for hi in range(NH):
                            ps = pp[hi]
                            sl = slice(hi * 512, (hi + 1) * 512)
                            sch.op("dve", lambda e: e.tensor_tensor(
                                out=ra.ap[:, sl], in0=ps.ap, in1=cosb.ap[:, sl], op=ALU.mult),
                                reads=[ps, cosb], writes=[ra])
                            sch.op("dve", lambda e: e.tensor_tensor(
                                out=rb.ap[0:64, sl], in0=ps.ap[64:128, :], in1=sinb.ap[0:64, sl], op=ALU.mult),
                                reads=[ps, sinb], writes=[rb])
                            sch.op("dve", lambda e: e.tensor_tensor(
                                out=rb.ap[64:128, sl], in0=ps.ap[0:64, :], in1=sinb.ap[64:128, sl], op=ALU.mult),
                                reads=[ps, sinb], writes=[rb])
                        sch.op("pool", lambda e: e.tensor_tensor(
                            out=qo.ap, in0=ra.ap, in1=rb.ap, op=ALU.add), reads=[ra, rb], writes=[qo])
                        dst = qT if j < H else kT
                        hh = j % H
                        sch.dma(dst[hh * 128:(hh + 1) * 128, t0:t0 + T], qo.ap, qo, False)
                    jobsA.append(fm_tile_job(wfm_r, wA[l * NA + j], next_pp, hb, KD, evac_qk))
                for g in range(G):
                    def evac_u(pp, g=g):
                        for hi in range(NH):
                            ps = pp[hi]
                            sch.op("act", lambda e: e.activation(
                                out=uT[g].ap[:, hi * 512:(hi + 1) * 512], in_=ps.ap, func=AF.Gelu),
                                reads=[ps], writes=[uT[g]])
                    jobsA.append(fm_tile_job(wfm_r, wA[l * NA + 2 * H + g], next_pp, hb, KD, evac_u))
                for nt in range(NVT):
                    stv = {}

                    def ld_v(nt=nt, stv=stv):
                        w = wtm_r.take()
                        stv["w"] = w
                        sch.dma(w.ap, wB[l * NB + nt], w, True)

                    def cp_v(nt=nt, stv=stv, t0=t0):
                        w = stv["w"]
                        for tt in range(T // 128):
                            ps = PA[pctr["bi"] % 2][0]
                            pctr["bi"] += 1
                            for k in range(KD):
                                last = (k == KD - 1)
                                sch.op("pe", lambda e: e.matmul(
                                    ps.ap[:, 0:CWV], lhsT=hT[k].ap[:, tt * 128:(tt + 1) * 128],
                                    rhs=w.ap[:, k * CWV:(k + 1) * CWV], start=(k == 0), stop=last),
                                    reads=[w, hT[k]], writes=[ps], signal=last)
                            if tt % 2 == 0:
                                sch.op("act", lambda e: e.copy(
                                    out=vout.ap[:, tt, nt * CWV:(nt + 1) * CWV], in_=ps.ap[:, 0:CWV]),
                                    reads=[ps], writes=[vout])
                            else:
                                sch.op("dve", lambda e: e.tensor_copy(
                                    out=vout.ap[:, tt, nt * CWV:(nt + 1) * CWV], in_=ps.ap[:, 0:CWV]),
                                    reads=[ps], writes=[vout])
                        wtm_r.done(w)
                        if nt == NVT - 1:
                            sch.dma(vS[t0:t0 + T, :].rearrange("(a p) f -> p a f", p=128), vout.ap, vout, False)
                    jobsA.append((ld_v, cp_v, True))
                stg_ = {}

                def ld_vg(stg_=stg_):
                    stg_["w"] = [wtm_r.take() for _ in range(NVT)]
                    for nt in range(NVT):
                        sch.dma(stg_["w"][nt].ap, wB[l * NB + NVT + nt], stg_["w"][nt], True)

                def cp_vg(stg_=stg_, t0=t0):
                    wv = stg_["w"]

                    def part_a(tt):
                        vgb, vnb, stb_ = vg[tt % 2], vn[tt % 2], st[tt % 2]
                        for nt in range(NVT):
                            ps = PA[pctr["bi"] % 2][1]
                            pctr["bi"] += 1
                            w = wv[nt]
                            for k in range(KD):
                                last = (k == KD - 1)
                                sch.op("pe", lambda e: e.matmul(
                                    ps.ap[:, 0:CWV], lhsT=hT[k].ap[:, tt * 128:(tt + 1) * 128],
                                    rhs=w.ap[:, k * CWV:(k + 1) * CWV], start=(k == 0), stop=last),
                                    reads=[w, hT[k]], writes=[ps], signal=last)
                            sch.op("act", lambda e: e.activation(
                                out=vgb.ap[:, nt * CWV:(nt + 1) * CWV], in_=ps.ap[:, 0:CWV], func=AF.Gelu,
                                accum_out=stb_.ap[:, nt:nt + 1]),
                                reads=[ps], writes=[vgb, stb_])
                        sch.op("act", lambda e: e.activation(
                            out=vnb.ap, in_=vgb.ap, func=AF.Square, accum_out=stb_.ap[:, 4:5]),
                            reads=[vgb], writes=[vnb, stb_])
                        if NVT == 2:
                            sch.op("dve", lambda e: e.tensor_tensor(
                                out=stb_.ap[:, 5:6], in0=stb_.ap[:, 0:1], in1=stb_.ap[:, 1:2], op=ALU.add),
                                reads=[stb_], writes=[stb_])
                        else:
                            sch.op("dve", lambda e: e.tensor_copy(out=stb_.ap[:, 5:6], in_=stb_.ap[:, 0:1]),
                                   reads=[stb_], writes=[stb_])
                        sch.op("dve", lambda e: e.tensor_scalar(
                            out=stb_.ap[:, 8:9], in0=stb_.ap[:, 5:6], scalar1=1.0 / DG, scalar2=None, op0=ALU.mult),
                            reads=[stb_], writes=[stb_])
                        sch.op("dve", lambda e: e.tensor_tensor(
                            out=stb_.ap[:, 6:7], in0=stb_.ap[:, 8:9], in1=stb_.ap[:, 8:9], op=ALU.mult),
                            reads=[stb_], writes=[stb_])
                        sch.op("dve", lambda e: e.scalar_tensor_tensor(
                            out=stb_.ap[:, 9:10], in0=stb_.ap[:, 4:5], scalar=1.0 / DG, in1=stb_.ap[:, 6:7],
                            op0=ALU.mult, op1=ALU.subtract),
                            reads=[stb_], writes=[stb_])
                        sch.op("act", lambda e: e.activation(
                            out=stb_.ap[:, 7:8], in_=stb_.ap[:, 9:10], func=AF.Sqrt, bias=eps_b.ap[:, 0:1], scale=1.0),
                            reads=[stb_, eps_b], writes=[stb_])
                        sch.op("dve", lambda e: e.reciprocal(out=stb_.ap[:, 10:11], in_=stb_.ap[:, 7:8]),
                               reads=[stb_], writes=[stb_])
                        sch.op("dve", lambda e: e.tensor_scalar(
                            out=vnb.ap, in0=vgb.ap, scalar1=stb_.ap[:, 8:9], scalar2=stb_.ap[:, 10:11],
                            op0=ALU.subtract, op1=ALU.mult),
                            reads=[vgb, stb_], writes=[vnb])

                    def part_b(tt):
                        vnb = vn[tt % 2]
                        for g in range(G):
                            ps = PG[(g // 4) % 2]
                            col = (g % 4) * 128
                            sch.op("pe", lambda e: e.matmul(
                                ps.ap[:, col:col + 128], lhsT=vnb.ap[:, g * 128:(g + 1) * 128],
                                rhs=wsT_b.ap[:, (l * G + g) * 128:(l * G + g + 1) * 128], start=True, stop=True),
                                reads=[vnb, wsT_b], writes=[ps])
                            gt = gtmp[g % 2]
                            sch.op("dve", lambda e: e.scalar_tensor_tensor(
                                out=gt.ap, in0=ps.ap[:, col:col + 128], scalar=lng_b.ap[:, l * G + g:l * G + g + 1],
                                in1=bsb.ap[:, g * 128:(g + 1) * 128], op0=ALU.mult, op1=ALU.add),
                                reads=[ps, lng_b, bsb], writes=[gt])
                            sch.op("pool", lambda e: e.tensor_tensor(
                                out=GT[g].ap[:, tt * 128:(tt + 1) * 128], in0=gt.ap,
                                in1=uT[g].ap[:, tt * 128:(tt + 1) * 128], op=ALU.mult),
                                reads=[gt, uT[g]], writes=[GT[g]])

                    ntt = T // 128
                    part_a(0)
                    for tt in range(ntt):
                        if tt + 1 < ntt:
                            part_a(tt + 1)
                        part_b(tt)
                    for w in wv:
                        wtm_r.done(w)
                    rms_stats(GT, sqr, PST, rstd, sqt, DG)
                    for g in range(G):
                        mo = mgo[g % 2]
                        sch.op("dve", lambda e: e.scalar_tensor_tensor(
                            out=mo.ap, in0=GT[g].ap, scalar=gg_b.ap[:, l * G + g:l * G + g + 1], in1=rstd.ap,
                            op0=ALU.mult, op1=ALU.mult),
                            reads=[GT[g], gg_b, rstd], writes=[mo])
                        sch.dma(mgT[g * 128:(g + 1) * 128, t0:t0 + T], mo.ap, mo, False)
                jobsA.append((ld_vg, cp_vg, False))
            run_jobs(jobsA, 2)

            phase_begin()
            def allocB1():
                qh = [ar.alloc([S], BF16, f"qh{i}") for i in range(2)]
                kh = [ar.alloc([S], BF16, f"kh{i}") for i in range(2)]
                Vr = [[ar.alloc([S // 128, 128], BF16, f"V{r}_{i}") for i in range(2)] for r in PATTERNS]
                num = ar.alloc([S], F32, "num")
                den = ar.alloc([S], F32, "den")
                oh = [ar.alloc([S], BF16, f"oh{i}") for i in range(2)]
                NPE, NPM = 4, 10
                pet = [ar.alloc([384], BF16, f"pet{i}") for i in range(NPE)]
                pmt = [ar.alloc([384], BF16, f"pmt{i}") for i in range(NPM)]
                cstg = [ar.alloc([PC], F32, f"cstg{i}") for i in range(5)]
                cstb = [ar.alloc([PC], BF16, f"cstb{i}") for i in range(3)]
                return dict(locals())
            nsB = bufs("B1", allocB1)
            qh, kh, Vr, num, den, oh, NPE, NPM, pet, pmt, cstg, cstb = (
                nsB[k] for k in ("qh", "kh", "Vr", "num", "den", "oh", "NPE", "NPM", "pet", "pmt", "cstg", "cstb"))
            cast_upto = cast_mark_A[l + 1] if l + 1 < L else cast_mark_R[l]
            cast_start = cast_state["next"]
            cast_n = cast_upto - cast_start
            PS_S = banks[0:3]
            PS_O = banks[3:5]
            PS_D = banks[5:7]
            LA = 3

            def load_head(h):
                i = h % 2
                sch.dma(qh[i].ap, qT[h * 128:(h + 1) * 128, :], qh[i], True)
                sch.dma(kh[i].ap, kT[h * 128:(h + 1) * 128, :], kh[i], True)
                for pi_, r in enumerate(PATTERNS):
                    vb = Vr[pi_][i]
                    nch = S // (128 * r)
                    if r == 1:
                        src = vS.rearrange("(c p) f -> p c f", p=128)[:, :, h * 128:(h + 1) * 128]
                        sch.dma(vb.ap, src, vb, True)
                    else:
                        for res in range(r):
                            src = vS.rearrange("(c p r) f -> p r c f", p=128, r=r)[:, res, :, h * 128:(h + 1) * 128]
                            sch.dma(vb.ap[:, res * nch:(res + 1) * nch, :], src, vb, True)

            load_head(0)
            if os.environ.get("DBG_BAR"):
                sch.barrier()
            sct = 0
            for h in range(H):
                i = h % 2
                if h + 1 < H and not os.environ.get("DBG_NOPF"):
                    load_head(h + 1)
                qb, kb = qh[i], kh[i]
                qk_jobs = []
                pv_jobs = []
                order = []
                for pi_, r in enumerate(PATTERNS):
                    nch = S // (128 * r)
                    nb = min(4, nch)
                    for res in range(r):
                        for cc in range(nch):
                            qk_jobs.append((pi_, r, res, cc, nch))
                            order.append(("qk", len(qk_jobs) - 1))
                            if cc >= 1:
                                pv_jobs.append((pi_, r, res, cc - 1, nch, nb))
                                order.append(("pv", len(pv_jobs) - 1))
                        pv_jobs.append((pi_, r, res, nch - 1, nch, nb))
                        order.append(("pv", len(pv_jobs) - 1))
                pm_of = {}
                qk_done = 0

                def emit_qk(idx):
                    nonlocal sct
                    pi_, r, res, cc, nch = qk_jobs[idx]
                    blo = max(0, cc - 1)
                    bhi = min(nch, cc + 2)
                    n = (bhi - blo) * 128
                    boff = (blo - (cc - 1)) * 128
                    ps = PS_S[sct % 3]
                    pe_t = pet[sct % NPE]
                    pm_t = pmt[sct % NPM]
                    sct += 1
                    k0 = 128 * cc * r + res
                    q0 = 128 * blo * r + res
                    sch.op("pe", lambda e, ps=ps, n=n, k0=k0, q0=q0, r=r: e.matmul(
                        ps.ap[:, 0:n], lhsT=kb.ap[:, k0:k0 + 127 * r + 1:r], rhs=qb.ap[:, q0:q0 + (n - 1) * r + 1:r],
                        start=True, stop=True), reads=[kb, qb], writes=[ps])
                    sch.op("act", lambda e, ps=ps, n=n, pe_t=pe_t: e.activation(
                        out=pe_t.ap[:, 0:n], in_=ps.ap[:, 0:n], func=AF.Exp, scale=scale_qk),
                        reads=[ps], writes=[pe_t])
                    sch.op("pool" if sct % 3 == 0 else "dve", lambda e, n=n, pe_t=pe_t, pm_t=pm_t, boff=boff: e.tensor_tensor(
                        out=pm_t.ap[:, 0:n], in0=pe_t.ap[:, 0:n], in1=band_b.ap[:, boff:boff + n], op=ALU.mult),
                        reads=[pe_t, band_b], writes=[pm_t])
                    pm_of[(pi_, res, cc)] = (pm_t, blo, sct)

                oct_ = [0]

                def emit_pv(idx):
                    pi_, r, res, b, nch, nb = pv_jobs[idx]
                    vb = Vr[pi_][i]
                    bslot = b % nb
                    if bslot == 0:
                        oct_[0] += 1
                    po = PS_O[oct_[0] % 2]
                    pd = PS_D[oct_[0] % 2]
                    cs = [cc for cc in (b - 1, b, b + 1) if 0 <= cc < nch]
                    for ci_, cc in enumerate(cs):
                        pm_t, blo, sct_at = pm_of[(pi_, res, cc)]
                        assert sct - sct_at < NPM - 1, "pm ring too small"
                        col = (b - blo) * 128
                        first = (ci_ == 0)
                        last = (ci_ == len(cs) - 1)
                        vidx = res * nch + cc if r > 1 else cc
                        sch.op("pe", lambda e, po=po, pm_t=pm_t, col=col, vidx=vidx, first=first, last=last, bslot=bslot, vb=vb: e.matmul(
                            po.ap[:, bslot * 128:(bslot + 1) * 128], lhsT=vb.ap[:, vidx, :], rhs=pm_t.ap[:, col:col + 128],
                            start=first, stop=last), reads=[vb, pm_t], writes=[po], signal=False)
                        sch.op("pe", lambda e, pd=pd, pm_t=pm_t, col=col, first=first, last=last, bslot=bslot: e.matmul(
                            pd.ap[:, bslot * 128:(bslot + 1) * 128], lhsT=ones_b.ap, rhs=pm_t.ap[:, col:col + 128],
                            start=first, stop=last), reads=[ones_b, pm_t], writes=[pd], signal=True)
                    if bslot == nb - 1:
                        b0 = b - (nb - 1)
                        w = nb * 128
                        if r == 1:
                            nv = num.ap[:, b0 * 128:b0 * 128 + w]
                            dv = den.ap[:, b0 * 128:b0 * 128 + w]
                        else:
                            nv = num.ap.rearrange("p (m r) -> p r m", r=r)[:, res, b0 * 128:b0 * 128 + w]
                            dv = den.ap.rearrange("p (m r) -> p r m", r=r)[:, res, b0 * 128:b0 * 128 + w]
                        if pi_ == 0:
                            sch.op("dve", lambda e, nv=nv, po=po, w=w: e.tensor_copy(out=nv, in_=po.ap[:, 0:w]),
                                   reads=[po], writes=[num])
                            sch.op("dve", lambda e, dv=dv, pd=pd, w=w: e.tensor_copy(out=dv, in_=pd.ap[:, 0:w]),
                                   reads=[pd], writes=[den])
                        else:
                            sch.op("dve", lambda e, nv=nv, po=po, w=w: e.tensor_tensor(
                                out=nv, in0=po.ap[:, 0:w], in1=nv, op=ALU.add), reads=[po, num], writes=[num])
                            sch.op("dve", lambda e, dv=dv, pd=pd, w=w: e.tensor_tensor(
                                out=dv, in0=pd.ap[:, 0:w], in1=dv, op=ALU.add), reads=[pd, den], writes=[den])

                qk_positions = [k for k, (kind, _) in enumerate(order) if kind == "qk"]
                for pos, (kind, idx) in enumerate(order):
                    if kind == "qk":
                        while qk_done <= min(idx + LA, len(qk_jobs) - 1):
                            emit_qk(qk_done)
                            qk_done += 1
                    else:
                        emit_pv(idx)
                    tgt = cast_start + (cast_n * (h * len(order) + pos + 1)) // (H * len(order))
                    emit_cast(min(tgt, cast_upto), cstg, cstb)
                ob = oh[i]
                sch.op("dve", lambda e: e.reciprocal(out=den.ap, in_=den.ap), reads=[den], writes=[den])
                sch.op("pool", lambda e, ob=ob: e.tensor_tensor(out=ob.ap, in0=num.ap, in1=den.ap, op=ALU.mult),
                       reads=[num, den], writes=[ob])
                sch.dma(aT[h * 128:(h + 1) * 128, :], ob.ap, ob, False)
                if h == H - 1:
                    emit_cast(cast_upto, cstg, cstb, flush=True)
                if h + 1 < H and os.environ.get("DBG_NOPF"):
                    load_head(h + 1)
                if debug_outputs and h == 0 and l == 0:
                    for nm, bb, dt in (("d_num", num, F32), ("d_rden", den, F32), ("d_qh", qb, BF16), ("d_kh", kb, BF16),
                                       ("d_V1", Vr[0][0], BF16), ("d_V4", Vr[min(1, len(Vr) - 1)][0], BF16), ("d_V16", Vr[-1][0], BF16)):
                        dd = nc.dram_tensor(nm, [128, S], dt, kind="ExternalOutput").ap()
                        src_ap = bb.ap if len(bb.ap.shape) == 2 else bb.ap.rearrange("p a b -> p (a b)")
                        sch.dma(dd, src_ap, bb, False)

            phase_begin()
            def allocB2():
                aall = ar.alloc([H, T], BF16, "aall")
                mall = ar.alloc([G, T], BF16, "mall")
                mixa = [ar.alloc([T], BF16, f"mx{h}") for h in range(H)]
                sqr = [ar.alloc([T], BF16, f"sq{i}") for i in range(2)]
                rstd = ar.alloc([T], F32, "rstd")
                sqt = ar.alloc([T], F32, "sqt")
                wfm = [ar.alloc([TA], BF16, f"wfm{i}") for i in range(3)]
                xold = [ar.alloc([T], F32, f"xo{i}") for i in range(3)]
                return dict(locals())
            nsC = bufs("B2", allocB2)
            aall, mall, mixa, sqr, rstd, sqt, wfm, xold = (
                nsC[k] for k in ("aall", "mall", "mixa", "sqr", "rstd", "sqt", "wfm", "xold"))
            PA = [banks[0:2], banks[2:4], banks[4:6]]
            PST = banks[6:8]
            xsrc = xT_in if l == 0 else xT
            wfm_r = Ring(wfm)
            xo_r = Ring(xold)
            pctr = {"pa": 0}

            def next_pp3():
                pp = PA[pctr["pa"] % 3]
                pctr["pa"] += 1
                return pp

            jobsB = []
            for s in range(NST):
                t0 = s * T

                def ld_am(t0=t0):
                    sch.dma(aall.ap, aT[:, t0:t0 + T].rearrange("(h p) t -> p h t", p=128), aall, True)
                    sch.dma(mall.ap, mgT[:, t0:t0 + T].rearrange("(g p) t -> p g t", p=128), mall, True)

                def cp_am():
                    rms_stats([(aall, aall.ap[:, h, :]) for h in range(H)], sqr, PST, rstd, sqt, DA)
                    for h in range(H):
                        sch.op("dve", lambda e: e.scalar_tensor_tensor(
                            out=mixa[h].ap, in0=aall.ap[:, h, :], scalar=ga_b.ap[:, l * H + h:l * H + h + 1],
                            in1=rstd.ap, op0=ALU.mult, op1=ALU.mult), reads=[aall, ga_b, rstd], writes=[mixa[h]])
                jobsB.append((ld_am, cp_am, False))
                mix = [(mixa[h], mixa[h].ap) for h in range(H)] + [(mall, mall.ap[:, g, :]) for g in range(G)]
                for j in range(KD):
                    stx = {}

                    def ev_o(pp, j=j, stx=stx, t0=t0):
                        xo = stx["xo"]
                        for hi in range(NH):
                            ps = pp[hi]
                            sl = slice(hi * 512, (hi + 1) * 512)
                            sch.op("dve", lambda e: e.tensor_tensor(
                                out=xo.ap[:, sl], in0=ps.ap, in1=xo.ap[:, sl], op=ALU.add), reads=[ps, xo], writes=[xo])
                        sch.dma(xT[j * 128:(j + 1) * 128, t0:t0 + T], xo.ap, xo, False)
                        xo_r.done(xo)
                    tj2 = fm_tile_job(wfm_r, wA[l * NA + 2 * H + G + j], next_pp3, mix, KD, ev_o)

                    def ld_o(j=j, stx=stx, tj2=tj2, t0=t0):
                        tj2[0]()
                        xo = xo_r.take()
                        stx["xo"] = xo
                        sch.dma(xo.ap, xsrc[j * 128:(j + 1) * 128, t0:t0 + T], xo, True)
                    jobsB.append((ld_o, tj2[1], True))
            run_jobs(jobsB, 1)

            phase_begin()
            def allocC():
                hT = [ar.alloc([T], BF16, f"h2T{k}") for k in range(KD)]
                ffT = [ar.alloc([T], BF16, f"ffT{f}") for f in range(KF)]
                xst = [ar.alloc([T], F32, f"xst{i}") for i in range(2)]
                xo2 = [ar.alloc([T], F32, f"xo{i}") for i in range(2)]
                sqr = [ar.alloc([T], BF16, f"sq{i}") for i in range(2)]
                rstd = ar.alloc([T], F32, "rstd")
                sqt = ar.alloc([T], F32, "sqt")
                wgs = [ar.alloc([TA], BF16, f"wg{i}") for i in range(2)]
                wus = [ar.alloc([TA], BF16, f"wu{i}") for i in range(2)]
                wds = [ar.alloc([TC], BF16, f"wd{i}") for i in range(2)]
                sg = [ar.alloc([T], BF16, f"sg{i}") for i in range(2)]
                return dict(locals())
            nsD = bufs("C", allocC)
            hT, ffT, xst, xo2, sqr, rstd, sqt, wgs, wus, wds, sg = (
                nsD[k] for k in ("hT", "ffT", "xst", "xo2", "sqr", "rstd", "sqt", "wgs", "wus", "wds", "sg"))
            PGt = [banks[0:2], banks[2:4]]
            PUp = [banks[4:6], banks[6:8]]
            PD = [banks[0:2], banks[2:4], banks[4:6]]
            PST = banks[6:8]
            xring = Ring(xst)
            xo_r = Ring(xo2)
            wg_r, wu_r, wd_r = Ring(wgs), Ring(wus), Ring(wds)
            pctr = {"pd": 0}

            def next_pd():
                pp = PD[pctr["pd"] % 3]
                pctr["pd"] += 1
                return pp

            def gu_job(f):
                st_ = {}

                def ld():
                    st_["wg"] = wg_r.take()
                    st_["wu"] = wu_r.take()
                    sch.dma(st_["wg"].ap, wA[l * NA + 2 * H + G + KD + f], st_["wg"], True)
                    sch.dma(st_["wu"].ap, wA[l * NA + 2 * H + G + KD + KF + f], st_["wu"], True)

                def cp():
                    wg_, wu_ = st_["wg"], st_["wu"]
                    pg, pu = PGt[f % 2], PUp[f % 2]
                    for w, pp in ((wg_, pg), (wu_, pu)):
                        for hi in range(NH):
                            ps = pp[hi]
                            for k in range(KD):
                                last = (k == KD - 1)
                                sch.op("pe", lambda e: e.matmul(
                                    ps.ap, lhsT=w.ap[:, k * 128:(k + 1) * 128], rhs=hT[k].ap[:, hi * 512:(hi + 1) * 512],
                                    start=(k == 0), stop=last), reads=[w, hT[k]], writes=[ps], signal=last)
                    wg_r.done(wg_)
                    wu_r.done(wu_)
                    sgb = sg[f % 2]
                    for hi in range(NH):
                        sl = slice(hi * 512, (hi + 1) * 512)
                        sch.op("act", lambda e: e.activation(
                            out=sgb.ap[:, sl], in_=pg[hi].ap, func=AF.Silu), reads=[pg[hi]], writes=[sgb])
                        sch.op("dve", lambda e: e.tensor_tensor(
                            out=ffT[f].ap[:, sl], in0=pu[hi].ap, in1=sgb.ap[:, sl], op=ALU.mult),
                            reads=[pu[hi], sgb], writes=[ffT[f]])
                return (ld, cp, True)

            def down_job(j, t0):
                stx = {}
                fb = [(ffT[k], ffT[k].ap) for k in range(KF)]

                def ev(pp):
                    xo = stx["xo"]
                    for hi in range(NH):
                        ps = pp[hi]
                        sl = slice(hi * 512, (hi + 1) * 512)
                        sch.op("dve", lambda e: e.tensor_tensor(
                            out=xo.ap[:, sl], in0=ps.ap, in1=xo.ap[:, sl], op=ALU.add), reads=[ps, xo], writes=[xo])
                    sch.dma(xT[j * 128:(j + 1) * 128, t0:t0 + T], xo.ap, xo, False)
                    xo_r.done(xo)
                tj = fm_tile_job(wd_r, wC[l * KD + j], next_pd, fb, KF, ev)

                def ld():
                    tj[0]()
                    xo = xo_r.take()
                    stx["xo"] = xo
                    sch.dma(xo.ap, xT[j * 128:(j + 1) * 128, t0:t0 + T], xo, True)
                return (ld, tj[1], True)

            jobsC = []
            nj = [norm_jobs(l, s, xT, g2_b, hT, xring, sqr, PST, rstd, sqt) for s in range(NST)]
            jobsC += nj[0]
            for s in range(NST):
                t0 = s * T
                for f in range(KF):
                    jobsC.append(gu_job(f))
                for j in range(KD):
                    jobsC.append(down_job(j, t0))
                    if s + 1 < NST:
                        jobsC.append(nj[s + 1][j])
                if s + 1 < NST:
                    jobsC += nj[s + 1][KD:]
            run_jobs(jobsC, 1)

        phase_begin()
        xall = ar.alloc([KD, T], F32, "xall")
        sqr = [ar.alloc([T], BF16, f"sq{i}") for i in range(2)]
        rstd = ar.alloc([T], F32, "rstd")
        sqt = ar.alloc([T], F32, "sqt")
        PST = banks[0:2]
        for s in range(NST):
            t0 = s * T
            sch.dma(xall.ap, xT[:, t0:t0 + T].rearrange("(k p) t -> p k t", p=128), xall, True)
            rms_stats([(xall, xall.ap[:, kc, :]) for kc in range(KD)], sqr, PST, rstd, sqt, D)
            for kc in range(KD):
                eng = "dve"
                sch.op(eng, lambda e, kc=kc: e.scalar_tensor_tensor(
                    out=xall.ap[:, kc, :], in0=xall.ap[:, kc, :], scalar=gf_b.ap[:, kc:kc + 1], in1=rstd.ap,
                    op0=ALU.mult, op1=ALU.mult), reads=[xall, gf_b, rstd], writes=[xall])
            sch.dma(outT[:, t0:t0 + T].rearrange("(k p) t -> p k t", p=128), xall.ap, xall, False)
        if debug_outputs:
            phase_begin()
            for nm, src, dt in (("d_xT", xT, F32), ("d_qT", qT, BF16), ("d_kT", kT, BF16), ("d_v", vS, BF16),
                                ("d_aT", aT, BF16), ("d_mgT", mgT, BF16)):
                rows, cols = src.shape
                dst = nc.dram_tensor(nm, [rows, cols], dt, kind="ExternalOutput").ap()
                ar.off = const_end
                dbuf = [ar.alloc([cols], dt, f"dbg{nm}{i}") for i in range(2)]
                for rr in range(rows // 128):
                    b = dbuf[rr % 2]
                    sch.dma(b.ap, src[rr * 128:(rr + 1) * 128, :], b, True)
                    sch.dma(dst[rr * 128:(rr + 1) * 128, :], b.ap, b, False)
                sch.barrier()
        sch.barrier()

        with nc.Block() as block:
            @block.sync
            def _(e):
                sch.replay("sp", e)

            @block.tensor
            def _(e):
                sch.replay("pe", e)

            @block.scalar
            def _(e):
                sch.replay("act", e)

            @block.vector
            def _(e):
                sch.replay("dve", e)

            @block.gpsimd
            def _(e):
                sch.replay("pool", e)
    return nc, {k: len(v) for k, v in sch.streams.items()}


def tile_w(W, cw):
    K, N = W.shape
    return np.ascontiguousarray(
        W.reshape(K // 128, 128, N // cw, cw).transpose(2, 1, 0, 3)).reshape(N // cw, 128, (K // 128) * cw)


def host_prep(cfg, p):
    c = derive(cfg)
    S, D, L = c["S"], c["D"], c["DEPTH"]
    DA, DG, H, G, KD, KF, CWV = c["DA"], c["DG"], c["H"], c["G"], c["KD"], c["KF"], c["CWV"]
    wA, wB, wC = [], [], []
    for l in range(L):
        w_in = p["w_in"][l]
        tiles = [tile_w(w_in[:, 0:2 * DA], 128),
                 tile_w(w_in[:, 3 * DA:3 * DA + DG], 128),
                 tile_w(p["w_out"][l], 128),
                 tile_w(p["w_gate"][l], 128),
                 tile_w(p["w_up"][l], 128)]
        wA.append(np.concatenate(tiles, 0))
        wB.append(np.concatenate([tile_w(w_in[:, 2 * DA:3 * DA], CWV),
                                  tile_w(w_in[:, 3 * DA + DG:3 * DA + 2 * DG], CWV)], 0))
        wC.append(tile_w(p["w_down"][l], 128))
    shared = {
        "wA": np.concatenate(wA, 0), "wB": np.concatenate(wB, 0), "wC": np.concatenate(wC, 0),
    }

    def pp(v, n):
        return np.ascontiguousarray(v.reshape(L, n, 128).transpose(2, 0, 1)).reshape(128, L * n)

    shared["g1"] = pp(p["norm1_g"], KD)
    shared["g2"] = pp(p["norm2_g"], KD)
    shared["ga"] = pp(p["mix_norm_attn_g"], H)
    shared["gg"] = pp(p["mix_norm_gmlp_g"], G)
    shared["lng"] = pp(p["gmlp_ln_g"], G)
    shared["gf"] = np.ascontiguousarray(p["final_g"].reshape(KD, 128).T)
    shared["wsT"] = np.ascontiguousarray(p["w_spatial"].transpose(3, 0, 1, 2)).reshape(128, L * G * 128)
    shared["bs"] = np.ascontiguousarray(np.broadcast_to(p["b_spatial"].reshape(1, L * G * 128), (128, L * G * 128)))
    pos = np.arange(S, dtype=np.float32)
    inv = (np.float32(10000.0) ** (-np.arange(0, 128, 2, dtype=np.float32) / np.float32(128))).astype(np.float32)
    ang = (pos[None, :] * inv[:, None]).astype(np.float32)
    cosv = np.cos(ang).astype(np.float32)
    sinv = np.sin(ang).astype(np.float32)
    shared["cosT"] = np.ascontiguousarray(np.concatenate([cosv, cosv], 0))
    shared["sinS"] = np.ascontiguousarray(np.concatenate([-sinv, sinv], 0))
    kk = np.arange(128)[:, None]
    jj = np.arange(384)[None, :]
    shared["band"] = (np.abs(kk + 128 - jj) <= 64).astype(np.float32)
    return shared


FULL_CFG = dict(S=4096, D=2048, DEPTH=4, T=1024)
_CACHE = {}


def run(cfg, inputs, n_cores):
    p = {k: np.asarray(v, dtype=np.float32) for k, v in inputs.items()}
    shared = host_prep(cfg, p)
    key = tuple(sorted(cfg.items()))
    if key not in _CACHE:
        _CACHE[key] = build(cfg)[0]
    nc = _CACHE[key]
    x = p["x"]
    in_maps = []
    for b in range(n_cores):
        m = dict(shared)
        m["xT"] = np.ascontiguousarray(x[b].T)
        in_maps.append(m)
    res = run_bass_kernel_spmd(nc, in_maps, core_ids=list(range(n_cores)))
    out = np.stack([np.ascontiguousarray(res.results[b]["outT"].T) for b in range(n_cores)], 0)
    return out.astype(np.float32)


def kernel(**inputs):
    return run(FULL_CFG, inputs, 8)
```

```python
import bisect
import math
from contextlib import ExitStack

import numpy as np
import ml_dtypes

import concourse.bass as bass
import concourse.mybir as mybir
from concourse.bass_utils import run_bass_kernel_spmd

F32 = mybir.dt.float32
BF16 = mybir.dt.bfloat16
ALU = mybir.AluOpType
AF = mybir.ActivationFunctionType

EPS = 1e-6
import os
PATTERNS = tuple(int(v) for v in os.environ.get("DBG_PAT", "1,4,16").split(","))
ENGS = ("pe", "act", "dve", "pool", "sp")
SEM_LIMIT = 30000


def derive(cfg):
    c = dict(cfg)
    D = c["D"]
    c["DA"] = D // 2
    c["DG"] = D - c["DA"]
    c["H"] = c["DA"] // 128
    c["G"] = c["DG"] // 128
    c["DFF"] = -(-(8 * D) // (3 * 256)) * 256
    c["KD"] = D // 128
    c["KF"] = c["DFF"] // 128
    c["CWV"] = min(512, c["DA"])
    c["NVT"] = c["DA"] // c["CWV"]
    c["NA"] = 2 * c["H"] + c["G"] + c["KD"] + 2 * c["KF"]
    c["NB"] = 2 * c["NVT"]
    return c


class Buf:
    __slots__ = ("ap", "w", "r", "rd", "dsem", "dval", "name")

    def __init__(self, ap, name=""):
        self.ap = ap
        self.w = None
        self.r = {}
        self.rd = []
        self.dsem = None
        self.dval = 0
        self.name = name


class _Rec:
    def __init__(self):
        self.call = None

    def __getattr__(self, name):
        def f(*a, **k):
            assert self.call is None
            self.call = (name, a, k)
            return self
        return f


class Sched:
    def __init__(self, sem_pool):
        self.pool = list(sem_pool)
        self.streams = {e: [] for e in ENGS}
        self.engsem = {}
        self.sig_seq = {e: [] for e in ENGS}
        self.sig_tok = {e: [] for e in ENGS}
        self.seq = {e: 0 for e in ENGS}
        self.seen = {e: {} for e in ENGS}
        self.dsems = []
        for e in ("pe", "act", "dve", "pool"):
            self.engsem[e] = [self.pool.pop(), 0]

    def _resolve(self, tok):
        if tok[0] == "d":
            return tok[1], tok[2]
        _, eng, seq = tok
        i = bisect.bisect_left(self.sig_seq[eng], seq)
        assert i < len(self.sig_seq[eng]), "dependency on unsignaled op"
        return self.sig_tok[eng][i]

    def _wait(self, eng, tok):
        sem, val = self._resolve(tok)
        key = id(sem)
        if self.seen[eng].get(key, 0) >= val:
            return
        self.seen[eng][key] = val
        self.streams[eng].append(("w", sem, val))

    def _deps(self, eng, reads, writes):
        for b in reads:
            if b.w is not None:
                self._dep(eng, b.w, "raw")
        for b in writes:
            if b.w is not None:
                self._dep(eng, b.w, "waw")
            for t in b.r.values():
                self._dep(eng, t, "war")
            for t in b.rd:
                self._dep(eng, t, "war")

    def _dep(self, eng, tok, kind):
        if tok[0] == "e" and tok[1] == eng:
            if eng == "pe" or kind == "war":
                return
        self._wait(eng, tok)

    def op(self, eng, fn, reads=(), writes=(), signal=True):
        self._deps(eng, reads, writes)
        seq = self.seq[eng]
        self.seq[eng] += 1
        tok = ("e", eng, seq)
        sem = None
        if signal:
            es = self.engsem[eng]
            if es[1] >= SEM_LIMIT:
                es[0] = self.pool.pop()
                es[1] = 0
            es[1] += 1
            sem = es[0]
            self.sig_seq[eng].append(seq)
            self.sig_tok[eng].append((sem, es[1]))
        rec = _Rec()
        fn(rec)
        assert rec.call is not None
        self.streams[eng].append(("o", rec.call, sem, 1))
        for b in reads:
            b.r[eng] = tok
        for b in writes:
            b.w = tok
            b.r = {}
            b.rd = []
        return tok

    def dma(self, out_ap, in_ap, sb, load, extra_reads=(), extra_writes=()):
        eng = "sp"
        if load:
            if sb.w is not None and sb.w[0] == "d" and sb.dsem is not None and sb.w[1] is sb.dsem \
                    and not sb.r and not sb.rd:
                pass
            else:
                self._deps(eng, extra_reads, [sb] + list(extra_writes))
        else:
            self._deps(eng, [sb] + list(extra_reads), extra_writes)
        if sb.dsem is None:
            sb.dsem = self.pool.pop()
            self.dsems.append(sb)
        sb.dval += 16
        assert sb.dval < 2 * SEM_LIMIT
        tok = ("d", sb.dsem, sb.dval)
        self.streams[eng].append(("d", out_ap, in_ap, sb.dsem))
        if load:
            sb.w = tok
            sb.r = {}
            sb.rd = []
        else:
            sb.rd.append(tok)
        return tok

    def fence_dma(self, engs=("sp",)):
        for e in engs:
            for b in self.dsems:
                if b.dval:
                    self._wait(e, ("d", b.dsem, b.dval))

    def barrier(self):
        for e in ENGS:
            for o in ("pe", "act", "dve", "pool"):
                if o != e and self.sig_tok[o]:
                    sem, val = self.sig_tok[o][-1]
                    assert self.sig_seq[o][-1] == self.seq[o] - 1, "last op must signal"
                    self._wait(e, ("d", sem, val))
        self.fence_dma(ENGS)

    def replay(self, eng, e):
        for it in self.streams[eng]:
            if it[0] == "w":
                e.wait_ge(it[1], it[2])
            elif it[0] == "o":
                name, a, k = it[1]
                ins = getattr(e, name)(*a, **k)
                if it[2] is not None:
                    ins.then_inc(it[2], 1)
            else:
                e.dma_start(out=it[1], in_=it[2]).then_inc(it[3], 16)


class Arena:
    def __init__(self, t, nwords):
        self.t = t
        self.n = nwords
        self.off = 0

    def reset(self):
        self.off = 0

    def alloc(self, free_shape, dtype, name=""):
        n = int(np.prod(free_shape))
        esz = 4 if dtype == F32 else 2
        nwords = (n * esz + 3) // 4
        nwords = (nwords + 7) // 8 * 8
        assert self.off + nwords <= self.n, f"arena overflow at {name}: {self.off + nwords} > {self.n}"
        ap = self.t[:, self.off:self.off + nwords]
        if dtype != F32:
            ap = ap.bitcast(dtype)
        ap = ap[:, 0:n]
        if len(free_shape) == 2:
            ap = ap.rearrange("p (a b) -> p a b", a=free_shape[0])
        elif len(free_shape) == 3:
            ap = ap.rearrange("p (a b c) -> p a b c", a=free_shape[0], b=free_shape[1])
        self.off += nwords
        return Buf(ap, name)


def build(cfg, debug_outputs=False):
    c = derive(cfg)
    S, D, L, T = c["S"], c["D"], c["DEPTH"], c["T"]
    DA, DG, H, G, DFF, KD, KF = c["DA"], c["DG"], c["H"], c["G"], c["DFF"], c["KD"], c["KF"]
    CWV, NVT, NA, NB = c["CWV"], c["NVT"], c["NA"], c["NB"]
    NST = S // T
    NH = T // 512
    assert T % 512 == 0 and S % T == 0
    TA = KD * 128
    TBc = KD * CWV
    TC = KF * 128
    scale_qk = 1.0 / math.sqrt(128.0)

    nc = bass.Bass("TRN2", target_bir_lowering=False)

    def din(name, shape, dt=F32):
        return nc.dram_tensor(name, list(shape), dt, kind="ExternalInput").ap()

    xT_in = din("xT", [D, S])
    wA_in = din("wA", [L * NA, 128, TA])
    wB_in = din("wB", [L * NB, 128, TBc])
    wC_in = din("wC", [L * KD, 128, TC])
    wsT_in = din("wsT", [128, L * G * 128])
    g1_in = din("g1", [128, L * KD])
    g2_in = din("g2", [128, L * KD])
    ga_in = din("ga", [128, L * H])
    gg_in = din("gg", [128, L * G])
    lng_in = din("lng", [128, L * G])
    bs_in = din("bs", [128, L * G * 128])
    gf_in = din("gf", [128, KD])
    cos_in = din("cosT", [128, S])
    sin_in = din("sinS", [128, S])
    band_in = din("band", [128, 384])
    outT = nc.dram_tensor("outT", [D, S], F32, kind="ExternalOutput").ap()

    dk = {}
    xT = nc.dram_tensor("xT_s", [D, S], F32, **dk).ap()
    wA = nc.dram_tensor("wA_b", [L * NA, 128, TA], BF16).ap()
    wB = nc.dram_tensor("wB_b", [L * NB, 128, TBc], BF16).ap()
    wC = nc.dram_tensor("wC_b", [L * KD, 128, TC], BF16).ap()
    qT = nc.dram_tensor("qT_s", [DA, S], BF16, **dk).ap()
    kT = nc.dram_tensor("kT_s", [DA, S], BF16, **dk).ap()
    vS = nc.dram_tensor("v_s", [S, DA], BF16, **dk).ap()
    aT = nc.dram_tensor("aT_s", [DA, S], BF16, **dk).ap()
    mgT = nc.dram_tensor("mgT_s", [DG, S], BF16, **dk).ap()

    es = ExitStack()
    with es:
        sems = [es.enter_context(nc.semaphore(f"s{i}")) for i in range(96)]
        ARENA_WORDS = (nc.SBUF_PARTITION_SIZE_BYTES - 16384 - 512) // 4
        arena_t = es.enter_context(nc.sbuf_tensor("arena", [128, ARENA_WORDS], F32))
        banks = [Buf(es.enter_context(nc.psum_tensor(f"ps{i}", [128, 512], F32))[:, :], f"ps{i}")
                 for i in range(8)]
        sch = Sched(sems)
        ar = Arena(arena_t, ARENA_WORDS)

        ones_b = ar.alloc([128], BF16, "ones")
        band_b = ar.alloc([384], BF16, "band")
        band_f = ar.alloc([384], F32, "bandf")
        g1_b = ar.alloc([L * KD], F32, "g1")
        g2_b = ar.alloc([L * KD], F32, "g2")
        ga_b = ar.alloc([L * H], F32, "ga")
        gg_b = ar.alloc([L * G], F32, "gg")
        lng_b = ar.alloc([L * G], F32, "lng")
        gf_b = ar.alloc([KD], F32, "gf")
        wsT_b = ar.alloc([L * G * 128], BF16, "wsT")
        eps_b = ar.alloc([8], F32, "eps")
        const_end = ar.off

        def phase_begin():
            sch.barrier()
            ar.off = const_end

        ar.off = const_end
        tmpc = ar.alloc([L * G * 128], F32, "tmpc")
        for dst, src in ((g1_b, g1_in), (g2_b, g2_in), (ga_b, ga_in), (gg_b, gg_in),
                         (lng_b, lng_in), (gf_b, gf_in), (band_f, band_in), (tmpc, wsT_in)):
            sch.dma(dst.ap, src, dst, True)
        sch.op("dve", lambda e: e.memset(ones_b.ap, 1.0), writes=[ones_b])
        sch.op("dve", lambda e: e.memset(eps_b.ap, EPS), writes=[eps_b])
        sch.op("dve", lambda e: e.tensor_copy(out=band_b.ap, in_=band_f.ap), reads=[band_f], writes=[band_b])
        sch.op("dve", lambda e: e.tensor_copy(out=wsT_b.ap, in_=tmpc.ap), reads=[tmpc], writes=[wsT_b])

        PC = 512
        cast_pieces = []
        cast_mark_A = []
        cast_mark_R = []

        def add_pieces(src, dst, t_lo, t_hi, tcols):
            for t in range(t_lo, t_hi):
                for c0 in range(0, tcols, PC):
                    cw = min(PC, tcols - c0)
                    cast_pieces.append((src[t, :, c0:c0 + cw], dst[t, :, c0:c0 + cw], cw))

        for l in range(L):
            add_pieces(wA_in, wA, l * NA, l * NA + 2 * H + G, TA)
            add_pieces(wB_in, wB, l * NB, (l + 1) * NB, TBc)
            cast_mark_A.append(len(cast_pieces))
            add_pieces(wA_in, wA, l * NA + 2 * H + G, (l + 1) * NA, TA)
            add_pieces(wC_in, wC, l * KD, (l + 1) * KD, TC)
            cast_mark_R.append(len(cast_pieces))
        cast_state = {"next": 0, "n": 0, "m": 0}
        cast_fl = []
        cast_engs = ("dve", "act")

        def emit_cast(upto, stg, stb, flush=False):
            def issue():
                src, dst, cw = cast_pieces[cast_state["next"]]
                cast_state["next"] += 1
                a = stg[cast_state["n"] % len(stg)]
                cast_state["n"] += 1
                sch.dma(a.ap[:, 0:cw], src, a, True)
                cast_fl.append((a, dst, cw))

            def finish():
                a, dst, cw = cast_fl.pop(0)
                b = stb[cast_state["m"] % len(stb)]
                ce = cast_engs[cast_state["m"] % 2]
                cast_state["m"] += 1
                if ce == "act":
                    sch.op("act", lambda e: e.copy(out=b.ap[:, 0:cw], in_=a.ap[:, 0:cw]), reads=[a], writes=[b])
                else:
                    sch.op(ce, lambda e: e.tensor_copy(out=b.ap[:, 0:cw], in_=a.ap[:, 0:cw]), reads=[a], writes=[b])
                sch.dma(dst, b.ap[:, 0:cw], b, False)

            while cast_state["next"] < upto:
                issue()
                if len(cast_fl) > len(stg) - 2:
                    finish()
            if flush:
                while cast_fl:
                    finish()

        phase_begin()
        stg0 = [ar.alloc([PC], F32, f"stg{i}") for i in range(5)]
        stb0 = [ar.alloc([PC], BF16, f"stb{i}") for i in range(4)]
        emit_cast(cast_mark_A[0], stg0, stb0, flush=True)

        def mm_group(ps_list, lhs_fn, rhs_fn, nk, reads):
            for hi, ps in enumerate(ps_list):
                for k in range(nk):
                    last = (k == nk - 1)
                    sch.op("pe",
                           lambda e, ps=ps, k=k, hi=hi, last=last: e.matmul(
                               ps.ap, lhsT=lhs_fn(k), rhs=rhs_fn(k, hi),
                               start=(k == 0), stop=last),
                           reads=reads(k), writes=[ps], signal=last)

        def rms_stats(chunk_bufs, sq_ring, stat_banks, rstd, sqt, dim):
            n = len(chunk_bufs)
            for k, cb in enumerate(chunk_bufs):
                sq = sq_ring[k % len(sq_ring)]
                cbuf, cap = cb if isinstance(cb, tuple) else (cb, cb.ap)
                sch.op("act", lambda e, cap=cap, sq=sq: e.activation(out=sq.ap, in_=cap, func=AF.Square),
                       reads=[cbuf], writes=[sq])
                for hi in range(NH):
                    ps = stat_banks[hi]
                    last = (k == n - 1)
                    sch.op("pe", lambda e, ps=ps, sq=sq, hi=hi, k=k, last=last: e.matmul(
                        ps.ap, lhsT=ones_b.ap, rhs=sq.ap[:, hi * 512:(hi + 1) * 512], start=(k == 0), stop=last),
                        reads=[sq, ones_b], writes=[ps], signal=(last or hi == NH - 1))
            for hi in range(NH):
                ps = stat_banks[hi]
                sch.op("act", lambda e, ps=ps, hi=hi: e.activation(
                    out=sqt.ap[:, hi * 512:(hi + 1) * 512], in_=ps.ap, func=AF.Sqrt, bias=eps_b.ap[:, 0:1], scale=1.0 / dim),
                    reads=[ps, eps_b], writes=[sqt])
            sch.op("dve", lambda e: e.reciprocal(out=rstd.ap, in_=sqt.ap), reads=[sqt], writes=[rstd])


        class Ring:
            def __init__(self, bl):
                self.b = bl
                self.i = 0
                self.pend = set()

            def take(self):
                b = self.b[self.i % len(self.b)]
                self.i += 1
                assert id(b) not in self.pend, "ring slot reused before its consumer was emitted"
                self.pend.add(id(b))
                return b

            def done(self, b):
                self.pend.discard(id(b))

        def run_jobs(jobs, d=1):
            n = len(jobs)
            loaded = [False] * n
            for i in range(n):
                if not loaded[i]:
                    if jobs[i][0] is not None:
                        jobs[i][0]()
                    loaded[i] = True
                for k in range(i + 1, min(n, i + 1 + d)):
                    if loaded[k]:
                        continue
                    if not jobs[k][2]:
                        break
                    if jobs[k][0] is not None:
                        jobs[k][0]()
                    loaded[k] = True
                jobs[i][1]()

        def norm_jobs(l, s, xsrc, g_b, hT, xring, sq_ring, stat_banks, rstd, sqt, extra_load=None):
            t0 = s * T
            jobs = []
            for kc in range(KD):
                st_ = {}

                def ld(kc=kc, st_=st_):
                    xb = xring.take()
                    st_["xb"] = xb
                    sch.dma(xb.ap, xsrc[kc * 128:(kc + 1) * 128, t0:t0 + T], xb, True)

                def cp(kc=kc, st_=st_):
                    xb = st_["xb"]
                    sq = sq_ring[kc % len(sq_ring)]
                    sch.op("act", lambda e: e.activation(out=sq.ap, in_=xb.ap, func=AF.Square),
                           reads=[xb], writes=[sq])
                    for hi in range(NH):
                        ps = stat_banks[hi]
                        last = (kc == KD - 1)
                        sch.op("pe", lambda e: e.matmul(
                            ps.ap, lhsT=ones_b.ap, rhs=sq.ap[:, hi * 512:(hi + 1) * 512], start=(kc == 0), stop=last),
                            reads=[sq, ones_b], writes=[ps], signal=(last or hi == NH - 1))
                    sch.op("dve", lambda e: e.tensor_scalar(
                        out=hT[kc].ap, in0=xb.ap, scalar1=g_b.ap[:, l * KD + kc:l * KD + kc + 1], scalar2=1.0,
                        op0=ALU.mult, op1=ALU.mult), reads=[xb, g_b], writes=[hT[kc]])
                    xring.done(xb)
                jobs.append((ld, cp, True))

            def fin():
                for hi in range(NH):
                    ps = stat_banks[hi]
                    sch.op("act", lambda e: e.activation(
                        out=sqt.ap[:, hi * 512:(hi + 1) * 512], in_=ps.ap, func=AF.Sqrt, bias=eps_b.ap[:, 0:1],
                        scale=1.0 / D), reads=[ps, eps_b], writes=[sqt])
                sch.op("dve", lambda e: e.reciprocal(out=rstd.ap, in_=sqt.ap), reads=[sqt], writes=[rstd])
                for kc in range(KD):
                    eng = "pool" if kc % 4 == 3 else "dve"
                    sch.op(eng, lambda e: e.tensor_tensor(out=hT[kc].ap, in0=hT[kc].ap, in1=rstd.ap, op=ALU.mult),
                           reads=[hT[kc], rstd], writes=[hT[kc]])
            jobs.append((extra_load, fin, False))
            return jobs

        def fm_tile_job(wring, w_src, pp_fn, rhs_bufs, nk, evac):
            st_ = {}

            def ld():
                w = wring.take()
                st_["w"] = w
                sch.dma(w.ap, w_src, w, True)

            def cp():
                w = st_["w"]
                pp = pp_fn()
                for hi in range(NH):
                    ps = pp[hi]
                    for k in range(nk):
                        last = (k == nk - 1)
                        rb, rap = rhs_bufs[k]
                        sch.op("pe", lambda e: e.matmul(
                            ps.ap, lhsT=w.ap[:, k * 128:(k + 1) * 128], rhs=rap[:, hi * 512:(hi + 1) * 512],
                            start=(k == 0), stop=last), reads=[w, rb], writes=[ps], signal=last)
                wring.done(w)
                evac(pp)
            return (ld, cp, True)

        _pcache = {}

        def bufs(name, fn):
            if name not in _pcache:
                ar.off = const_end
                _pcache[name] = fn()
            return _pcache[name]

        for l in range(L):
            phase_begin()
            def allocA():
                hT = [ar.alloc([T], BF16, f"hT{k}") for k in range(KD)]
                xst = [ar.alloc([T], F32, f"xst{i}") for i in range(3)]
                sqr = [ar.alloc([T], BF16, f"sq{i}") for i in range(2)]
                rstd = ar.alloc([T], F32, "rstd")
                sqt = ar.alloc([T], F32, "sqt")
                wfm = [ar.alloc([TA], BF16, f"wfm{i}") for i in range(3)]
                wtm = [ar.alloc([TBc], BF16, f"wtm{i}") for i in range(2)]
                wvg = wtm
                assert NVT <= 2
                cosb = ar.alloc([T], F32, "cos")
                sinb = ar.alloc([T], F32, "sin")
                r1 = [ar.alloc([T], F32, "r1_0")] * 2
                r2 = [ar.alloc([T], F32, "r2_0")] * 2
                qko = [ar.alloc([T], BF16, f"qko{i}") for i in range(2)]
                vout = ar.alloc([T // 128, DA], BF16, "vout")
                uT = [ar.alloc([T], BF16, f"uT{g}") for g in range(G)]
                vg = [ar.alloc([DG], F32, f"vg{i}") for i in range(2)]
                vn = [ar.alloc([DG], BF16, f"vn{i}") for i in range(2)]
                st = [ar.alloc([16], F32, f"st{i}") for i in range(2)]
                GT = [ar.alloc([T], BF16, f"GT{g}") for g in range(G)]
                gtmp = [ar.alloc([128], F32, f"gtmp{i}") for i in range(2)]
                mgo = [ar.alloc([T], BF16, f"mgo{i}") for i in range(2)]
                bsb = ar.alloc([G * 128], F32, "bsb")
                return dict(locals())
            nsA = bufs("A", allocA)
            hT, xst, sqr, rstd, sqt, wfm, wtm, wvg, cosb, sinb, r1, r2, qko, vout, uT, vg, vn, st, GT, gtmp, mgo, bsb = (
                nsA[k] for k in ("hT", "xst", "sqr", "rstd", "sqt", "wfm", "wtm", "wvg", "cosb", "sinb", "r1", "r2",
                                 "qko", "vout", "uT", "vg", "vn", "st", "GT", "gtmp", "mgo", "bsb"))
            sch.dma(bsb.ap, bs_in[:, l * G * 128:(l + 1) * G * 128], bsb, True)
            PA = [banks[0:2], banks[2:4]]
            PST = banks[4:6]
            PG = banks[6:8]
            xsrc = xT_in if l == 0 else xT
            xring = Ring(xst)
            wfm_r = Ring(wfm)
            wtm_r = Ring(wtm)
            pctr = {"pa": 0, "bi": 0}

            def next_pp():
                pp = PA[pctr["pa"] % 2]
                pctr["pa"] += 1
                return pp

            jobsA = []
            for s in range(NST):
                t0 = s * T

                def ld_cs(t0=t0):
                    sch.dma(cosb.ap, cos_in[:, t0:t0 + T], cosb, True)
                    sch.dma(sinb.ap, sin_in[:, t0:t0 + T], sinb, True)
                jobsA += norm_jobs(l, s, xsrc, g1_b, hT, xring, sqr, PST, rstd, sqt, extra_load=ld_cs)
                hb = [(hT[k], hT[k].ap) for k in range(KD)]
                for j in range(2 * H):
                    def evac_qk(pp, j=j, t0=t0):
                        ra, rb, qo = r1[j % 2], r2[j % 2], qko[j % 2]
                        for hi in range(NH):
                            ps = pp[hi]
                            sl = slice(hi * 512, (hi + 1) * 512)
                            sch.op("dve", lambda e: e.tensor_tensor(
                                out=ra.ap[:, sl], in0=ps.ap, in1=cosb.ap[:, sl], op=ALU.mult),
                                reads=[ps, cosb], writes=[ra])
                            sch.op("dve", lambda e: e.tensor_tensor(
                                out=rb.ap[0:64, sl], in0=ps.ap[64:128, :], in1=sinb.ap[0:64, sl], op=ALU.mult),
                                reads=[ps, sinb], writes=[rb])
                            sch.op("dve", lambda e: e.tensor_tensor(
                                out=rb.ap[64:128, sl], in0=ps.ap[0:64, :], in1=sinb.ap[64:128, sl], op=ALU.mult),
                                reads=[ps, sinb], writes=[rb])
                        sch.op("pool", lambda e: e.tensor_tensor(
                            out=qo.ap, in0=ra.ap, in1=rb.ap, op=ALU.add), reads=[ra, rb], writes=[qo])
                        dst = qT if j < H else kT
                        hh = j % H
                        sch.dma(dst[hh * 128:(hh + 1) * 128, t0:t0 + T], qo.ap, qo, False)
                    jobsA.append(fm_tile_job(wfm_r, wA[l * NA + j], next_pp, hb, KD, evac_qk))
                for g in range(G):
                    def evac_u(pp, g=g):
                        for hi in range(NH):
                            ps = pp[hi]
                            sch.op("act", lambda e: e.activation(
                                out=uT[g].ap[:, hi * 512:(hi + 1) * 512], in_=ps.ap, func=AF.Gelu),
                                reads=[ps], writes=[uT[g]])
                    jobsA.append(fm_tile_job(wfm_r, wA[l * NA + 2 * H + g], next_pp, hb, KD, evac_u))
                for nt in range(NVT):
                    stv = {}

                    def ld_v(nt=nt, stv=stv):
                        w = wtm_r.take()
                        stv["w"] = w
                        sch.dma(w.ap, wB[l * NB + nt], w, True)

                    def cp_v(nt=nt, stv=stv, t0=t0):
                        w = stv["w"]
                        for tt in range(T // 128):
                            ps = PA[pctr["bi"] % 2][0]
                            pctr["bi"] += 1
                            for k in range(KD):
                                last = (k == KD - 1)
                                sch.op("pe", lambda e: e.matmul(
                                    ps.ap[:, 0:CWV], lhsT=hT[k].ap[:, tt * 128:(tt + 1) * 128],
                                    rhs=w.ap[:, k * CWV:(k + 1) * CWV], start=(k == 0), stop=last),
                                    reads=[w, hT[k]], writes=[ps], signal=last)
                            if tt % 2 == 0:
                                sch.op("act", lambda e: e.copy(
                                    out=vout.ap[:, tt, nt * CWV:(nt + 1) * CWV], in_=ps.ap[:, 0:CWV]),
                                    reads=[ps], writes=[vout])
                            else:
                                sch.op("dve", lambda e: e.tensor_copy(
                                    out=vout.ap[:, tt, nt * CWV:(nt + 1) * CWV], in_=ps.ap[:, 0:CWV]),
                                    reads=[ps], writes=[vout])
                        wtm_r.done(w)
                        if nt == NVT - 1:
                            sch.dma(vS[t0:t0 + T, :].rearrange("(a p) f -> p a f", p=128), vout.ap, vout, False)
                    jobsA.append((ld_v, cp_v, True))
                stg_ = {}

                def ld_vg(stg_=stg_):
                    stg_["w"] = [wtm_r.take() for _ in range(NVT)]
                    for nt in range(NVT):
                        sch.dma(stg_["w"][nt].ap, wB[l * NB + NVT + nt], stg_["w"][nt], True)

                def cp_vg(stg_=stg_, t0=t0):
                    wv = stg_["w"]

                    def part_a(tt):
                        vgb, vnb, stb_ = vg[tt % 2], vn[tt % 2], st[tt % 2]
                        for nt in range(NVT):
                            ps = PA[pctr["bi"] % 2][1]
                            pctr["bi"] += 1
                            w = wv[nt]
                            for k in range(KD):
                                last = (k == KD - 1)
                                sch.op("pe", lambda e: e.matmul(
                                    ps.ap[:, 0:CWV], lhsT=hT[k].ap[:, tt * 128:(tt + 1) * 128],
                                    rhs=w.ap[:, k * CWV:(k + 1) * CWV], start=(k == 0), stop=last),
                                    reads=[w, hT[k]], writes=[ps], signal=last)
                            sch.op("act", lambda e: e.activation(
                                out=vgb.ap[:, nt * CWV:(nt + 1) * CWV], in_=ps.ap[:, 0:CWV], func=AF.Gelu,
                                accum_out=stb_.ap[:, nt:nt + 1]),
                                reads=[ps], writes=[vgb, stb_])
                        sch.op("act", lambda e: e.activation(
                            out=vnb.ap, in_=vgb.ap, func=AF.Square, accum_out=stb_.ap[:, 4:5]),
                            reads=[vgb], writes=[vnb, stb_])
                        if NVT == 2:
                            sch.op("dve", lambda e: e.tensor_tensor(
                                out=stb_.ap[:, 5:6], in0=stb_.ap[:, 0:1], in1=stb_.ap[:, 1:2], op=ALU.add),
                                reads=[stb_], writes=[stb_])
                        else:
                            sch.op("dve", lambda e: e.tensor_copy(out=stb_.ap[:, 5:6], in_=stb_.ap[:, 0:1]),
                                   reads=[stb_], writes=[stb_])
                        sch.op("dve", lambda e: e.tensor_scalar(
                            out=stb_.ap[:, 8:9], in0=stb_.ap[:, 5:6], scalar1=1.0 / DG, scalar2=None, op0=ALU.mult),
                            reads=[stb_], writes=[stb_])
                        sch.op("dve", lambda e: e.tensor_tensor(
                            out=stb_.ap[:, 6:7], in0=stb_.ap[:, 8:9], in1=stb_.ap[:, 8:9], op=ALU.mult),
                            reads=[stb_], writes=[stb_])
                        sch.op("dve", lambda e: e.scalar_tensor_tensor(
                            out=stb_.ap[:, 9:10], in0=stb_.ap[:, 4:5], scalar=1.0 / DG, in1=stb_.ap[:, 6:7],
                            op0=ALU.mult, op1=ALU.subtract),
                            reads=[stb_], writes=[stb_])
                        sch.op("act", lambda e: e.activation(
                            out=stb_.ap[:, 7:8], in_=stb_.ap[:, 9:10], func=AF.Sqrt, bias=eps_b.ap[:, 0:1], scale=1.0),
                            reads=[stb_, eps_b], writes=[stb_])
                        sch.op("dve", lambda e: e.reciprocal(out=stb_.ap[:, 10:11], in_=stb_.ap[:, 7:8]),
                               reads=[stb_], writes=[stb_])
                        sch.op("dve", lambda e: e.tensor_scalar(
                            out=vnb.ap, in0=vgb.ap, scalar1=stb_.ap[:, 8:9], scalar2=stb_.ap[:, 10:11],
                            op0=ALU.subtract, op1=ALU.mult),
                            reads=[vgb, stb_], writes=[vnb])

                    def part_b(tt):
                        vnb = vn[tt % 2]
                        for g in range(G):
                            ps = PG[(g // 4) % 2]
                            col = (g % 4) * 128
                            sch.op("pe", lambda e: e.matmul(
                                ps.ap[:, col:col + 128], lhsT=vnb.ap[:, g * 128:(g + 1) * 128],
                                rhs=wsT_b.ap[:, (l * G + g) * 128:(l * G + g + 1) * 128], start=True, stop=True),
                                reads=[vnb, wsT_b], writes=[ps])
                            gt = gtmp[g % 2]
                            sch.op("dve", lambda e: e.scalar_tensor_tensor(
                                out=gt.ap, in0=ps.ap[:, col:col + 128], scalar=lng_b.ap[:, l * G + g:l * G + g + 1],
                                in1=bsb.ap[:, g * 128:(g + 1) * 128], op0=ALU.mult, op1=ALU.add),
                                reads=[ps, lng_b, bsb], writes=[gt])
                            sch.op("pool", lambda e: e.tensor_tensor(
                                out=GT[g].ap[:, tt * 128:(tt + 1) * 128], in0=gt.ap,
                                in1=uT[g].ap[:, tt * 128:(tt + 1) * 128], op=ALU.mult),
                                reads=[gt, uT[g]], writes=[GT[g]])

                    ntt = T // 128
                    part_a(0)
                    for tt in range(ntt):
                        if tt + 1 < ntt:
                            part_a(tt + 1)
                        part_b(tt)
                    for w in wv:
                        wtm_r.done(w)
                    rms_stats(GT, sqr, PST, rstd, sqt, DG)
                    for g in range(G):
                        mo = mgo[g % 2]
                        sch.op("dve", lambda e: e.scalar_tensor_tensor(
                            out=mo.ap, in0=GT[g].ap, scalar=gg_b.ap[:, l * G + g:l * G + g + 1], in1=rstd.ap,
                            op0=ALU.mult, op1=ALU.mult),
                            reads=[GT[g], gg_b, rstd], writes=[mo])
                        sch.dma(mgT[g * 128:(g + 1) * 128, t0:t0 + T], mo.ap, mo, False)
                jobsA.append((ld_vg, cp_vg, False))
            run_jobs(jobsA, 2)

            phase_begin()
            def allocB1():
                qh = [ar.alloc([S], BF16, f"qh{i}") for i in range(2)]
                kh = [ar.alloc([S], BF16, f"kh{i}") for i in range(2)]
                Vr = [[ar.alloc([S // 128, 128], BF16, f"V{r}_{i}") for i in range(2)] for r in PATTERNS]
                num = ar.alloc([S], F32, "num")
                den = ar.alloc([S], F32, "den")
                oh = [ar.alloc([S], BF16, f"oh{i}") for i in range(2)]
                NPE, NPM = 4, 10
                pet = [ar.alloc([384], BF16, f"pet{i}") for i in range(NPE)]
                pmt = [ar.alloc([384], BF16, f"pmt{i}") for i in range(NPM)]
                cstg = [ar.alloc([PC], F32, f"cstg{i}") for i in range(5)]
                cstb = [ar.alloc([PC], BF16, f"cstb{i}") for i in range(3)]
                return dict(locals())
            nsB = bufs("B1", allocB1)
            qh, kh, Vr, num, den, oh, NPE, NPM, pet, pmt, cstg, cstb = (
                nsB[k] for k in ("qh", "kh", "Vr", "num", "den", "oh", "NPE", "NPM", "pet", "pmt", "cstg", "cstb"))
            cast_upto = max(cast_state["next"], cast_mark_R[l])
            cast_start = cast_state["next"]
            cast_n = cast_upto - cast_start
            PS_S = banks[0:3]
            PS_O = banks[3:5]
            PS_D = banks[5:7]
            LA = 3

            def load_head(h):
                i = h % 2
                sch.dma(qh[i].ap, qT[h * 128:(h + 1) * 128, :], qh[i], True)
                sch.dma(kh[i].ap, kT[h * 128:(h + 1) * 128, :], kh[i], True)
                for pi_, r in enumerate(PATTERNS):
                    vb = Vr[pi_][i]
                    nch = S // (128 * r)
                    if r == 1:
                        src = vS.rearrange("(c p) f -> p c f", p=128)[:, :, h * 128:(h + 1) * 128]
                        sch.dma(vb.ap, src, vb, True)
                    else:
                        for res in range(r):
                            src = vS.rearrange("(c p r) f -> p r c f", p=128, r=r)[:, res, :, h * 128:(h + 1) * 128]
                            sch.dma(vb.ap[:, res * nch:(res + 1) * nch, :], src, vb, True)

            load_head(0)
            if os.environ.get("DBG_BAR"):
                sch.barrier()
            sct = 0
            for h in range(H):
                i = h % 2
                if h + 1 < H and not os.environ.get("DBG_NOPF"):
                    load_head(h + 1)
                qb, kb = qh[i], kh[i]
                qk_jobs = []
                pv_jobs = []
                order = []
                for pi_, r in enumerate(PATTERNS):
                    nch = S // (128 * r)
                    nb = min(4, nch)
                    for res in range(r):
                        for cc in range(nch):
                            qk_jobs.append((pi_, r, res, cc, nch))
                            order.append(("qk", len(qk_jobs) - 1))
                            if cc >= 1:
                                pv_jobs.append((pi_, r, res, cc - 1, nch, nb))
                                order.append(("pv", len(pv_jobs) - 1))
                        pv_jobs.append((pi_, r, res, nch - 1, nch, nb))
                        order.append(("pv", len(pv_jobs) - 1))
                pm_of = {}
                qk_done = 0

                def emit_qk(idx):
                    nonlocal sct
                    pi_, r, res, cc, nch = qk_jobs[idx]
                    blo = max(0, cc - 1)
                    bhi = min(nch, cc + 2)
                    n = (bhi - blo) * 128
                    boff = (blo - (cc - 1)) * 128
                    ps = PS_S[sct % 3]
                    pe_t = pet[sct % NPE]
                    pm_t = pmt[sct % NPM]
                    sct += 1
                    k0 = 128 * cc * r + res
                    q0 = 128 * blo * r + res
                    sch.op("pe", lambda e, ps=ps, n=n, k0=k0, q0=q0, r=r: e.matmul(
                        ps.ap[:, 0:n], lhsT=kb.ap[:, k0:k0 + 127 * r + 1:r], rhs=qb.ap[:, q0:q0 + (n - 1) * r + 1:r],
                        start=True, stop=True), reads=[kb, qb], writes=[ps])
                    sch.op("act", lambda e, ps=ps, n=n, pe_t=pe_t: e.activation(
                        out=pe_t.ap[:, 0:n], in_=ps.ap[:, 0:n], func=AF.Exp, scale=scale_qk),
                        reads=[ps], writes=[pe_t])
                    sch.op("pool" if sct % 3 == 0 else "dve", lambda e, n=n, pe_t=pe_t, pm_t=pm_t, boff=boff: e.tensor_tensor(
                        out=pm_t.ap[:, 0:n], in0=pe_t.ap[:, 0:n], in1=band_b.ap[:, boff:boff + n], op=ALU.mult),
                        reads=[pe_t, band_b], writes=[pm_t])
                    pm_of[(pi_, res, cc)] = (pm_t, blo, sct)

                oct_ = [0]

                def emit_pv(idx):
                    pi_, r, res, b, nch, nb = pv_jobs[idx]
                    vb = Vr[pi_][i]
                    bslot = b % nb
                    if bslot == 0:
                        oct_[0] += 1
                    po = PS_O[oct_[0] % 2]
                    pd = PS_D[oct_[0] % 2]
                    cs = [cc for cc in (b - 1, b, b + 1) if 0 <= cc < nch]
                    for ci_, cc in enumerate(cs):
                        pm_t, blo, sct_at = pm_of[(pi_, res, cc)]
                        assert sct - sct_at < NPM - 1, "pm ring too small"
                        col = (b - blo) * 128
                        first = (ci_ == 0)
                        last = (ci_ == len(cs) - 1)
                        vidx = res * nch + cc if r > 1 else cc
                        sch.op("pe", lambda e, po=po, pm_t=pm_t, col=col, vidx=vidx, first=first, last=last, bslot=bslot, vb=vb: e.matmul(
                            po.ap[:, bslot * 128:(bslot + 1) * 128], lhsT=vb.ap[:, vidx, :], rhs=pm_t.ap[:, col:col + 128],
                            start=first, stop=last), reads=[vb, pm_t], writes=[po], signal=False)
                        sch.op("pe", lambda e, pd=pd, pm_t=pm_t, col=col, first=first, last=last, bslot=bslot: e.matmul(
                            pd.ap[:, bslot * 128:(bslot + 1) * 128], lhsT=ones_b.ap, rhs=pm_t.ap[:, col:col + 128],
                            start=first, stop=last), reads=[ones_b, pm_t], writes=[pd], signal=True)
                    if bslot == nb - 1:
                        b0 = b - (nb - 1)
                        w = nb * 128
                        if r == 1:
                            nv = num.ap[:, b0 * 128:b0 * 128 + w]
                            dv = den.ap[:, b0 * 128:b0 * 128 + w]
                        else:
                            nv = num.ap.rearrange("p (m r) -> p r m", r=r)[:, res, b0 * 128:b0 * 128 + w]
                            dv = den.ap.rearrange("p (m r) -> p r m", r=r)[:, res, b0 * 128:b0 * 128 + w]
                        if pi_ == 0:
                            sch.op("dve", lambda e, nv=nv, po=po, w=w: e.tensor_copy(out=nv, in_=po.ap[:, 0:w]),
                                   reads=[po], writes=[num])
                            sch.op("dve", lambda e, dv=dv, pd=pd, w=w: e.tensor_copy(out=dv, in_=pd.ap[:, 0:w]),
                                   reads=[pd], writes=[den])
                        else:
                            sch.op("dve", lambda e, nv=nv, po=po, w=w: e.tensor_tensor(
                                out=nv, in0=po.ap[:, 0:w], in1=nv, op=ALU.add), reads=[po, num], writes=[num])
                            sch.op("dve", lambda e, dv=dv, pd=pd, w=w: e.tensor_tensor(
                                out=dv, in0=pd.ap[:, 0:w], in1=dv, op=ALU.add), reads=[pd, den], writes=[den])

                qk_positions = [k for k, (kind, _) in enumerate(order) if kind == "qk"]
                for pos, (kind, idx) in enumerate(order):
                    if kind == "qk":
                        while qk_done <= min(idx + LA, len(qk_jobs) - 1):
                            emit_qk(qk_done)
                            qk_done += 1
                    else:
                        emit_pv(idx)
                    tgt = cast_start + (cast_n * (h * len(order) + pos + 1)) // (H * len(order))
                    emit_cast(min(tgt, cast_upto), cstg, cstb)
                ob = oh[i]
                sch.op("dve", lambda e: e.reciprocal(out=den.ap, in_=den.ap), reads=[den], writes=[den])
                sch.op("pool", lambda e, ob=ob: e.tensor_tensor(out=ob.ap, in0=num.ap, in1=den.ap, op=ALU.mult),
                       reads=[num, den], writes=[ob])
                sch.dma(aT[h * 128:(h + 1) * 128, :], ob.ap, ob, False)
                if h == H - 1:
                    emit_cast(cast_upto, cstg, cstb, flush=True)
                if h + 1 < H and os.environ.get("DBG_NOPF"):
                    load_head(h + 1)
                if debug_outputs and h == 0 and l == 0:
                    for nm, bb, dt in (("d_num", num, F32), ("d_rden", den, F32), ("d_qh", qb, BF16), ("d_kh", kb, BF16),
                                       ("d_V1", Vr[0][0], BF16), ("d_V4", Vr[min(1, len(Vr) - 1)][0], BF16), ("d_V16", Vr[-1][0], BF16)):
                        dd = nc.dram_tensor(nm, [128, S], dt, kind="ExternalOutput").ap()
                        src_ap = bb.ap if len(bb.ap.shape) == 2 else bb.ap.rearrange("p a b -> p (a b)")
                        sch.dma(dd, src_ap, bb, False)

            phase_begin()
            def allocB2():
                aall = ar.alloc([H, T], BF16, "aall")
                mall = ar.alloc([G, T], BF16, "mall")
                mixa = [ar.alloc([T], BF16, f"mx{h}") for h in range(H)]
                sqr = [ar.alloc([T], BF16, f"sq{i}") for i in range(2)]
                rstd = ar.alloc([T], F32, "rstd")
                sqt = ar.alloc([T], F32, "sqt")
                wfm = [ar.alloc([TA], BF16, f"wfm{i}") for i in range(3)]
                xold = [ar.alloc([T], F32, f"xo{i}") for i in range(3)]
                return dict(locals())
            nsC = bufs("B2", allocB2)
            aall, mall, mixa, sqr, rstd, sqt, wfm, xold = (
                nsC[k] for k in ("aall", "mall", "mixa", "sqr", "rstd", "sqt", "wfm", "xold"))
            PA = [banks[0:2], banks[2:4], banks[4:6]]
            PST = banks[6:8]
            xsrc = xT_in if l == 0 else xT
            wfm_r = Ring(wfm)
            xo_r = Ring(xold)
            pctr = {"pa": 0}

            def next_pp3():
                pp = PA[pctr["pa"] % 3]
                pctr["pa"] += 1
                return pp

            jobsB = []
            for s in range(NST):
                t0 = s * T

                def ld_am(t0=t0):
                    sch.dma(aall.ap, aT[:, t0:t0 + T].rearrange("(h p) t -> p h t", p=128), aall, True)
                    sch.dma(mall.ap, mgT[:, t0:t0 + T].rearrange("(g p) t -> p g t", p=128), mall, True)

                def cp_am():
                    rms_stats([(aall, aall.ap[:, h, :]) for h in range(H)], sqr, PST, rstd, sqt, DA)
                    for h in range(H):
                        sch.op("dve", lambda e: e.scalar_tensor_tensor(
                            out=mixa[h].ap, in0=aall.ap[:, h, :], scalar=ga_b.ap[:, l * H + h:l * H + h + 1],
                            in1=rstd.ap, op0=ALU.mult, op1=ALU.mult), reads=[aall, ga_b, rstd], writes=[mixa[h]])
                jobsB.append((ld_am, cp_am, False))
                mix = [(mixa[h], mixa[h].ap) for h in range(H)] + [(mall, mall.ap[:, g, :]) for g in range(G)]
                for j in range(KD):
                    stx = {}

                    def ev_o(pp, j=j, stx=stx, t0=t0):
                        xo = stx["xo"]
                        for hi in range(NH):
                            ps = pp[hi]
                            sl = slice(hi * 512, (hi + 1) * 512)
                            sch.op("dve", lambda e: e.tensor_tensor(
                                out=xo.ap[:, sl], in0=ps.ap, in1=xo.ap[:, sl], op=ALU.add), reads=[ps, xo], writes=[xo])
                        sch.dma(xT[j * 128:(j + 1) * 128, t0:t0 + T], xo.ap, xo, False)
                        xo_r.done(xo)
                    tj2 = fm_tile_job(wfm_r, wA[l * NA + 2 * H + G + j], next_pp3, mix, KD, ev_o)

                    def ld_o(j=j, stx=stx, tj2=tj2, t0=t0):
                        tj2[0]()
                        xo = xo_r.take()
                        stx["xo"] = xo
                        sch.dma(xo.ap, xsrc[j * 128:(j + 1) * 128, t0:t0 + T], xo, True)
                    jobsB.append((ld_o, tj2[1], True))
            run_jobs(jobsB, 1)

            phase_begin()
            def allocC():
                hT = [ar.alloc([T], BF16, f"h2T{k}") for k in range(KD)]
                ffT = [ar.alloc([T], BF16, f"ffT{f}") for f in range(KF)]
                xst = [ar.alloc([T], F32, f"xst{i}") for i in range(2)]
                xo2 = [ar.alloc([T], F32, f"xo{i}") for i in range(2)]
                sqr = [ar.alloc([T], BF16, f"sq{i}") for i in range(2)]
                sqt = ar.alloc([T], F32, "sqt")
                rstd = sqt
                cstgC = [ar.alloc([PC], F32, f"cstgC{i}") for i in range(4)]
                cstbC = [ar.alloc([PC], BF16, f"cstbC{i}") for i in range(2)]
                wgs = [ar.alloc([TA], BF16, f"wg{i}") for i in range(2)]
                wus = [ar.alloc([TA], BF16, f"wu{i}") for i in range(2)]
                wds = [ar.alloc([TC], BF16, f"wd{i}") for i in range(2)]
                sg = [ar.alloc([T], BF16, f"sg{i}") for i in range(2)]
                return dict(locals())
            nsD = bufs("C", allocC)
            hT, ffT, xst, xo2, sqr, rstd, sqt, wgs, wus, wds, sg, cstgC, cstbC = (
                nsD[k] for k in ("hT", "ffT", "xst", "xo2", "sqr", "rstd", "sqt", "wgs", "wus", "wds", "sg",
                                 "cstgC", "cstbC"))
            PGt = [banks[0:2], banks[2:4]]
            PUp = [banks[4:6], banks[6:8]]
            PD = [banks[0:2], banks[2:4], banks[4:6]]
            PST = banks[6:8]
            xring = Ring(xst)
            xo_r = Ring(xo2)
            wg_r, wu_r, wd_r = Ring(wgs), Ring(wus), Ring(wds)
            pctr = {"pd": 0}

            def next_pd():
                pp = PD[pctr["pd"] % 3]
                pctr["pd"] += 1
                return pp

            def gu_job(f):
                st_ = {}

                def ld():
                    st_["wg"] = wg_r.take()
                    st_["wu"] = wu_r.take()
                    sch.dma(st_["wg"].ap, wA[l * NA + 2 * H + G + KD + f], st_["wg"], True)
                    sch.dma(st_["wu"].ap, wA[l * NA + 2 * H + G + KD + KF + f], st_["wu"], True)

                def cp():
                    wg_, wu_ = st_["wg"], st_["wu"]
                    pg, pu = PGt[f % 2], PUp[f % 2]
                    for w, pp in ((wg_, pg), (wu_, pu)):
                        for hi in range(NH):
                            ps = pp[hi]
                            for k in range(KD):
                                last = (k == KD - 1)
                                sch.op("pe", lambda e: e.matmul(
                                    ps.ap, lhsT=w.ap[:, k * 128:(k + 1) * 128], rhs=hT[k].ap[:, hi * 512:(hi + 1) * 512],
                                    start=(k == 0), stop=last), reads=[w, hT[k]], writes=[ps], signal=last)
                    wg_r.done(wg_)
                    wu_r.done(wu_)
                    sgb = sg[f % 2]
                    for hi in range(NH):
                        sl = slice(hi * 512, (hi + 1) * 512)
                        sch.op("act", lambda e: e.activation(
                            out=sgb.ap[:, sl], in_=pg[hi].ap, func=AF.Silu), reads=[pg[hi]], writes=[sgb])
                        sch.op("dve", lambda e: e.tensor_tensor(
                            out=ffT[f].ap[:, sl], in0=pu[hi].ap, in1=sgb.ap[:, sl], op=ALU.mult),
                            reads=[pu[hi], sgb], writes=[ffT[f]])
                return (ld, cp, True)

            def down_job(j, t0):
                stx = {}
                fb = [(ffT[k], ffT[k].ap) for k in range(KF)]

                def ev(pp):
                    xo = stx["xo"]
                    for hi in range(NH):
                        ps = pp[hi]
                        sl = slice(hi * 512, (hi + 1) * 512)
                        sch.op("dve", lambda e: e.tensor_tensor(
                            out=xo.ap[:, sl], in0=ps.ap, in1=xo.ap[:, sl], op=ALU.add), reads=[ps, xo], writes=[xo])
                    sch.dma(xT[j * 128:(j + 1) * 128, t0:t0 + T], xo.ap, xo, False)
                    xo_r.done(xo)
                tj = fm_tile_job(wd_r, wC[l * KD + j], next_pd, fb, KF, ev)

                def ld():
                    tj[0]()
                    xo = xo_r.take()
                    stx["xo"] = xo
                    sch.dma(xo.ap, xT[j * 128:(j + 1) * 128, t0:t0 + T], xo, True)
                return (ld, tj[1], True)

            jobsC = []
            nj = [norm_jobs(l, s, xT, g2_b, hT, xring, sqr, PST, rstd, sqt) for s in range(NST)]
            jobsC += nj[0]
            for s in range(NST):
                t0 = s * T
                for f in range(KF):
                    jobsC.append(gu_job(f))
                for j in range(KD):
                    jobsC.append(down_job(j, t0))
                    if s + 1 < NST:
                        jobsC.append(nj[s + 1][j])
                if s + 1 < NST:
                    jobsC += nj[s + 1][KD:]
            if l + 1 < L:
                c_start = cast_state["next"]
                c_n = cast_mark_R[l + 1] - c_start
                njc = len(jobsC)

                def wrap(jb, i):
                    def cp():
                        jb[1]()
                        emit_cast(c_start + (c_n * (i + 1)) // njc, cstgC, cstbC, flush=(i == njc - 1))
                    return (jb[0], cp, jb[2])
                jobsC = [wrap(jb, i) for i, jb in enumerate(jobsC)]
            run_jobs(jobsC, 1)

        phase_begin()
        xall = ar.alloc([KD, T], F32, "xall")
        sqr = [ar.alloc([T], BF16, f"sq{i}") for i in range(2)]
        rstd = ar.alloc([T], F32, "rstd")
        sqt = ar.alloc([T], F32, "sqt")
        PST = banks[0:2]
        for s in range(NST):
            t0 = s * T
            sch.dma(xall.ap, xT[:, t0:t0 + T].rearrange("(k p) t -> p k t", p=128), xall, True)
            rms_stats([(xall, xall.ap[:, kc, :]) for kc in range(KD)], sqr, PST, rstd, sqt, D)
            for kc in range(KD):
                eng = "dve"
                sch.op(eng, lambda e, kc=kc: e.scalar_tensor_tensor(
                    out=xall.ap[:, kc, :], in0=xall.ap[:, kc, :], scalar=gf_b.ap[:, kc:kc + 1], in1=rstd.ap,
                    op0=ALU.mult, op1=ALU.mult), reads=[xall, gf_b, rstd], writes=[xall])
            sch.dma(outT[:, t0:t0 + T].rearrange("(k p) t -> p k t", p=128), xall.ap, xall, False)
        if debug_outputs:
            phase_begin()
            for nm, src, dt in (("d_xT", xT, F32), ("d_qT", qT, BF16), ("d_kT", kT, BF16), ("d_v", vS, BF16),
                                ("d_aT", aT, BF16), ("d_mgT", mgT, BF16)):
                rows, cols = src.shape
                dst = nc.dram_tensor(nm, [rows, cols], dt, kind="ExternalOutput").ap()
                ar.off = const_end
                dbuf = [ar.alloc([cols], dt, f"dbg{nm}{i}") for i in range(2)]
                for rr in range(rows // 128):
                    b = dbuf[rr % 2]
                    sch.dma(b.ap, src[rr * 128:(rr + 1) * 128, :], b, True)
                    sch.dma(dst[rr * 128:(rr + 1) * 128, :], b.ap, b, False)
                sch.barrier()
        sch.barrier()

        with nc.Block() as block:
            @block.sync
            def _(e):
                sch.replay("sp", e)

            @block.tensor
            def _(e):
                sch.replay("pe", e)

            @block.scalar
            def _(e):
                sch.replay("act", e)

            @block.vector
            def _(e):
                sch.replay("dve", e)

            @block.gpsimd
            def _(e):
                sch.replay("pool", e)
    return nc, {k: len(v) for k, v in sch.streams.items()}


def tile_w(W, cw):
    K, N = W.shape
    return np.ascontiguousarray(
        W.reshape(K // 128, 128, N // cw, cw).transpose(2, 1, 0, 3)).reshape(N // cw, 128, (K // 128) * cw)


def host_prep(cfg, p):
    c = derive(cfg)
    S, D, L = c["S"], c["D"], c["DEPTH"]
    DA, DG, H, G, KD, KF, CWV = c["DA"], c["DG"], c["H"], c["G"], c["KD"], c["KF"], c["CWV"]
    wA, wB, wC = [], [], []
    for l in range(L):
        w_in = p["w_in"][l]
        tiles = [tile_w(w_in[:, 0:2 * DA], 128),
                 tile_w(w_in[:, 3 * DA:3 * DA + DG], 128),
                 tile_w(p["w_out"][l], 128),
                 tile_w(p["w_gate"][l], 128),
                 tile_w(p["w_up"][l], 128)]
        wA.append(np.concatenate(tiles, 0))
        wB.append(np.concatenate([tile_w(w_in[:, 2 * DA:3 * DA], CWV),
                                  tile_w(w_in[:, 3 * DA + DG:3 * DA + 2 * DG], CWV)], 0))
        wC.append(tile_w(p["w_down"][l], 128))
    shared = {
        "wA": np.concatenate(wA, 0), "wB": np.concatenate(wB, 0), "wC": np.concatenate(wC, 0),
    }

    def pp(v, n):
        return np.ascontiguousarray(v.reshape(L, n, 128).transpose(2, 0, 1)).reshape(128, L * n)

    shared["g1"] = pp(p["norm1_g"], KD)
    shared["g2"] = pp(p["norm2_g"], KD)
    shared["ga"] = pp(p["mix_norm_attn_g"], H)
    shared["gg"] = pp(p["mix_norm_gmlp_g"], G)
    shared["lng"] = pp(p["gmlp_ln_g"], G)
    shared["gf"] = np.ascontiguousarray(p["final_g"].reshape(KD, 128).T)
    shared["wsT"] = np.ascontiguousarray(p["w_spatial"].transpose(3, 0, 1, 2)).reshape(128, L * G * 128)
    shared["bs"] = np.ascontiguousarray(np.broadcast_to(p["b_spatial"].reshape(1, L * G * 128), (128, L * G * 128)))
    pos = np.arange(S, dtype=np.float32)
    inv = (np.float32(10000.0) ** (-np.arange(0, 128, 2, dtype=np.float32) / np.float32(128))).astype(np.float32)
    ang = (pos[None, :] * inv[:, None]).astype(np.float32)
    cosv = np.cos(ang).astype(np.float32)
    sinv = np.sin(ang).astype(np.float32)
    shared["cosT"] = np.ascontiguousarray(np.concatenate([cosv, cosv], 0))
    shared["sinS"] = np.ascontiguousarray(np.concatenate([-sinv, sinv], 0))
    kk = np.arange(128)[:, None]
    jj = np.arange(384)[None, :]
    shared["band"] = (np.abs(kk + 128 - jj) <= 64).astype(np.float32)
    return shared


FULL_CFG = dict(S=4096, D=2048, DEPTH=4, T=1024)
_CACHE = {}


def run(cfg, inputs, n_cores):
    p = {k: np.asarray(v, dtype=np.float32) for k, v in inputs.items()}
    shared = host_prep(cfg, p)
    key = tuple(sorted(cfg.items()))
    if key not in _CACHE:
        _CACHE[key] = build(cfg)[0]
    nc = _CACHE[key]
    x = p["x"]
    in_maps = []
    for b in range(n_cores):
        m = dict(shared)
        m["xT"] = np.ascontiguousarray(x[b].T)
        in_maps.append(m)
    res = run_bass_kernel_spmd(nc, in_maps, core_ids=list(range(n_cores)))
    out = np.stack([np.ascontiguousarray(res.results[b]["outT"].T) for b in range(n_cores)], 0)
    return out.astype(np.float32)


def kernel(**inputs):
    return run(FULL_CFG, inputs, 8)
```

```python
import bisect
import math
from contextlib import ExitStack

import numpy as np
import ml_dtypes

import concourse.bass as bass
import concourse.mybir as mybir
from concourse.bass_utils import run_bass_kernel_spmd

F32 = mybir.dt.float32
BF16 = mybir.dt.bfloat16
ALU = mybir.AluOpType
AF = mybir.ActivationFunctionType

EPS = 1e-6
import os
PATTERNS = tuple(int(v) for v in os.environ.get("DBG_PAT", "1,4,16").split(","))
ENGS = ("pe", "act", "dve", "pool", "sp")
SEM_LIMIT = 30000
STORE_Q = "pool"


def derive(cfg):
    c = dict(cfg)
    D = c["D"]
    c["DA"] = D // 2
    c["DG"] = D - c["DA"]
    c["H"] = c["DA"] // 128
    c["G"] = c["DG"] // 128
    c["DFF"] = -(-(8 * D) // (3 * 256)) * 256
    c["KD"] = D // 128
    c["KF"] = c["DFF"] // 128
    c["CWV"] = min(512, c["DA"])
    c["NVT"] = c["DA"] // c["CWV"]
    c["NA"] = 2 * c["H"] + c["G"] + c["KD"] + 2 * c["KF"]
    c["NB"] = 2 * c["NVT"]
    return c


class Buf:
    __slots__ = ("ap", "w", "r", "rd", "dsem", "dval", "name")

    def __init__(self, ap, name=""):
        self.ap = ap
        self.w = None
        self.r = {}
        self.rd = []
        self.dsem = None
        self.dval = 0
        self.name = name


class _Rec:
    def __init__(self):
        self.call = None

    def __getattr__(self, name):
        def f(*a, **k):
            assert self.call is None
            self.call = (name, a, k)
            return self
        return f


class Sched:
    def __init__(self, sem_pool):
        self.pool = list(sem_pool)
        self.streams = {e: [] for e in ENGS}
        self.engsem = {}
        self.sig_seq = {e: [] for e in ENGS}
        self.sig_tok = {e: [] for e in ENGS}
        self.seq = {e: 0 for e in ENGS}
        self.seen = {e: {} for e in ENGS}
        self.dsems = []
        for e in ("pe", "act", "dve", "pool"):
            self.engsem[e] = [self.pool.pop(), 0]

    def _resolve(self, tok):
        if tok[0] == "d":
            return tok[1], tok[2]
        _, eng, seq = tok
        i = bisect.bisect_left(self.sig_seq[eng], seq)
        assert i < len(self.sig_seq[eng]), "dependency on unsignaled op"
        return self.sig_tok[eng][i]

    def _wait(self, eng, tok):
        sem, val = self._resolve(tok)
        key = id(sem)
        if self.seen[eng].get(key, 0) >= val:
            return
        self.seen[eng][key] = val
        self.streams[eng].append(("w", sem, val))

    def _deps(self, eng, reads, writes):
        for b in reads:
            if b.w is not None:
                self._dep(eng, b.w, "raw")
        for b in writes:
            if b.w is not None:
                self._dep(eng, b.w, "waw")
            for t in b.r.values():
                self._dep(eng, t, "war")
            for t in b.rd:
                self._dep(eng, t, "war")

    def _dep(self, eng, tok, kind):
        if tok[0] == "e" and tok[1] == eng:
            if eng == "pe" or kind == "war":
                return
        self._wait(eng, tok)

    def op(self, eng, fn, reads=(), writes=(), signal=True):
        self._deps(eng, reads, writes)
        seq = self.seq[eng]
        self.seq[eng] += 1
        tok = ("e", eng, seq)
        sem = None
        if signal:
            es = self.engsem[eng]
            if es[1] >= SEM_LIMIT:
                es[0] = self.pool.pop()
                es[1] = 0
            es[1] += 1
            sem = es[0]
            self.sig_seq[eng].append(seq)
            self.sig_tok[eng].append((sem, es[1]))
        rec = _Rec()
        fn(rec)
        assert rec.call is not None
        self.streams[eng].append(("o", rec.call, sem, 1))
        for b in reads:
            b.r[eng] = tok
        for b in writes:
            b.w = tok
            b.r = {}
            b.rd = []
        return tok

    def dma(self, out_ap, in_ap, sb, load, extra_reads=(), extra_writes=()):
        eng = "sp" if load else STORE_Q
        if load:
            if sb.w is not None and sb.w[0] == "d" and sb.dsem is not None and sb.w[1] is sb.dsem \
                    and not sb.r and not sb.rd:
                pass
            else:
                self._deps(eng, extra_reads, [sb] + list(extra_writes))
        else:
            self._deps(eng, [sb] + list(extra_reads), extra_writes)
        if sb.dsem is None:
            sb.dsem = self.pool.pop()
            self.dsems.append(sb)
        sb.dval += 16
        assert sb.dval < 2 * SEM_LIMIT
        tok = ("d", sb.dsem, sb.dval)
        self.streams[eng].append(("d", out_ap, in_ap, sb.dsem))
        if load:
            sb.w = tok
            sb.r = {}
            sb.rd = []
        else:
            sb.rd.append(tok)
        return tok

    def fence_dma(self, engs=("sp",)):
        for e in engs:
            for b in self.dsems:
                if b.dval:
                    self._wait(e, ("d", b.dsem, b.dval))

    def barrier(self):
        for e in ENGS:
            for o in ("pe", "act", "dve", "pool"):
                if o != e and self.sig_tok[o]:
                    sem, val = self.sig_tok[o][-1]
                    assert self.sig_seq[o][-1] == self.seq[o] - 1, "last op must signal"
                    self._wait(e, ("d", sem, val))
        self.fence_dma(ENGS)

    def replay(self, eng, e):
        for it in self.streams[eng]:
            if it[0] == "w":
                e.wait_ge(it[1], it[2])
            elif it[0] == "o":
                name, a, k = it[1]
                ins = getattr(e, name)(*a, **k)
                if it[2] is not None:
                    ins.then_inc(it[2], 1)
            else:
                e.dma_start(out=it[1], in_=it[2]).then_inc(it[3], 16)


class Arena:
    def __init__(self, t, nwords):
        self.t = t
        self.n = nwords
        self.off = 0

    def reset(self):
        self.off = 0

    def alloc(self, free_shape, dtype, name=""):
        n = int(np.prod(free_shape))
        esz = 4 if dtype == F32 else 2
        nwords = (n * esz + 3) // 4
        nwords = (nwords + 7) // 8 * 8
        assert self.off + nwords <= self.n, f"arena overflow at {name}: {self.off + nwords} > {self.n}"
        ap = self.t[:, self.off:self.off + nwords]
        if dtype != F32:
            ap = ap.bitcast(dtype)
        ap = ap[:, 0:n]
        if len(free_shape) == 2:
            ap = ap.rearrange("p (a b) -> p a b", a=free_shape[0])
        elif len(free_shape) == 3:
            ap = ap.rearrange("p (a b c) -> p a b c", a=free_shape[0], b=free_shape[1])
        self.off += nwords
        return Buf(ap, name)


def build(cfg, debug_outputs=False):
    c = derive(cfg)
    S, D, L, T = c["S"], c["D"], c["DEPTH"], c["T"]
    DA, DG, H, G, DFF, KD, KF = c["DA"], c["DG"], c["H"], c["G"], c["DFF"], c["KD"], c["KF"]
    CWV, NVT, NA, NB = c["CWV"], c["NVT"], c["NA"], c["NB"]
    NST = S // T
    NH = T // 512
    assert T % 512 == 0 and S % T == 0
    TA = KD * 128
    TBc = KD * CWV
    TC = KF * 128
    scale_qk = 1.0 / math.sqrt(128.0)

    nc = bass.Bass("TRN2", target_bir_lowering=False)

    def din(name, shape, dt=F32):
        return nc.dram_tensor(name, list(shape), dt, kind="ExternalInput").ap()

    xT_in = din("xT", [D, S])
    wA_in = din("wA", [L * NA, 128, TA])
    wB_in = din("wB", [L * NB, 128, TBc])
    wC_in = din("wC", [L * KD, 128, TC])
    wsT_in = din("wsT", [128, L * G * 128])
    g1_in = din("g1", [128, L * KD])
    g2_in = din("g2", [128, L * KD])
    ga_in = din("ga", [128, L * H])
    gg_in = din("gg", [128, L * G])
    lng_in = din("lng", [128, L * G])
    bs_in = din("bs", [128, L * G * 128])
    gf_in = din("gf", [128, KD])
    cos_in = din("cosT", [128, S])
    sin_in = din("sinS", [128, S])
    band_in = din("band", [128, 384])
    outT = nc.dram_tensor("outT", [D, S], F32, kind="ExternalOutput").ap()

    dk = {}
    xT = nc.dram_tensor("xT_s", [D, S], F32, **dk).ap()
    wA = nc.dram_tensor("wA_b", [L * NA, 128, TA], BF16).ap()
    wB = nc.dram_tensor("wB_b", [L * NB, 128, TBc], BF16).ap()
    wC = nc.dram_tensor("wC_b", [L * KD, 128, TC], BF16).ap()
    qT = nc.dram_tensor("qT_s", [DA, S], BF16, **dk).ap()
    kT = nc.dram_tensor("kT_s", [DA, S], BF16, **dk).ap()
    vS = nc.dram_tensor("v_s", [S, DA], BF16, **dk).ap()
    aT = nc.dram_tensor("aT_s", [DA, S], BF16, **dk).ap()
    mgT = nc.dram_tensor("mgT_s", [DG, S], BF16, **dk).ap()

    es = ExitStack()
    with es:
        sems = [es.enter_context(nc.semaphore(f"s{i}")) for i in range(96)]
        ARENA_WORDS = (nc.SBUF_PARTITION_SIZE_BYTES - 16384 - 512) // 4
        arena_t = es.enter_context(nc.sbuf_tensor("arena", [128, ARENA_WORDS], F32))
        banks = [Buf(es.enter_context(nc.psum_tensor(f"ps{i}", [128, 512], F32))[:, :], f"ps{i}")
                 for i in range(8)]
        sch = Sched(sems)
        ar = Arena(arena_t, ARENA_WORDS)

        ones_b = ar.alloc([128], BF16, "ones")
        band_b = ar.alloc([384], BF16, "band")
        band_f = ar.alloc([384], F32, "bandf")
        g1_b = ar.alloc([L * KD], F32, "g1")
        g2_b = ar.alloc([L * KD], F32, "g2")
        ga_b = ar.alloc([L * H], F32, "ga")
        gg_b = ar.alloc([L * G], F32, "gg")
        lng_b = ar.alloc([L * G], F32, "lng")
        gf_b = ar.alloc([KD], F32, "gf")
        wsT_b = ar.alloc([L * G * 128], BF16, "wsT")
        eps_b = ar.alloc([8], F32, "eps")
        const_end = ar.off

        def phase_begin():
            sch.barrier()
            ar.off = const_end

        ar.off = const_end
        tmpc = ar.alloc([L * G * 128], F32, "tmpc")
        for dst, src in ((g1_b, g1_in), (g2_b, g2_in), (ga_b, ga_in), (gg_b, gg_in),
                         (lng_b, lng_in), (gf_b, gf_in), (band_f, band_in), (tmpc, wsT_in)):
            sch.dma(dst.ap, src, dst, True)
        sch.op("dve", lambda e: e.memset(ones_b.ap, 1.0), writes=[ones_b])
        sch.op("dve", lambda e: e.memset(eps_b.ap, EPS), writes=[eps_b])
        sch.op("dve", lambda e: e.tensor_copy(out=band_b.ap, in_=band_f.ap), reads=[band_f], writes=[band_b])
        sch.op("dve", lambda e: e.tensor_copy(out=wsT_b.ap, in_=tmpc.ap), reads=[tmpc], writes=[wsT_b])

        PC = 512
        cast_pieces = []
        cast_mark_A = []
        cast_mark_R = []

        def add_pieces(src, dst, t_lo, t_hi, tcols):
            for t in range(t_lo, t_hi):
                for c0 in range(0, tcols, PC):
                    cw = min(PC, tcols - c0)
                    cast_pieces.append((src[t, :, c0:c0 + cw], dst[t, :, c0:c0 + cw], cw))

        for l in range(L):
            add_pieces(wA_in, wA, l * NA, l * NA + 2 * H + G, TA)
            add_pieces(wB_in, wB, l * NB, (l + 1) * NB, TBc)
            cast_mark_A.append(len(cast_pieces))
            add_pieces(wA_in, wA, l * NA + 2 * H + G, (l + 1) * NA, TA)
            add_pieces(wC_in, wC, l * KD, (l + 1) * KD, TC)
            cast_mark_R.append(len(cast_pieces))
        cast_state = {"next": 0, "n": 0, "m": 0}
        cast_fl = []
        cast_engs = ("dve", "act")

        def emit_cast(upto, stg, stb, flush=False):
            def issue():
                src, dst, cw = cast_pieces[cast_state["next"]]
                cast_state["next"] += 1
                a = stg[cast_state["n"] % len(stg)]
                cast_state["n"] += 1
                sch.dma(a.ap[:, 0:cw], src, a, True)
                cast_fl.append((a, dst, cw))

            def finish():
                a, dst, cw = cast_fl.pop(0)
                b = stb[cast_state["m"] % len(stb)]
                ce = cast_engs[cast_state["m"] % 2]
                cast_state["m"] += 1
                if ce == "act":
                    sch.op("act", lambda e: e.copy(out=b.ap[:, 0:cw], in_=a.ap[:, 0:cw]), reads=[a], writes=[b])
                else:
                    sch.op(ce, lambda e: e.tensor_copy(out=b.ap[:, 0:cw], in_=a.ap[:, 0:cw]), reads=[a], writes=[b])
                sch.dma(dst, b.ap[:, 0:cw], b, False)

            while cast_state["next"] < upto:
                issue()
                if len(cast_fl) > len(stg) - 2:
                    finish()
            if flush:
                while cast_fl:
                    finish()

        phase_begin()
        stg0 = [ar.alloc([PC], F32, f"stg{i}") for i in range(5)]
        stb0 = [ar.alloc([PC], BF16, f"stb{i}") for i in range(4)]
        emit_cast(cast_mark_A[0], stg0, stb0, flush=True)

        def mm_group(ps_list, lhs_fn, rhs_fn, nk, reads):
            for hi, ps in enumerate(ps_list):
                for k in range(nk):
                    last = (k == nk - 1)
                    sch.op("pe",
                           lambda e, ps=ps, k=k, hi=hi, last=last: e.matmul(
                               ps.ap, lhsT=lhs_fn(k), rhs=rhs_fn(k, hi),
                               start=(k == 0), stop=last),
                           reads=reads(k), writes=[ps], signal=last)

        def rms_stats(chunk_bufs, sq_ring, stat_banks, rstd, sqt, dim):
            n = len(chunk_bufs)
            for k, cb in enumerate(chunk_bufs):
                sq = sq_ring[k % len(sq_ring)]
                cbuf, cap = cb if isinstance(cb, tuple) else (cb, cb.ap)
                sch.op("act", lambda e, cap=cap, sq=sq: e.activation(out=sq.ap, in_=cap, func=AF.Square),
                       reads=[cbuf], writes=[sq])
                for hi in range(NH):
                    ps = stat_banks[hi]
                    last = (k == n - 1)
                    sch.op("pe", lambda e, ps=ps, sq=sq, hi=hi, k=k, last=last: e.matmul(
                        ps.ap, lhsT=ones_b.ap, rhs=sq.ap[:, hi * 512:(hi + 1) * 512], start=(k == 0), stop=last),
                        reads=[sq, ones_b], writes=[ps], signal=(last or hi == NH - 1))
            for hi in range(NH):
                ps = stat_banks[hi]
                sch.op("act", lambda e, ps=ps, hi=hi: e.activation(
                    out=sqt.ap[:, hi * 512:(hi + 1) * 512], in_=ps.ap, func=AF.Sqrt, bias=eps_b.ap[:, 0:1], scale=1.0 / dim),
                    reads=[ps, eps_b], writes=[sqt])
            sch.op("dve", lambda e: e.reciprocal(out=rstd.ap, in_=sqt.ap), reads=[sqt], writes=[rstd])


        class Ring:
            def __init__(self, bl):
                self.b = bl
                self.i = 0
                self.pend = set()

            def take(self):
                b = self.b[self.i % len(self.b)]
                self.i += 1
                assert id(b) not in self.pend, "ring slot reused before its consumer was emitted"
                self.pend.add(id(b))
                return b

            def done(self, b):
                self.pend.discard(id(b))

        def run_jobs(jobs, d=1):
            n = len(jobs)
            loaded = [False] * n
            for i in range(n):
                if not loaded[i]:
                    if jobs[i][0] is not None:
                        jobs[i][0]()
                    loaded[i] = True
                for k in range(i + 1, min(n, i + 1 + d)):
                    if loaded[k]:
                        continue
                    if not jobs[k][2]:
                        break
                    if jobs[k][0] is not None:
                        jobs[k][0]()
                    loaded[k] = True
                jobs[i][1]()

        def norm_jobs(l, s, xsrc, g_b, hT, xring, sq_ring, stat_banks, rstd, sqt, extra_load=None):
            t0 = s * T
            jobs = []
            for kc in range(KD):
                st_ = {}

                def ld(kc=kc, st_=st_):
                    xb = xring.take()
                    st_["xb"] = xb
                    sch.dma(xb.ap, xsrc[kc * 128:(kc + 1) * 128, t0:t0 + T], xb, True)

                def cp(kc=kc, st_=st_):
                    xb = st_["xb"]
                    sq = sq_ring[kc % len(sq_ring)]
                    sch.op("act", lambda e: e.activation(out=sq.ap, in_=xb.ap, func=AF.Square),
                           reads=[xb], writes=[sq])
                    for hi in range(NH):
                        ps = stat_banks[hi]
                        last = (kc == KD - 1)
                        sch.op("pe", lambda e: e.matmul(
                            ps.ap, lhsT=ones_b.ap, rhs=sq.ap[:, hi * 512:(hi + 1) * 512], start=(kc == 0), stop=last),
                            reads=[sq, ones_b], writes=[ps], signal=(last or hi == NH - 1))
                    sch.op("dve", lambda e: e.tensor_scalar(
                        out=hT[kc].ap, in0=xb.ap, scalar1=g_b.ap[:, l * KD + kc:l * KD + kc + 1], scalar2=1.0,
                        op0=ALU.mult, op1=ALU.mult), reads=[xb, g_b], writes=[hT[kc]])
                    xring.done(xb)
                jobs.append((ld, cp, True))

            def fin():
                for hi in range(NH):
                    ps = stat_banks[hi]
                    sch.op("act", lambda e: e.activation(
                        out=sqt.ap[:, hi * 512:(hi + 1) * 512], in_=ps.ap, func=AF.Sqrt, bias=eps_b.ap[:, 0:1],
                        scale=1.0 / D), reads=[ps, eps_b], writes=[sqt])
                sch.op("dve", lambda e: e.reciprocal(out=rstd.ap, in_=sqt.ap), reads=[sqt], writes=[rstd])
                for kc in range(KD):
                    eng = "pool" if kc % 4 == 3 else "dve"
                    sch.op(eng, lambda e: e.tensor_tensor(out=hT[kc].ap, in0=hT[kc].ap, in1=rstd.ap, op=ALU.mult),
                           reads=[hT[kc], rstd], writes=[hT[kc]])
            jobs.append((extra_load, fin, False))
            return jobs

        def fm_tile_job(wring, w_src, pp_fn, rhs_bufs, nk, evac):
            st_ = {}

            def ld():
                w = wring.take()
                st_["w"] = w
                sch.dma(w.ap, w_src, w, True)

            def cp():
                w = st_["w"]
                pp = pp_fn()
                for hi in range(NH):
                    ps = pp[hi]
                    for k in range(nk):
                        last = (k == nk - 1)
                        rb, rap = rhs_bufs[k]
                        sch.op("pe", lambda e: e.matmul(
                            ps.ap, lhsT=w.ap[:, k * 128:(k + 1) * 128], rhs=rap[:, hi * 512:(hi + 1) * 512],
                            start=(k == 0), stop=last), reads=[w, rb], writes=[ps], signal=last)
                wring.done(w)
                evac(pp)
            return (ld, cp, True)

        _pcache = {}

        def bufs(name, fn):
            if name not in _pcache:
                ar.off = const_end
                _pcache[name] = fn()
            return _pcache[name]

        for l in range(L):
            phase_begin()
            def allocA():
                hT = [[ar.alloc([T], BF16, f"hT{i}_{k}") for k in range(KD)] for i in range(2)]
                xst = [ar.alloc([T], F32, f"xst{i}") for i in range(3)]
                sqr = [ar.alloc([T], BF16, f"sq{i}") for i in range(2)]
                rstd = ar.alloc([T], F32, "rstd")
                sqt = ar.alloc([T], F32, "sqt")
                wfm = [ar.alloc([TA], BF16, f"wfm{i}") for i in range(3)]
                wtm = [ar.alloc([TBc], BF16, f"wtm{i}") for i in range(2)]
                wvg = wtm
                assert NVT <= 2
                cosb = ar.alloc([T], F32, "cos")
                sinb = ar.alloc([T], F32, "sin")
                r1 = [ar.alloc([T], F32, "r1_0")] * 2
                r2 = [ar.alloc([T], F32, "r2_0")] * 2
                qko = [ar.alloc([T], BF16, f"qko{i}") for i in range(2)]
                vout = [ar.alloc([CWV], BF16, f"vout{i}") for i in range(3)]
                uT = [ar.alloc([T], BF16, f"uT{g}") for g in range(G)]
                vg = [ar.alloc([DG], F32, f"vg{i}") for i in range(2)]
                vn = [ar.alloc([DG], BF16, f"vn{i}") for i in range(2)]
                st = [ar.alloc([16], F32, f"st{i}") for i in range(2)]
                GT = uT
                gtmp = [ar.alloc([128], F32, f"gtmp{i}") for i in range(2)]
                mgo = [ar.alloc([T], BF16, f"mgo{i}") for i in range(2)]
                bsb = ar.alloc([G * 128], F32, "bsb")
                return dict(locals())
            nsA = bufs("A", allocA)
            hT, xst, sqr, rstd, sqt, wfm, wtm, wvg, cosb, sinb, r1, r2, qko, vout, uT, vg, vn, st, GT, gtmp, mgo, bsb = (
                nsA[k] for k in ("hT", "xst", "sqr", "rstd", "sqt", "wfm", "wtm", "wvg", "cosb", "sinb", "r1", "r2",
                                 "qko", "vout", "uT", "vg", "vn", "st", "GT", "gtmp", "mgo", "bsb"))
            sch.dma(bsb.ap, bs_in[:, l * G * 128:(l + 1) * G * 128], bsb, True)
            PA = [banks[0:2], banks[2:4]]
            PST = banks[4:6]
            PG = banks[6:8]
            xsrc = xT_in if l == 0 else xT
            xring = Ring(xst)
            wfm_r = Ring(wfm)
            wtm_r = Ring(wtm)
            pctr = {"pa": 0, "bi": 0}

            def next_pp():
                pp = PA[pctr["pa"] % 2]
                pctr["pa"] += 1
                return pp

            jobsA = []
            for s in range(NST):
                t0 = s * T

                def ld_cs(t0=t0):
                    sch.dma(cosb.ap, cos_in[:, t0:t0 + T], cosb, True)
                    sch.dma(sinb.ap, sin_in[:, t0:t0 + T], sinb, True)
                hTs = hT[s % 2]
                if s == 0:
                    jobsA += norm_jobs(l, 0, xsrc, g1_b, hTs, xring, sqr, PST, rstd, sqt, extra_load=ld_cs)
                if s + 1 < NST:
                    def ld_cs_n(t1=(s + 1) * T):
                        sch.dma(cosb.ap, cos_in[:, t1:t1 + T], cosb, True)
                        sch.dma(sinb.ap, sin_in[:, t1:t1 + T], sinb, True)
                    nxt = norm_jobs(l, s + 1, xsrc, g1_b, hT[(s + 1) % 2], xring, sqr, PST, rstd, sqt, extra_load=ld_cs_n)
                else:
                    nxt = []
                fmj = []
                hb = [(hTs[k], hTs[k].ap) for k in range(KD)]
                for j in range(2 * H):
                    def evac_qk(pp, j=j, t0=t0):
                        ra, rb, qo = r1[j % 2], r2[j % 2], qko[j % 2]
                        for hi in range(NH):
                            ps = pp[hi]
                            sl = slice(hi * 512, (hi + 1) * 512)
                            sch.op("dve", lambda e: e.tensor_tensor(
                                out=ra.ap[:, sl], in0=ps.ap, in1=cosb.ap[:, sl], op=ALU.mult),
                                reads=[ps, cosb], writes=[ra])
                            sch.op("dve", lambda e: e.tensor_tensor(
                                out=rb.ap[0:64, sl], in0=ps.ap[64:128, :], in1=sinb.ap[0:64, sl], op=ALU.mult),
                                reads=[ps, sinb], writes=[rb])
                            sch.op("dve", lambda e: e.tensor_tensor(
                                out=rb.ap[64:128, sl], in0=ps.ap[0:64, :], in1=sinb.ap[64:128, sl], op=ALU.mult),
                                reads=[ps, sinb], writes=[rb])
                        sch.op("pool", lambda e: e.tensor_tensor(
                            out=qo.ap, in0=ra.ap, in1=rb.ap, op=ALU.add), reads=[ra, rb], writes=[qo])
                        dst = qT if j < H else kT
                        hh = j % H
                        sch.dma(dst[hh * 128:(hh + 1) * 128, t0:t0 + T], qo.ap, qo, False)
                    fmj.append(fm_tile_job(wfm_r, wA[l * NA + j], next_pp, hb, KD, evac_qk))
                for g in range(G):
                    def evac_u(pp, g=g):
                        for hi in range(NH):
                            ps = pp[hi]
                            sch.op("act", lambda e: e.activation(
                                out=uT[g].ap[:, hi * 512:(hi + 1) * 512], in_=ps.ap, func=AF.Gelu),
                                reads=[ps], writes=[uT[g]])
                    fmj.append(fm_tile_job(wfm_r, wA[l * NA + 2 * H + g], next_pp, hb, KD, evac_u))
                assert len(fmj) >= KD
                for ji, jb in enumerate(fmj):
                    jobsA.append(jb)
                    if nxt and ji < KD:
                        jobsA.append(nxt[ji])
                jobsA += nxt[KD:]
                for nt in range(NVT):
                    stv = {}

                    def ld_v(nt=nt, stv=stv):
                        w = wtm_r.take()
                        stv["w"] = w
                        sch.dma(w.ap, wB[l * NB + nt], w, True)

                    def cp_v(nt=nt, stv=stv, t0=t0, hTs=hTs):
                        w = stv["w"]
                        for tt in range(T // 128):
                            ps = PA[pctr["bi"] % 2][0]
                            vo = vout[pctr["bi"] % 3]
                            pctr["bi"] += 1
                            for k in range(KD):
                                last = (k == KD - 1)
                                sch.op("pe", lambda e: e.matmul(
                                    ps.ap[:, 0:CWV], lhsT=hTs[k].ap[:, tt * 128:(tt + 1) * 128],
                                    rhs=w.ap[:, k * CWV:(k + 1) * CWV], start=(k == 0), stop=last),
                                    reads=[w, hTs[k]], writes=[ps], signal=last)
                            if tt % 2 == 0:
                                sch.op("act", lambda e: e.copy(out=vo.ap, in_=ps.ap[:, 0:CWV]), reads=[ps], writes=[vo])
                            else:
                                sch.op("dve", lambda e: e.tensor_copy(out=vo.ap, in_=ps.ap[:, 0:CWV]),
                                       reads=[ps], writes=[vo])
                            sch.dma(vS[t0 + tt * 128:t0 + (tt + 1) * 128, nt * CWV:(nt + 1) * CWV], vo.ap, vo, False)
                        wtm_r.done(w)
                    jobsA.append((ld_v, cp_v, True))
                stg_ = {}

                def ld_vg(stg_=stg_):
                    stg_["w"] = [wtm_r.take() for _ in range(NVT)]
                    for nt in range(NVT):
                        sch.dma(stg_["w"][nt].ap, wB[l * NB + NVT + nt], stg_["w"][nt], True)

                def cp_vg(stg_=stg_, t0=t0, hTs=hTs):
                    wv = stg_["w"]

                    def part_a(tt):
                        vgb, vnb, stb_ = vg[tt % 2], vn[tt % 2], st[tt % 2]
                        for nt in range(NVT):
                            ps = PA[pctr["bi"] % 2][1]
                            pctr["bi"] += 1
                            w = wv[nt]
                            for k in range(KD):
                                last = (k == KD - 1)
                                sch.op("pe", lambda e: e.matmul(
                                    ps.ap[:, 0:CWV], lhsT=hTs[k].ap[:, tt * 128:(tt + 1) * 128],
                                    rhs=w.ap[:, k * CWV:(k + 1) * CWV], start=(k == 0), stop=last),
                                    reads=[w, hTs[k]], writes=[ps], signal=last)
                            sch.op("act", lambda e: e.activation(
                                out=vgb.ap[:, nt * CWV:(nt + 1) * CWV], in_=ps.ap[:, 0:CWV], func=AF.Gelu,
                                accum_out=stb_.ap[:, nt:nt + 1]),
                                reads=[ps], writes=[vgb, stb_])
                        sch.op("act", lambda e: e.activation(
                            out=vnb.ap, in_=vgb.ap, func=AF.Square, accum_out=stb_.ap[:, 4:5]),
                            reads=[vgb], writes=[vnb, stb_])
                        if NVT == 2:
                            sch.op("dve", lambda e: e.tensor_tensor(
                                out=stb_.ap[:, 5:6], in0=stb_.ap[:, 0:1], in1=stb_.ap[:, 1:2], op=ALU.add),
                                reads=[stb_], writes=[stb_])
                        else:
                            sch.op("dve", lambda e: e.tensor_copy(out=stb_.ap[:, 5:6], in_=stb_.ap[:, 0:1]),
                                   reads=[stb_], writes=[stb_])
                        sch.op("dve", lambda e: e.tensor_scalar(
                            out=stb_.ap[:, 8:9], in0=stb_.ap[:, 5:6], scalar1=1.0 / DG, scalar2=None, op0=ALU.mult),
                            reads=[stb_], writes=[stb_])
                        sch.op("dve", lambda e: e.tensor_tensor(
                            out=stb_.ap[:, 6:7], in0=stb_.ap[:, 8:9], in1=stb_.ap[:, 8:9], op=ALU.mult),
                            reads=[stb_], writes=[stb_])
                        sch.op("dve", lambda e: e.scalar_tensor_tensor(
                            out=stb_.ap[:, 9:10], in0=stb_.ap[:, 4:5], scalar=1.0 / DG, in1=stb_.ap[:, 6:7],
                            op0=ALU.mult, op1=ALU.subtract),
                            reads=[stb_], writes=[stb_])
                        sch.op("act", lambda e: e.activation(
                            out=stb_.ap[:, 7:8], in_=stb_.ap[:, 9:10], func=AF.Sqrt, bias=eps_b.ap[:, 0:1], scale=1.0),
                            reads=[stb_, eps_b], writes=[stb_])
                        sch.op("dve", lambda e: e.reciprocal(out=stb_.ap[:, 10:11], in_=stb_.ap[:, 7:8]),
                               reads=[stb_], writes=[stb_])
                        sch.op("dve", lambda e: e.tensor_scalar(
                            out=vnb.ap, in0=vgb.ap, scalar1=stb_.ap[:, 8:9], scalar2=stb_.ap[:, 10:11],
                            op0=ALU.subtract, op1=ALU.mult),
                            reads=[vgb, stb_], writes=[vnb])

                    def part_b(tt):
                        vnb = vn[tt % 2]
                        for g in range(G):
                            ps = PG[(g // 4) % 2]
                            col = (g % 4) * 128
                            sch.op("pe", lambda e: e.matmul(
                                ps.ap[:, col:col + 128], lhsT=vnb.ap[:, g * 128:(g + 1) * 128],
                                rhs=wsT_b.ap[:, (l * G + g) * 128:(l * G + g + 1) * 128], start=True, stop=True),
                                reads=[vnb, wsT_b], writes=[ps])
                            gt = gtmp[g % 2]
                            sch.op("dve", lambda e: e.scalar_tensor_tensor(
                                out=gt.ap, in0=ps.ap[:, col:col + 128], scalar=lng_b.ap[:, l * G + g:l * G + g + 1],
                                in1=bsb.ap[:, g * 128:(g + 1) * 128], op0=ALU.mult, op1=ALU.add),
                                reads=[ps, lng_b, bsb], writes=[gt])
                            sch.op("pool", lambda e: e.tensor_tensor(
                                out=GT[g].ap[:, tt * 128:(tt + 1) * 128], in0=gt.ap,
                                in1=uT[g].ap[:, tt * 128:(tt + 1) * 128], op=ALU.mult),
                                reads=[gt, uT[g]], writes=[GT[g]])

                    ntt = T // 128
                    part_a(0)
                    for tt in range(ntt):
                        if tt + 1 < ntt:
                            part_a(tt + 1)
                        part_b(tt)
                    for w in wv:
                        wtm_r.done(w)
                    rms_stats(GT, sqr, PST, rstd, sqt, DG)
                    for g in range(G):
                        mo = mgo[g % 2]
                        sch.op("dve", lambda e: e.scalar_tensor_tensor(
                            out=mo.ap, in0=GT[g].ap, scalar=gg_b.ap[:, l * G + g:l * G + g + 1], in1=rstd.ap,
                            op0=ALU.mult, op1=ALU.mult),
                            reads=[GT[g], gg_b, rstd], writes=[mo])
                        sch.dma(mgT[g * 128:(g + 1) * 128, t0:t0 + T], mo.ap, mo, False)
                jobsA.append((ld_vg, cp_vg, False))
            run_jobs(jobsA, 2)

            phase_begin()
            def allocB1():
                qh = [ar.alloc([S], BF16, f"qh{i}") for i in range(2)]
                kh = [ar.alloc([S], BF16, f"kh{i}") for i in range(2)]
                Vr = [[ar.alloc([S // 128, 128], BF16, f"V{r}_{i}") for i in range(2)] for r in PATTERNS]
                num = ar.alloc([S], F32, "num")
                den = ar.alloc([S], F32, "den")
                oh = [ar.alloc([S], BF16, f"oh{i}") for i in range(2)]
                NPE, NPM = 4, 10
                pet = [ar.alloc([384], BF16, f"pet{i}") for i in range(NPE)]
                pmt = [ar.alloc([384], BF16, f"pmt{i}") for i in range(NPM)]
                cstg = [ar.alloc([PC], F32, f"cstg{i}") for i in range(5)]
                cstb = [ar.alloc([PC], BF16, f"cstb{i}") for i in range(3)]
                return dict(locals())
            nsB = bufs("B1", allocB1)
            qh, kh, Vr, num, den, oh, NPE, NPM, pet, pmt, cstg, cstb = (
                nsB[k] for k in ("qh", "kh", "Vr", "num", "den", "oh", "NPE", "NPM", "pet", "pmt", "cstg", "cstb"))
            cast_upto = max(cast_state["next"], cast_mark_R[l])
            cast_start = cast_state["next"]
            cast_n = cast_upto - cast_start
            PS_S = banks[0:3]
            PS_O = banks[3:5]
            PS_D = banks[5:7]
            LA = 3

            def load_head(h):
                i = h % 2
                sch.dma(qh[i].ap, qT[h * 128:(h + 1) * 128, :], qh[i], True)
                sch.dma(kh[i].ap, kT[h * 128:(h + 1) * 128, :], kh[i], True)
                for pi_, r in enumerate(PATTERNS):
                    vb = Vr[pi_][i]
                    nch = S // (128 * r)
                    if r == 1:
                        src = vS.rearrange("(c p) f -> p c f", p=128)[:, :, h * 128:(h + 1) * 128]
                        sch.dma(vb.ap, src, vb, True)
                    else:
                        for res in range(r):
                            src = vS.rearrange("(c p r) f -> p r c f", p=128, r=r)[:, res, :, h * 128:(h + 1) * 128]
                            sch.dma(vb.ap[:, res * nch:(res + 1) * nch, :], src, vb, True)

            load_head(0)
            if os.environ.get("DBG_BAR"):
                sch.barrier()
            sct = 0
            for h in range(H):
                i = h % 2
                if h + 1 < H and not os.environ.get("DBG_NOPF"):
                    load_head(h + 1)
                qb, kb = qh[i], kh[i]
                qk_jobs = []
                pv_jobs = []
                order = []
                for pi_, r in enumerate(PATTERNS):
                    nch = S // (128 * r)
                    nb = min(4, nch)
                    for res in range(r):
                        for cc in range(nch):
                            qk_jobs.append((pi_, r, res, cc, nch))
                            order.append(("qk", len(qk_jobs) - 1))
                            if cc >= 1:
                                pv_jobs.append((pi_, r, res, cc - 1, nch, nb))
                                order.append(("pv", len(pv_jobs) - 1))
                        pv_jobs.append((pi_, r, res, nch - 1, nch, nb))
                        order.append(("pv", len(pv_jobs) - 1))
                pm_of = {}
                qk_done = 0

                def emit_qk(idx):
                    nonlocal sct
                    pi_, r, res, cc, nch = qk_jobs[idx]
                    blo = max(0, cc - 1)
                    bhi = min(nch, cc + 2)
                    n = (bhi - blo) * 128
                    boff = (blo - (cc - 1)) * 128
                    ps = PS_S[sct % 3]
                    pe_t = pet[sct % NPE]
                    pm_t = pmt[sct % NPM]
                    sct += 1
                    k0 = 128 * cc * r + res
                    q0 = 128 * blo * r + res
                    sch.op("pe", lambda e, ps=ps, n=n, k0=k0, q0=q0, r=r: e.matmul(
                        ps.ap[:, 0:n], lhsT=kb.ap[:, k0:k0 + 127 * r + 1:r], rhs=qb.ap[:, q0:q0 + (n - 1) * r + 1:r],
                        start=True, stop=True), reads=[kb, qb], writes=[ps])
                    sch.op("act", lambda e, ps=ps, n=n, pe_t=pe_t: e.activation(
                        out=pe_t.ap[:, 0:n], in_=ps.ap[:, 0:n], func=AF.Exp, scale=scale_qk),
                        reads=[ps], writes=[pe_t])
                    sch.op("pool" if sct % 3 == 0 else "dve", lambda e, n=n, pe_t=pe_t, pm_t=pm_t, boff=boff: e.tensor_tensor(
                        out=pm_t.ap[:, 0:n], in0=pe_t.ap[:, 0:n], in1=band_b.ap[:, boff:boff + n], op=ALU.mult),
                        reads=[pe_t, band_b], writes=[pm_t])
                    pm_of[(pi_, res, cc)] = (pm_t, blo, sct)

                oct_ = [0]

                def emit_pv(idx):
                    pi_, r, res, b, nch, nb = pv_jobs[idx]
                    vb = Vr[pi_][i]
                    bslot = b % nb
                    if bslot == 0:
                        oct_[0] += 1
                    po = PS_O[oct_[0] % 2]
                    pd = PS_D[oct_[0] % 2]
                    cs = [cc for cc in (b - 1, b, b + 1) if 0 <= cc < nch]
                    for ci_, cc in enumerate(cs):
                        pm_t, blo, sct_at = pm_of[(pi_, res, cc)]
                        assert sct - sct_at < NPM - 1, "pm ring too small"
                        col = (b - blo) * 128
                        first = (ci_ == 0)
                        last = (ci_ == len(cs) - 1)
                        vidx = res * nch + cc if r > 1 else cc
                        sch.op("pe", lambda e, po=po, pm_t=pm_t, col=col, vidx=vidx, first=first, last=last, bslot=bslot, vb=vb: e.matmul(
                            po.ap[:, bslot * 128:(bslot + 1) * 128], lhsT=vb.ap[:, vidx, :], rhs=pm_t.ap[:, col:col + 128],
                            start=first, stop=last), reads=[vb, pm_t], writes=[po], signal=False)
                        sch.op("pe", lambda e, pd=pd, pm_t=pm_t, col=col, first=first, last=last, bslot=bslot: e.matmul(
                            pd.ap[:, bslot * 128:(bslot + 1) * 128], lhsT=ones_b.ap, rhs=pm_t.ap[:, col:col + 128],
                            start=first, stop=last), reads=[ones_b, pm_t], writes=[pd], signal=True)
                    if bslot == nb - 1:
                        b0 = b - (nb - 1)
                        w = nb * 128
                        if r == 1:
                            nv = num.ap[:, b0 * 128:b0 * 128 + w]
                            dv = den.ap[:, b0 * 128:b0 * 128 + w]
                        else:
                            nv = num.ap.rearrange("p (m r) -> p r m", r=r)[:, res, b0 * 128:b0 * 128 + w]
                            dv = den.ap.rearrange("p (m r) -> p r m", r=r)[:, res, b0 * 128:b0 * 128 + w]
                        if pi_ == 0:
                            sch.op("dve", lambda e, nv=nv, po=po, w=w: e.tensor_copy(out=nv, in_=po.ap[:, 0:w]),
                                   reads=[po], writes=[num])
                            sch.op("dve", lambda e, dv=dv, pd=pd, w=w: e.tensor_copy(out=dv, in_=pd.ap[:, 0:w]),
                                   reads=[pd], writes=[den])
                        else:
                            sch.op("dve", lambda e, nv=nv, po=po, w=w: e.tensor_tensor(
                                out=nv, in0=po.ap[:, 0:w], in1=nv, op=ALU.add), reads=[po, num], writes=[num])
                            sch.op("dve", lambda e, dv=dv, pd=pd, w=w: e.tensor_tensor(
                                out=dv, in0=pd.ap[:, 0:w], in1=dv, op=ALU.add), reads=[pd, den], writes=[den])

                qk_positions = [k for k, (kind, _) in enumerate(order) if kind == "qk"]
                for pos, (kind, idx) in enumerate(order):
                    if kind == "qk":
                        while qk_done <= min(idx + LA, len(qk_jobs) - 1):
                            emit_qk(qk_done)
                            qk_done += 1
                    else:
                        emit_pv(idx)
                    tgt = cast_start + (cast_n * (h * len(order) + pos + 1)) // (H * len(order))
                    emit_cast(min(tgt, cast_upto), cstg, cstb)
                ob = oh[i]
                sch.op("dve", lambda e: e.reciprocal(out=den.ap, in_=den.ap), reads=[den], writes=[den])
                sch.op("pool", lambda e, ob=ob: e.tensor_tensor(out=ob.ap, in0=num.ap, in1=den.ap, op=ALU.mult),
                       reads=[num, den], writes=[ob])
                sch.dma(aT[h * 128:(h + 1) * 128, :], ob.ap, ob, False)
                if h == H - 1:
                    emit_cast(cast_upto, cstg, cstb, flush=True)
                if h + 1 < H and os.environ.get("DBG_NOPF"):
                    load_head(h + 1)
                if debug_outputs and h == 0 and l == 0:
                    for nm, bb, dt in (("d_num", num, F32), ("d_rden", den, F32), ("d_qh", qb, BF16), ("d_kh", kb, BF16),
                                       ("d_V1", Vr[0][0], BF16), ("d_V4", Vr[min(1, len(Vr) - 1)][0], BF16), ("d_V16", Vr[-1][0], BF16)):
                        dd = nc.dram_tensor(nm, [128, S], dt, kind="ExternalOutput").ap()
                        src_ap = bb.ap if len(bb.ap.shape) == 2 else bb.ap.rearrange("p a b -> p (a b)")
                        sch.dma(dd, src_ap, bb, False)

            phase_begin()
            def allocB2():
                aall = ar.alloc([H, T], BF16, "aall")
                mall = ar.alloc([G, T], BF16, "mall")
                mixa = [ar.alloc([T], BF16, f"mx{h}") for h in range(H)]
                sqr = [ar.alloc([T], BF16, f"sq{i}") for i in range(2)]
                rstd = ar.alloc([T], F32, "rstd")
                sqt = ar.alloc([T], F32, "sqt")
                wfm = [ar.alloc([TA], BF16, f"wfm{i}") for i in range(3)]
                xold = [ar.alloc([T], F32, f"xo{i}") for i in range(3)]
                return dict(locals())
            nsC = bufs("B2", allocB2)
            aall, mall, mixa, sqr, rstd, sqt, wfm, xold = (
                nsC[k] for k in ("aall", "mall", "mixa", "sqr", "rstd", "sqt", "wfm", "xold"))
            PA = [banks[0:2], banks[2:4], banks[4:6]]
            PST = banks[6:8]
            xsrc = xT_in if l == 0 else xT
            wfm_r = Ring(wfm)
            xo_r = Ring(xold)
            pctr = {"pa": 0}

            def next_pp3():
                pp = PA[pctr["pa"] % 3]
                pctr["pa"] += 1
                return pp

            jobsB = []
            for s in range(NST):
                t0 = s * T

                def ld_am(t0=t0):
                    sch.dma(aall.ap, aT[:, t0:t0 + T].rearrange("(h p) t -> p h t", p=128), aall, True)
                    sch.dma(mall.ap, mgT[:, t0:t0 + T].rearrange("(g p) t -> p g t", p=128), mall, True)

                def cp_am():
                    rms_stats([(aall, aall.ap[:, h, :]) for h in range(H)], sqr, PST, rstd, sqt, DA)
                    for h in range(H):
                        sch.op("dve", lambda e: e.scalar_tensor_tensor(
                            out=mixa[h].ap, in0=aall.ap[:, h, :], scalar=ga_b.ap[:, l * H + h:l * H + h + 1],
                            in1=rstd.ap, op0=ALU.mult, op1=ALU.mult), reads=[aall, ga_b, rstd], writes=[mixa[h]])
                jobsB.append((ld_am, cp_am, False))
                mix = [(mixa[h], mixa[h].ap) for h in range(H)] + [(mall, mall.ap[:, g, :]) for g in range(G)]
                for j in range(KD):
                    stx = {}

                    def ev_o(pp, j=j, stx=stx, t0=t0):
                        xo = stx["xo"]
                        for hi in range(NH):
                            ps = pp[hi]
                            sl = slice(hi * 512, (hi + 1) * 512)
                            sch.op("dve", lambda e: e.tensor_tensor(
                                out=xo.ap[:, sl], in0=ps.ap, in1=xo.ap[:, sl], op=ALU.add), reads=[ps, xo], writes=[xo])
                        sch.dma(xT[j * 128:(j + 1) * 128, t0:t0 + T], xo.ap, xo, False)
                        xo_r.done(xo)
                    tj2 = fm_tile_job(wfm_r, wA[l * NA + 2 * H + G + j], next_pp3, mix, KD, ev_o)

                    def ld_o(j=j, stx=stx, tj2=tj2, t0=t0):
                        tj2[0]()
                        xo = xo_r.take()
                        stx["xo"] = xo
                        sch.dma(xo.ap, xsrc[j * 128:(j + 1) * 128, t0:t0 + T], xo, True)
                    jobsB.append((ld_o, tj2[1], True))
            run_jobs(jobsB, 1)

            phase_begin()
            def allocC():
                hT = [ar.alloc([T], BF16, f"h2T{k}") for k in range(KD)]
                ffT = [ar.alloc([T], BF16, f"ffT{f}") for f in range(KF)]
                xst = [ar.alloc([T], F32, f"xst{i}") for i in range(2)]
                xo2 = [ar.alloc([T], F32, f"xo{i}") for i in range(2)]
                sqr = [ar.alloc([T], BF16, f"sq{i}") for i in range(2)]
                sqt = ar.alloc([T], F32, "sqt")
                rstd = sqt
                cstgC = [ar.alloc([PC], F32, f"cstgC{i}") for i in range(4)]
                cstbC = [ar.alloc([PC], BF16, f"cstbC{i}") for i in range(2)]
                wgs = [ar.alloc([TA], BF16, f"wg{i}") for i in range(2)]
                wus = [ar.alloc([TA], BF16, f"wu{i}") for i in range(2)]
                wds = [ar.alloc([TC], BF16, f"wd{i}") for i in range(2)]
                sg = [ar.alloc([T], BF16, f"sg{i}") for i in range(2)]
                return dict(locals())
            nsD = bufs("C", allocC)
            hT, ffT, xst, xo2, sqr, rstd, sqt, wgs, wus, wds, sg, cstgC, cstbC = (
                nsD[k] for k in ("hT", "ffT", "xst", "xo2", "sqr", "rstd", "sqt", "wgs", "wus", "wds", "sg",
                                 "cstgC", "cstbC"))
            PGt = [banks[0:2], banks[2:4]]
            PUp = [banks[4:6], banks[6:8]]
            PD = [banks[0:2], banks[2:4], banks[4:6]]
            PST = banks[6:8]
            xring = Ring(xst)
            xo_r = Ring(xo2)
            wg_r, wu_r, wd_r = Ring(wgs), Ring(wus), Ring(wds)
            pctr = {"pd": 0}

            def next_pd():
                pp = PD[pctr["pd"] % 3]
                pctr["pd"] += 1
                return pp

            def gu_job(f):
                st_ = {}

                def ld():
                    st_["wg"] = wg_r.take()
                    st_["wu"] = wu_r.take()
                    sch.dma(st_["wg"].ap, wA[l * NA + 2 * H + G + KD + f], st_["wg"], True)
                    sch.dma(st_["wu"].ap, wA[l * NA + 2 * H + G + KD + KF + f], st_["wu"], True)

                def cp():
                    wg_, wu_ = st_["wg"], st_["wu"]
                    pg, pu = PGt[f % 2], PUp[f % 2]
                    for w, pp in ((wg_, pg), (wu_, pu)):
                        for hi in range(NH):
                            ps = pp[hi]
                            for k in range(KD):
                                last = (k == KD - 1)
                                sch.op("pe", lambda e: e.matmul(
                                    ps.ap, lhsT=w.ap[:, k * 128:(k + 1) * 128], rhs=hT[k].ap[:, hi * 512:(hi + 1) * 512],
                                    start=(k == 0), stop=last), reads=[w, hT[k]], writes=[ps], signal=last)
                    wg_r.done(wg_)
                    wu_r.done(wu_)
                    sgb = sg[f % 2]
                    for hi in range(NH):
                        sl = slice(hi * 512, (hi + 1) * 512)
                        sch.op("act", lambda e: e.activation(
                            out=sgb.ap[:, sl], in_=pg[hi].ap, func=AF.Silu), reads=[pg[hi]], writes=[sgb])
                        sch.op("dve", lambda e: e.tensor_tensor(
                            out=ffT[f].ap[:, sl], in0=pu[hi].ap, in1=sgb.ap[:, sl], op=ALU.mult),
                            reads=[pu[hi], sgb], writes=[ffT[f]])
                return (ld, cp, True)

            def down_job(j, t0):
                stx = {}
                fb = [(ffT[k], ffT[k].ap) for k in range(KF)]

                def ev(pp):
                    xo = stx["xo"]
                    for hi in range(NH):
                        ps = pp[hi]
                        sl = slice(hi * 512, (hi + 1) * 512)
                        sch.op("dve", lambda e: e.tensor_tensor(
                            out=xo.ap[:, sl], in0=ps.ap, in1=xo.ap[:, sl], op=ALU.add), reads=[ps, xo], writes=[xo])
                    sch.dma(xT[j * 128:(j + 1) * 128, t0:t0 + T], xo.ap, xo, False)
                    xo_r.done(xo)
                tj = fm_tile_job(wd_r, wC[l * KD + j], next_pd, fb, KF, ev)

                def ld():
                    tj[0]()
                    xo = xo_r.take()
                    stx["xo"] = xo
                    sch.dma(xo.ap, xT[j * 128:(j + 1) * 128, t0:t0 + T], xo, True)
                return (ld, tj[1], True)

            jobsC = []
            nj = [norm_jobs(l, s, xT, g2_b, hT, xring, sqr, PST, rstd, sqt) for s in range(NST)]
            jobsC += nj[0]
            for s in range(NST):
                t0 = s * T
                for f in range(KF):
                    jobsC.append(gu_job(f))
                for j in range(KD):
                    jobsC.append(down_job(j, t0))
                    if s + 1 < NST:
                        jobsC.append(nj[s + 1][j])
                if s + 1 < NST:
                    jobsC += nj[s + 1][KD:]
            if l + 1 < L:
                c_start = cast_state["next"]
                c_n = cast_mark_R[l + 1] - c_start
                njc = len(jobsC)

                def wrap(jb, i):
                    def cp():
                        jb[1]()
                        emit_cast(c_start + (c_n * (i + 1)) // njc, cstgC, cstbC, flush=(i == njc - 1))
                    return (jb[0], cp, jb[2])
                jobsC = [wrap(jb, i) for i, jb in enumerate(jobsC)]
            run_jobs(jobsC, 1)

        phase_begin()
        xall = ar.alloc([KD, T], F32, "xall")
        sqr = [ar.alloc([T], BF16, f"sq{i}") for i in range(2)]
        rstd = ar.alloc([T], F32, "rstd")
        sqt = ar.alloc([T], F32, "sqt")
        PST = banks[0:2]
        for s in range(NST):
            t0 = s * T
            sch.dma(xall.ap, xT[:, t0:t0 + T].rearrange("(k p) t -> p k t", p=128), xall, True)
            rms_stats([(xall, xall.ap[:, kc, :]) for kc in range(KD)], sqr, PST, rstd, sqt, D)
            for kc in range(KD):
                eng = "dve"
                sch.op(eng, lambda e, kc=kc: e.scalar_tensor_tensor(
                    out=xall.ap[:, kc, :], in0=xall.ap[:, kc, :], scalar=gf_b.ap[:, kc:kc + 1], in1=rstd.ap,
                    op0=ALU.mult, op1=ALU.mult), reads=[xall, gf_b, rstd], writes=[xall])
            sch.dma(outT[:, t0:t0 + T].rearrange("(k p) t -> p k t", p=128), xall.ap, xall, False)
        if debug_outputs:
            phase_begin()
            for nm, src, dt in (("d_xT", xT, F32), ("d_qT", qT, BF16), ("d_kT", kT, BF16), ("d_v", vS, BF16),
                                ("d_aT", aT, BF16), ("d_mgT", mgT, BF16)):
                rows, cols = src.shape
                dst = nc.dram_tensor(nm, [rows, cols], dt, kind="ExternalOutput").ap()
                ar.off = const_end
                dbuf = [ar.alloc([cols], dt, f"dbg{nm}{i}") for i in range(2)]
                for rr in range(rows // 128):
                    b = dbuf[rr % 2]
                    sch.dma(b.ap, src[rr * 128:(rr + 1) * 128, :], b, True)
                    sch.dma(dst[rr * 128:(rr + 1) * 128, :], b.ap, b, False)
                sch.barrier()
        sch.barrier()

        with nc.Block() as block:
            @block.sync
            def _(e):
                sch.replay("sp", e)

            @block.tensor
            def _(e):
                sch.replay("pe", e)

            @block.scalar
            def _(e):
                sch.replay("act", e)

            @block.vector
            def _(e):
                sch.replay("dve", e)

            @block.gpsimd
            def _(e):
                sch.replay("pool", e)
    return nc, {k: len(v) for k, v in sch.streams.items()}


def tile_w(W, cw):
    K, N = W.shape
    return np.ascontiguousarray(
        W.reshape(K // 128, 128, N // cw, cw).transpose(2, 1, 0, 3)).reshape(N // cw, 128, (K // 128) * cw)


def host_prep(cfg, p):
    c = derive(cfg)
    S, D, L = c["S"], c["D"], c["DEPTH"]
    DA, DG, H, G, KD, KF, CWV = c["DA"], c["DG"], c["H"], c["G"], c["KD"], c["KF"], c["CWV"]
    wA, wB, wC = [], [], []
    for l in range(L):
        w_in = p["w_in"][l]
        tiles = [tile_w(w_in[:, 0:2 * DA], 128),
                 tile_w(w_in[:, 3 * DA:3 * DA + DG], 128),
                 tile_w(p["w_out"][l], 128),
                 tile_w(p["w_gate"][l], 128),
                 tile_w(p["w_up"][l], 128)]
        wA.append(np.concatenate(tiles, 0))
        wB.append(np.concatenate([tile_w(w_in[:, 2 * DA:3 * DA], CWV),
                                  tile_w(w_in[:, 3 * DA + DG:3 * DA + 2 * DG], CWV)], 0))
        wC.append(tile_w(p["w_down"][l], 128))
    shared = {
        "wA": np.concatenate(wA, 0), "wB": np.concatenate(wB, 0), "wC": np.concatenate(wC, 0),
    }

    def pp(v, n):
        return np.ascontiguousarray(v.reshape(L, n, 128).transpose(2, 0, 1)).reshape(128, L * n)

    shared["g1"] = pp(p["norm1_g"], KD)
    shared["g2"] = pp(p["norm2_g"], KD)
    shared["ga"] = pp(p["mix_norm_attn_g"], H)
    shared["gg"] = pp(p["mix_norm_gmlp_g"], G)
    shared["lng"] = pp(p["gmlp_ln_g"], G)
    shared["gf"] = np.ascontiguousarray(p["final_g"].reshape(KD, 128).T)
    shared["wsT"] = np.ascontiguousarray(p["w_spatial"].transpose(3, 0, 1, 2)).reshape(128, L * G * 128)
    shared["bs"] = np.ascontiguousarray(np.broadcast_to(p["b_spatial"].reshape(1, L * G * 128), (128, L * G * 128)))
    pos = np.arange(S, dtype=np.float32)
    inv = (np.float32(10000.0) ** (-np.arange(0, 128, 2, dtype=np.float32) / np.float32(128))).astype(np.float32)
    ang = (pos[None, :] * inv[:, None]).astype(np.float32)
    cosv = np.cos(ang).astype(np.float32)
    sinv = np.sin(ang).astype(np.float32)
    shared["cosT"] = np.ascontiguousarray(np.concatenate([cosv, cosv], 0))
    shared["sinS"] = np.ascontiguousarray(np.concatenate([-sinv, sinv], 0))
    kk = np.arange(128)[:, None]
    jj = np.arange(384)[None, :]
    shared["band"] = (np.abs(kk + 128 - jj) <= 64).astype(np.float32)
    return shared


FULL_CFG = dict(S=4096, D=2048, DEPTH=4, T=1024)
_CACHE = {}


def run(cfg, inputs, n_cores):
    p = {k: np.asarray(v, dtype=np.float32) for k, v in inputs.items()}
    shared = host_prep(cfg, p)
    key = tuple(sorted(cfg.items()))
    if key not in _CACHE:
        _CACHE[key] = build(cfg)[0]
    nc = _CACHE[key]
    x = p["x"]
    in_maps = []
    for b in range(n_cores):
        m = dict(shared)
        m["xT"] = np.ascontiguousarray(x[b].T)
        in_maps.append(m)
    res = run_bass_kernel_spmd(nc, in_maps, core_ids=list(range(n_cores)))
    out = np.stack([np.ascontiguousarray(res.results[b]["outT"].T) for b in range(n_cores)], 0)
    return out.astype(np.float32)


def kernel(**inputs):
    return run(FULL_CFG, inputs, 8)
```

```python
import bisect
import math
from contextlib import ExitStack

import numpy as np
import ml_dtypes

import concourse.bass as bass
import concourse.mybir as mybir
from concourse.bass_utils import run_bass_kernel_spmd

F32 = mybir.dt.float32
BF16 = mybir.dt.bfloat16
ALU = mybir.AluOpType
AF = mybir.ActivationFunctionType

EPS = 1e-6
import os
PATTERNS = tuple(int(v) for v in os.environ.get("DBG_PAT", "1,4,16").split(","))
ENGS = ("pe", "act", "dve", "pool", "sp")
SEM_LIMIT = 30000
STORE_Q = "pool"


def derive(cfg):
    c = dict(cfg)
    D = c["D"]
    c["DA"] = D // 2
    c["DG"] = D - c["DA"]
    c["H"] = c["DA"] // 128
    c["G"] = c["DG"] // 128
    c["DFF"] = -(-(8 * D) // (3 * 256)) * 256
    c["KD"] = D // 128
    c["KF"] = c["DFF"] // 128
    c["CWV"] = min(512, c["DA"])
    c["NVT"] = c["DA"] // c["CWV"]
    c["NA"] = 2 * c["H"] + c["G"] + c["KD"] + 2 * c["KF"]
    c["NB"] = 2 * c["NVT"]
    return c


class Buf:
    __slots__ = ("ap", "w", "r", "rd", "dsem", "dval", "name")

    def __init__(self, ap, name=""):
        self.ap = ap
        self.w = None
        self.r = {}
        self.rd = []
        self.dsem = None
        self.dval = 0
        self.name = name


class _Rec:
    def __init__(self):
        self.call = None

    def __getattr__(self, name):
        def f(*a, **k):
            assert self.call is None
            self.call = (name, a, k)
            return self
        return f


class Sched:
    def __init__(self, sem_pool):
        self.pool = list(sem_pool)
        self.streams = {e: [] for e in ENGS}
        self.engsem = {}
        self.sig_seq = {e: [] for e in ENGS}
        self.sig_tok = {e: [] for e in ENGS}
        self.seq = {e: 0 for e in ENGS}
        self.seen = {e: {} for e in ENGS}
        self.dsems = []
        for e in ("pe", "act", "dve", "pool"):
            self.engsem[e] = [self.pool.pop(), 0]

    def _resolve(self, tok):
        if tok[0] == "d":
            return tok[1], tok[2]
        _, eng, seq = tok
        i = bisect.bisect_left(self.sig_seq[eng], seq)
        assert i < len(self.sig_seq[eng]), "dependency on unsignaled op"
        return self.sig_tok[eng][i]

    def _wait(self, eng, tok):
        sem, val = self._resolve(tok)
        key = id(sem)
        if self.seen[eng].get(key, 0) >= val:
            return
        self.seen[eng][key] = val
        self.streams[eng].append(("w", sem, val))

    def _deps(self, eng, reads, writes):
        for b in reads:
            if b.w is not None:
                self._dep(eng, b.w, "raw")
        for b in writes:
            if b.w is not None:
                self._dep(eng, b.w, "waw")
            for t in b.r.values():
                self._dep(eng, t, "war")
            for t in b.rd:
                self._dep(eng, t, "war")

    def _dep(self, eng, tok, kind):
        if tok[0] == "e" and tok[1] == eng:
            if eng == "pe" or kind == "war":
                return
        self._wait(eng, tok)

    def op(self, eng, fn, reads=(), writes=(), signal=True):
        self._deps(eng, reads, writes)
        seq = self.seq[eng]
        self.seq[eng] += 1
        tok = ("e", eng, seq)
        sem = None
        if signal:
            es = self.engsem[eng]
            if es[1] >= SEM_LIMIT:
                es[0] = self.pool.pop()
                es[1] = 0
            es[1] += 1
            sem = es[0]
            self.sig_seq[eng].append(seq)
            self.sig_tok[eng].append((sem, es[1]))
        rec = _Rec()
        fn(rec)
        assert rec.call is not None
        self.streams[eng].append(("o", rec.call, sem, 1))
        for b in reads:
            b.r[eng] = tok
        for b in writes:
            b.w = tok
            b.r = {}
            b.rd = []
        return tok

    def dma(self, out_ap, in_ap, sb, load, extra_reads=(), extra_writes=()):
        eng = "sp" if load else STORE_Q
        if load:
            if sb.w is not None and sb.w[0] == "d" and sb.dsem is not None and sb.w[1] is sb.dsem \
                    and not sb.r and not sb.rd:
                pass
            else:
                self._deps(eng, extra_reads, [sb] + list(extra_writes))
        else:
            self._deps(eng, [sb] + list(extra_reads), extra_writes)
        if sb.dsem is None:
            sb.dsem = self.pool.pop()
            self.dsems.append(sb)
        sb.dval += 16
        assert sb.dval < 2 * SEM_LIMIT
        tok = ("d", sb.dsem, sb.dval)
        self.streams[eng].append(("d", out_ap, in_ap, sb.dsem))
        if load:
            sb.w = tok
            sb.r = {}
            sb.rd = []
        else:
            sb.rd.append(tok)
        return tok

    def fence_dma(self, engs=("sp",)):
        for e in engs:
            for b in self.dsems:
                if b.dval:
                    self._wait(e, ("d", b.dsem, b.dval))

    def barrier(self):
        for e in ENGS:
            for o in ("pe", "act", "dve", "pool"):
                if o != e and self.sig_tok[o]:
                    sem, val = self.sig_tok[o][-1]
                    assert self.sig_seq[o][-1] == self.seq[o] - 1, "last op must signal"
                    self._wait(e, ("d", sem, val))
        self.fence_dma(ENGS)

    def replay(self, eng, e):
        for it in self.streams[eng]:
            if it[0] == "w":
                e.wait_ge(it[1], it[2])
            elif it[0] == "o":
                name, a, k = it[1]
                ins = getattr(e, name)(*a, **k)
                if it[2] is not None:
                    ins.then_inc(it[2], 1)
            else:
                e.dma_start(out=it[1], in_=it[2]).then_inc(it[3], 16)


class Arena:
    def __init__(self, t, nwords):
        self.t = t
        self.n = nwords
        self.off = 0

    def reset(self):
        self.off = 0

    def alloc(self, free_shape, dtype, name=""):
        n = int(np.prod(free_shape))
        esz = 4 if dtype == F32 else 2
        nwords = (n * esz + 3) // 4
        nwords = (nwords + 7) // 8 * 8
        assert self.off + nwords <= self.n, f"arena overflow at {name}: {self.off + nwords} > {self.n}"
        ap = self.t[:, self.off:self.off + nwords]
        if dtype != F32:
            ap = ap.bitcast(dtype)
        ap = ap[:, 0:n]
        if len(free_shape) == 2:
            ap = ap.rearrange("p (a b) -> p a b", a=free_shape[0])
        elif len(free_shape) == 3:
            ap = ap.rearrange("p (a b c) -> p a b c", a=free_shape[0], b=free_shape[1])
        self.off += nwords
        return Buf(ap, name)


def build(cfg, debug_outputs=False):
    c = derive(cfg)
    S, D, L, T = c["S"], c["D"], c["DEPTH"], c["T"]
    DA, DG, H, G, DFF, KD, KF = c["DA"], c["DG"], c["H"], c["G"], c["DFF"], c["KD"], c["KF"]
    CWV, NVT, NA, NB = c["CWV"], c["NVT"], c["NA"], c["NB"]
    NST = S // T
    NH = T // 512
    assert T % 512 == 0 and S % T == 0
    TA = KD * 128
    TBc = KD * CWV
    TC = KF * 128
    scale_qk = 1.0 / math.sqrt(128.0)

    nc = bass.Bass("TRN2", target_bir_lowering=False)

    def din(name, shape, dt=F32):
        return nc.dram_tensor(name, list(shape), dt, kind="ExternalInput").ap()

    xT_in = din("xT", [D, S])
    wA_in = din("wA", [L * NA, 128, TA])
    wB_in = din("wB", [L * NB, 128, TBc])
    wC_in = din("wC", [L * KD, 128, TC])
    wsT_in = din("wsT", [128, L * G * 128])
    g1_in = din("g1", [128, L * KD])
    g2_in = din("g2", [128, L * KD])
    ga_in = din("ga", [128, L * H])
    gg_in = din("gg", [128, L * G])
    lng_in = din("lng", [128, L * G])
    bs_in = din("bs", [128, L * G * 128])
    gf_in = din("gf", [128, KD])
    cos_in = din("cosT", [128, S])
    sin_in = din("sinS", [128, S])
    band_in = din("band", [128, 384])
    outT = nc.dram_tensor("outT", [D, S], F32, kind="ExternalOutput").ap()

    dk = {}
    xT = nc.dram_tensor("xT_s", [D, S], F32, **dk).ap()
    wA = nc.dram_tensor("wA_b", [L * NA, 128, TA], BF16).ap()
    wB = nc.dram_tensor("wB_b", [L * NB, 128, TBc], BF16).ap()
    wC = nc.dram_tensor("wC_b", [L * KD, 128, TC], BF16).ap()
    qT = nc.dram_tensor("qT_s", [DA, S], BF16, **dk).ap()
    kT = nc.dram_tensor("kT_s", [DA, S], BF16, **dk).ap()
    vS = nc.dram_tensor("v_s", [S, DA], BF16, **dk).ap()
    aT = nc.dram_tensor("aT_s", [DA, S], BF16, **dk).ap()
    mgT = nc.dram_tensor("mgT_s", [DG, S], BF16, **dk).ap()

    es = ExitStack()
    with es:
        sems = [es.enter_context(nc.semaphore(f"s{i}")) for i in range(96)]
        ARENA_WORDS = (nc.SBUF_PARTITION_SIZE_BYTES - 16384 - 512) // 4
        arena_t = es.enter_context(nc.sbuf_tensor("arena", [128, ARENA_WORDS], F32))
        banks = [Buf(es.enter_context(nc.psum_tensor(f"ps{i}", [128, 512], F32))[:, :], f"ps{i}")
                 for i in range(8)]
        sch = Sched(sems)
        ar = Arena(arena_t, ARENA_WORDS)

        ones_b = ar.alloc([128], BF16, "ones")
        band_b = ar.alloc([384], BF16, "band")
        band_f = ar.alloc([384], F32, "bandf")
        g1_b = ar.alloc([L * KD], F32, "g1")
        g2_b = ar.alloc([L * KD], F32, "g2")
        ga_b = ar.alloc([L * H], F32, "ga")
        gg_b = ar.alloc([L * G], F32, "gg")
        lng_b = ar.alloc([L * G], F32, "lng")
        gf_b = ar.alloc([KD], F32, "gf")
        wsT_b = ar.alloc([L * G * 128], BF16, "wsT")
        eps_b = ar.alloc([8], F32, "eps")
        const_end = ar.off

        def phase_begin():
            sch.barrier()
            ar.off = const_end

        ar.off = const_end
        tmpc = ar.alloc([L * G * 128], F32, "tmpc")
        for dst, src in ((g1_b, g1_in), (g2_b, g2_in), (ga_b, ga_in), (gg_b, gg_in),
                         (lng_b, lng_in), (gf_b, gf_in), (band_f, band_in), (tmpc, wsT_in)):
            sch.dma(dst.ap, src, dst, True)
        sch.op("dve", lambda e: e.memset(ones_b.ap, 1.0), writes=[ones_b])
        sch.op("dve", lambda e: e.memset(eps_b.ap, EPS), writes=[eps_b])
        sch.op("dve", lambda e: e.tensor_copy(out=band_b.ap, in_=band_f.ap), reads=[band_f], writes=[band_b])
        sch.op("dve", lambda e: e.tensor_copy(out=wsT_b.ap, in_=tmpc.ap), reads=[tmpc], writes=[wsT_b])

        PC = 512
        cast_pieces = []
        cast_mark_A = []
        cast_mark_R = []

        def add_pieces(src, dst, t_lo, t_hi, tcols):
            for t in range(t_lo, t_hi):
                for c0 in range(0, tcols, PC):
                    cw = min(PC, tcols - c0)
                    cast_pieces.append((src[t, :, c0:c0 + cw], dst[t, :, c0:c0 + cw], cw))

        for l in range(L):
            add_pieces(wA_in, wA, l * NA, l * NA + 2 * H + G, TA)
            add_pieces(wB_in, wB, l * NB, (l + 1) * NB, TBc)
            cast_mark_A.append(len(cast_pieces))
            add_pieces(wA_in, wA, l * NA + 2 * H + G, (l + 1) * NA, TA)
            add_pieces(wC_in, wC, l * KD, (l + 1) * KD, TC)
            cast_mark_R.append(len(cast_pieces))
        cast_state = {"next": 0, "n": 0, "m": 0}
        cast_fl = []
        cast_engs = ("dve", "act")

        def emit_cast(upto, stg, stb, flush=False):
            def issue():
                src, dst, cw = cast_pieces[cast_state["next"]]
                cast_state["next"] += 1
                a = stg[cast_state["n"] % len(stg)]
                cast_state["n"] += 1
                sch.dma(a.ap[:, 0:cw], src, a, True)
                cast_fl.append((a, dst, cw))

            def finish():
                a, dst, cw = cast_fl.pop(0)
                b = stb[cast_state["m"] % len(stb)]
                ce = cast_engs[cast_state["m"] % 2]
                cast_state["m"] += 1
                if ce == "act":
                    sch.op("act", lambda e: e.copy(out=b.ap[:, 0:cw], in_=a.ap[:, 0:cw]), reads=[a], writes=[b])
                else:
                    sch.op(ce, lambda e: e.tensor_copy(out=b.ap[:, 0:cw], in_=a.ap[:, 0:cw]), reads=[a], writes=[b])
                sch.dma(dst, b.ap[:, 0:cw], b, False)

            while cast_state["next"] < upto:
                issue()
                if len(cast_fl) > len(stg) - 2:
                    finish()
            if flush:
                while cast_fl:
                    finish()

        phase_begin()
        stg0 = [ar.alloc([PC], F32, f"stg{i}") for i in range(5)]
        stb0 = [ar.alloc([PC], BF16, f"stb{i}") for i in range(4)]
        emit_cast(cast_mark_A[0], stg0, stb0, flush=True)

        def mm_group(ps_list, lhs_fn, rhs_fn, nk, reads):
            for hi, ps in enumerate(ps_list):
                for k in range(nk):
                    last = (k == nk - 1)
                    sch.op("pe",
                           lambda e, ps=ps, k=k, hi=hi, last=last: e.matmul(
                               ps.ap, lhsT=lhs_fn(k), rhs=rhs_fn(k, hi),
                               start=(k == 0), stop=last),
                           reads=reads(k), writes=[ps], signal=last)

        def rms_stats(chunk_bufs, sq_ring, stat_banks, rstd, sqt, dim):
            n = len(chunk_bufs)
            for k, cb in enumerate(chunk_bufs):
                sq = sq_ring[k % len(sq_ring)]
                cbuf, cap = cb if isinstance(cb, tuple) else (cb, cb.ap)
                sch.op("act", lambda e, cap=cap, sq=sq: e.activation(out=sq.ap, in_=cap, func=AF.Square),
                       reads=[cbuf], writes=[sq])
                for hi in range(NH):
                    ps = stat_banks[hi]
                    last = (k == n - 1)
                    sch.op("pe", lambda e, ps=ps, sq=sq, hi=hi, k=k, last=last: e.matmul(
                        ps.ap, lhsT=ones_b.ap, rhs=sq.ap[:, hi * 512:(hi + 1) * 512], start=(k == 0), stop=last),
                        reads=[sq, ones_b], writes=[ps], signal=(last or hi == NH - 1))
            for hi in range(NH):
                ps = stat_banks[hi]
                sch.op("act", lambda e, ps=ps, hi=hi: e.activation(
                    out=sqt.ap[:, hi * 512:(hi + 1) * 512], in_=ps.ap, func=AF.Sqrt, bias=eps_b.ap[:, 0:1], scale=1.0 / dim),
                    reads=[ps, eps_b], writes=[sqt])
            sch.op("dve", lambda e: e.reciprocal(out=rstd.ap, in_=sqt.ap), reads=[sqt], writes=[rstd])


        class Ring:
            def __init__(self, bl):
                self.b = bl
                self.i = 0
                self.pend = set()

            def take(self):
                b = self.b[self.i % len(self.b)]
                self.i += 1
                assert id(b) not in self.pend, "ring slot reused before its consumer was emitted"
                self.pend.add(id(b))
                return b

            def done(self, b):
                self.pend.discard(id(b))

        def run_jobs(jobs, d=1):
            n = len(jobs)
            loaded = [False] * n
            for i in range(n):
                if not loaded[i]:
                    if jobs[i][0] is not None:
                        jobs[i][0]()
                    loaded[i] = True
                for k in range(i + 1, min(n, i + 1 + d)):
                    if loaded[k]:
                        continue
                    if not jobs[k][2]:
                        break
                    if jobs[k][0] is not None:
                        jobs[k][0]()
                    loaded[k] = True
                jobs[i][1]()

        def norm_jobs(l, s, xsrc, g_b, hT, xring, sq_ring, stat_banks, rstd, sqt, extra_load=None):
            t0 = s * T
            jobs = []
            for kc in range(KD):
                st_ = {}

                def ld(kc=kc, st_=st_):
                    xb = xring.take()
                    st_["xb"] = xb
                    sch.dma(xb.ap, xsrc[kc * 128:(kc + 1) * 128, t0:t0 + T], xb, True)

                def cp(kc=kc, st_=st_):
                    xb = st_["xb"]
                    sq = sq_ring[kc % len(sq_ring)]
                    sch.op("act", lambda e: e.activation(out=sq.ap, in_=xb.ap, func=AF.Square),
                           reads=[xb], writes=[sq])
                    for hi in range(NH):
                        ps = stat_banks[hi]
                        last = (kc == KD - 1)
                        sch.op("pe", lambda e: e.matmul(
                            ps.ap, lhsT=ones_b.ap, rhs=sq.ap[:, hi * 512:(hi + 1) * 512], start=(kc == 0), stop=last),
                            reads=[sq, ones_b], writes=[ps], signal=(last or hi == NH - 1))
                    sch.op("dve", lambda e: e.tensor_scalar(
                        out=hT[kc].ap, in0=xb.ap, scalar1=g_b.ap[:, l * KD + kc:l * KD + kc + 1], scalar2=1.0,
                        op0=ALU.mult, op1=ALU.mult), reads=[xb, g_b], writes=[hT[kc]])
                    xring.done(xb)
                jobs.append((ld, cp, True))

            def fin():
                for hi in range(NH):
                    ps = stat_banks[hi]
                    sch.op("act", lambda e: e.activation(
                        out=sqt.ap[:, hi * 512:(hi + 1) * 512], in_=ps.ap, func=AF.Sqrt, bias=eps_b.ap[:, 0:1],
                        scale=1.0 / D), reads=[ps, eps_b], writes=[sqt])
                sch.op("dve", lambda e: e.reciprocal(out=rstd.ap, in_=sqt.ap), reads=[sqt], writes=[rstd])
                for kc in range(KD):
                    eng = "pool" if kc % 4 == 3 else "dve"
                    sch.op(eng, lambda e: e.tensor_tensor(out=hT[kc].ap, in0=hT[kc].ap, in1=rstd.ap, op=ALU.mult),
                           reads=[hT[kc], rstd], writes=[hT[kc]])
            jobs.append((extra_load, fin, False))
            return jobs

        def fm_tile_job(wring, w_src, pp_fn, rhs_bufs, nk, evac):
            st_ = {}

            def ld():
                w = wring.take()
                st_["w"] = w
                sch.dma(w.ap, w_src, w, True)

            def cp():
                w = st_["w"]
                pp = pp_fn()
                for hi in range(NH):
                    ps = pp[hi]
                    for k in range(nk):
                        last = (k == nk - 1)
                        rb, rap = rhs_bufs[k]
                        sch.op("pe", lambda e: e.matmul(
                            ps.ap, lhsT=w.ap[:, k * 128:(k + 1) * 128], rhs=rap[:, hi * 512:(hi + 1) * 512],
                            start=(k == 0), stop=last), reads=[w, rb], writes=[ps], signal=last)
                wring.done(w)
                evac(pp)
            return (ld, cp, True)

        _pcache = {}

        def bufs(name, fn):
            if name not in _pcache:
                ar.off = const_end
                _pcache[name] = fn()
            return _pcache[name]

        for l in range(L):
            phase_begin()
            def allocA():
                hT = [[ar.alloc([T], BF16, f"hT{i}_{k}") for k in range(KD)] for i in range(2)]
                xst = [ar.alloc([T], F32, f"xst{i}") for i in range(3)]
                sqr = [ar.alloc([T], BF16, f"sq{i}") for i in range(2)]
                rstd = ar.alloc([T], F32, "rstd")
                sqt = ar.alloc([T], F32, "sqt")
                wfm = [ar.alloc([TA], BF16, f"wfm{i}") for i in range(3)]
                wtm = [ar.alloc([TBc], BF16, f"wtm{i}") for i in range(2)]
                wvg = wtm
                assert NVT <= 2
                cosb = ar.alloc([T], F32, "cos")
                sinb = ar.alloc([T], F32, "sin")
                r1 = [ar.alloc([T], F32, "r1_0")] * 2
                r2 = [ar.alloc([T], F32, "r2_0")] * 2
                qko = [ar.alloc([T], BF16, f"qko{i}") for i in range(2)]
                vout = [ar.alloc([CWV], BF16, f"vout{i}") for i in range(3)]
                uT = [ar.alloc([T], BF16, f"uT{g}") for g in range(G)]
                vg = [ar.alloc([DG], F32, f"vg{i}") for i in range(2)]
                vn = [ar.alloc([DG], BF16, f"vn{i}") for i in range(2)]
                st = [ar.alloc([16], F32, f"st{i}") for i in range(2)]
                GT = uT
                gtmp = [ar.alloc([128], F32, f"gtmp{i}") for i in range(2)]
                mgo = [ar.alloc([T], BF16, f"mgo{i}") for i in range(2)]
                bsb = ar.alloc([G * 128], F32, "bsb")
                return dict(locals())
            nsA = bufs("A", allocA)
            hT, xst, sqr, rstd, sqt, wfm, wtm, wvg, cosb, sinb, r1, r2, qko, vout, uT, vg, vn, st, GT, gtmp, mgo, bsb = (
                nsA[k] for k in ("hT", "xst", "sqr", "rstd", "sqt", "wfm", "wtm", "wvg", "cosb", "sinb", "r1", "r2",
                                 "qko", "vout", "uT", "vg", "vn", "st", "GT", "gtmp", "mgo", "bsb"))
            sch.dma(bsb.ap, bs_in[:, l * G * 128:(l + 1) * G * 128], bsb, True)
            PA = [banks[0:2], banks[2:4]]
            PST = banks[4:6]
            PG = banks[6:8]
            xsrc = xT_in if l == 0 else xT
            xring = Ring(xst)
            wfm_r = Ring(wfm)
            wtm_r = Ring(wtm)
            pctr = {"pa": 0, "bi": 0}

            def next_pp():
                pp = PA[pctr["pa"] % 2]
                pctr["pa"] += 1
                return pp

            jobsA = []
            for s in range(NST):
                t0 = s * T

                def ld_cs(t0=t0):
                    sch.dma(cosb.ap, cos_in[:, t0:t0 + T], cosb, True)
                    sch.dma(sinb.ap, sin_in[:, t0:t0 + T], sinb, True)
                hTs = hT[s % 2]
                if s == 0:
                    jobsA += norm_jobs(l, 0, xsrc, g1_b, hTs, xring, sqr, PST, rstd, sqt, extra_load=ld_cs)
                if s + 1 < NST:
                    def ld_cs_n(t1=(s + 1) * T):
                        sch.dma(cosb.ap, cos_in[:, t1:t1 + T], cosb, True)
                        sch.dma(sinb.ap, sin_in[:, t1:t1 + T], sinb, True)
                    nxt = norm_jobs(l, s + 1, xsrc, g1_b, hT[(s + 1) % 2], xring, sqr, PST, rstd, sqt, extra_load=ld_cs_n)
                else:
                    nxt = []
                fmj = []
                hb = [(hTs[k], hTs[k].ap) for k in range(KD)]
                for j in range(2 * H):
                    def evac_qk(pp, j=j, t0=t0):
                        ra, rb, qo = r1[j % 2], r2[j % 2], qko[j % 2]
                        for hi in range(NH):
                            ps = pp[hi]
                            sl = slice(hi * 512, (hi + 1) * 512)
                            sch.op("dve", lambda e: e.tensor_tensor(
                                out=ra.ap[:, sl], in0=ps.ap, in1=cosb.ap[:, sl], op=ALU.mult),
                                reads=[ps, cosb], writes=[ra])
                            sch.op("dve", lambda e: e.tensor_tensor(
                                out=rb.ap[0:64, sl], in0=ps.ap[64:128, :], in1=sinb.ap[0:64, sl], op=ALU.mult),
                                reads=[ps, sinb], writes=[rb])
                            sch.op("dve", lambda e: e.tensor_tensor(
                                out=rb.ap[64:128, sl], in0=ps.ap[0:64, :], in1=sinb.ap[64:128, sl], op=ALU.mult),
                                reads=[ps, sinb], writes=[rb])
                        sch.op("pool", lambda e: e.tensor_tensor(
                            out=qo.ap, in0=ra.ap, in1=rb.ap, op=ALU.add), reads=[ra, rb], writes=[qo])
                        dst = qT if j < H else kT
                        hh = j % H
                        sch.dma(dst[hh * 128:(hh + 1) * 128, t0:t0 + T], qo.ap, qo, False)
                    fmj.append(fm_tile_job(wfm_r, wA[l * NA + j], next_pp, hb, KD, evac_qk))
                for g in range(G):
                    def evac_u(pp, g=g):
                        for hi in range(NH):
                            ps = pp[hi]
                            sch.op("act", lambda e: e.activation(
                                out=uT[g].ap[:, hi * 512:(hi + 1) * 512], in_=ps.ap, func=AF.Gelu),
                                reads=[ps], writes=[uT[g]])
                    fmj.append(fm_tile_job(wfm_r, wA[l * NA + 2 * H + g], next_pp, hb, KD, evac_u))
                assert len(fmj) >= KD
                for ji, jb in enumerate(fmj):
                    jobsA.append(jb)
                    if nxt and ji < KD:
                        jobsA.append(nxt[ji])
                jobsA += nxt[KD:]
                for nt in range(NVT):
                    stv = {}

                    def ld_v(nt=nt, stv=stv):
                        w = wtm_r.take()
                        stv["w"] = w
                        sch.dma(w.ap, wB[l * NB + nt], w, True)

                    def cp_v(nt=nt, stv=stv, t0=t0, hTs=hTs):
                        w = stv["w"]
                        for tt in range(T // 128):
                            ps = PA[pctr["bi"] % 2][0]
                            vo = vout[pctr["bi"] % 3]
                            pctr["bi"] += 1
                            for k in range(KD):
                                last = (k == KD - 1)
                                sch.op("pe", lambda e: e.matmul(
                                    ps.ap[:, 0:CWV], lhsT=hTs[k].ap[:, tt * 128:(tt + 1) * 128],
                                    rhs=w.ap[:, k * CWV:(k + 1) * CWV], start=(k == 0), stop=last),
                                    reads=[w, hTs[k]], writes=[ps], signal=last)
                            if tt % 2 == 0:
                                sch.op("act", lambda e: e.copy(out=vo.ap, in_=ps.ap[:, 0:CWV]), reads=[ps], writes=[vo])
                            else:
                                sch.op("dve", lambda e: e.tensor_copy(out=vo.ap, in_=ps.ap[:, 0:CWV]),
                                       reads=[ps], writes=[vo])
                            sch.dma(vS[t0 + tt * 128:t0 + (tt + 1) * 128, nt * CWV:(nt + 1) * CWV], vo.ap, vo, False)
                        wtm_r.done(w)
                    jobsA.append((ld_v, cp_v, True))
                stg_ = {}

                def ld_vg(stg_=stg_):
                    stg_["w"] = [wtm_r.take() for _ in range(NVT)]
                    for nt in range(NVT):
                        sch.dma(stg_["w"][nt].ap, wB[l * NB + NVT + nt], stg_["w"][nt], True)

                def cp_vg(stg_=stg_, t0=t0, hTs=hTs):
                    wv = stg_["w"]

                    def part_a(tt):
                        vgb, vnb, stb_ = vg[tt % 2], vn[tt % 2], st[tt % 2]
                        for nt in range(NVT):
                            ps = PA[pctr["bi"] % 2][1]
                            pctr["bi"] += 1
                            w = wv[nt]
                            for k in range(KD):
                                last = (k == KD - 1)
                                sch.op("pe", lambda e: e.matmul(
                                    ps.ap[:, 0:CWV], lhsT=hTs[k].ap[:, tt * 128:(tt + 1) * 128],
                                    rhs=w.ap[:, k * CWV:(k + 1) * CWV], start=(k == 0), stop=last),
                                    reads=[w, hTs[k]], writes=[ps], signal=last)
                            sch.op("act", lambda e: e.activation(
                                out=vgb.ap[:, nt * CWV:(nt + 1) * CWV], in_=ps.ap[:, 0:CWV], func=AF.Gelu,
                                accum_out=stb_.ap[:, nt:nt + 1]),
                                reads=[ps], writes=[vgb, stb_])
                        sch.op("act", lambda e: e.activation(
                            out=vnb.ap, in_=vgb.ap, func=AF.Square, accum_out=stb_.ap[:, 4:5]),
                            reads=[vgb], writes=[vnb, stb_])
                        if NVT == 2:
                            sch.op("dve", lambda e: e.tensor_tensor(
                                out=stb_.ap[:, 5:6], in0=stb_.ap[:, 0:1], in1=stb_.ap[:, 1:2], op=ALU.add),
                                reads=[stb_], writes=[stb_])
                        else:
                            sch.op("dve", lambda e: e.tensor_copy(out=stb_.ap[:, 5:6], in_=stb_.ap[:, 0:1]),
                                   reads=[stb_], writes=[stb_])
                        sch.op("dve", lambda e: e.tensor_scalar(
                            out=stb_.ap[:, 8:9], in0=stb_.ap[:, 5:6], scalar1=1.0 / DG, scalar2=None, op0=ALU.mult),
                            reads=[stb_], writes=[stb_])
                        sch.op("dve", lambda e: e.tensor_tensor(
                            out=stb_.ap[:, 6:7], in0=stb_.ap[:, 8:9], in1=stb_.ap[:, 8:9], op=ALU.mult),
                            reads=[stb_], writes=[stb_])
                        sch.op("dve", lambda e: e.scalar_tensor_tensor(
                            out=stb_.ap[:, 9:10], in0=stb_.ap[:, 4:5], scalar=1.0 / DG, in1=stb_.ap[:, 6:7],
                            op0=ALU.mult, op1=ALU.subtract),
                            reads=[stb_], writes=[stb_])
                        sch.op("act", lambda e: e.activation(
                            out=stb_.ap[:, 7:8], in_=stb_.ap[:, 9:10], func=AF.Sqrt, bias=eps_b.ap[:, 0:1], scale=1.0),
                            reads=[stb_, eps_b], writes=[stb_])
                        sch.op("dve", lambda e: e.reciprocal(out=stb_.ap[:, 10:11], in_=stb_.ap[:, 7:8]),
                               reads=[stb_], writes=[stb_])
                        sch.op("dve", lambda e: e.tensor_scalar(
                            out=vnb.ap, in0=vgb.ap, scalar1=stb_.ap[:, 8:9], scalar2=stb_.ap[:, 10:11],
                            op0=ALU.subtract, op1=ALU.mult),
                            reads=[vgb, stb_], writes=[vnb])

                    def part_b(tt):
                        vnb = vn[tt % 2]
                        for g in range(G):
                            ps = PG[(g // 4) % 2]
                            col = (g % 4) * 128
                            sch.op("pe", lambda e: e.matmul(
                                ps.ap[:, col:col + 128], lhsT=vnb.ap[:, g * 128:(g + 1) * 128],
                                rhs=wsT_b.ap[:, (l * G + g) * 128:(l * G + g + 1) * 128], start=True, stop=True),
                                reads=[vnb, wsT_b], writes=[ps])
                            gt = gtmp[g % 2]
                            sch.op("dve", lambda e: e.scalar_tensor_tensor(
                                out=gt.ap, in0=ps.ap[:, col:col + 128], scalar=lng_b.ap[:, l * G + g:l * G + g + 1],
                                in1=bsb.ap[:, g * 128:(g + 1) * 128], op0=ALU.mult, op1=ALU.add),
                                reads=[ps, lng_b, bsb], writes=[gt])
                            sch.op("pool", lambda e: e.tensor_tensor(
                                out=GT[g].ap[:, tt * 128:(tt + 1) * 128], in0=gt.ap,
                                in1=uT[g].ap[:, tt * 128:(tt + 1) * 128], op=ALU.mult),
                                reads=[gt, uT[g]], writes=[GT[g]])

                    ntt = T // 128
                    part_a(0)
                    for tt in range(ntt):
                        if tt + 1 < ntt:
                            part_a(tt + 1)
                        part_b(tt)
                    for w in wv:
                        wtm_r.done(w)
                    rms_stats(GT, sqr, PST, rstd, sqt, DG)
                    for g in range(G):
                        mo = mgo[g % 2]
                        sch.op("dve", lambda e: e.scalar_tensor_tensor(
                            out=mo.ap, in0=GT[g].ap, scalar=gg_b.ap[:, l * G + g:l * G + g + 1], in1=rstd.ap,
                            op0=ALU.mult, op1=ALU.mult),
                            reads=[GT[g], gg_b, rstd], writes=[mo])
                        sch.dma(mgT[g * 128:(g + 1) * 128, t0:t0 + T], mo.ap, mo, False)
                jobsA.append((ld_vg, cp_vg, False))
            run_jobs(jobsA, 2)

            phase_begin()
            def allocB1():
                qh = [ar.alloc([S], BF16, f"qh{i}") for i in range(2)]
                kh = [ar.alloc([S], BF16, f"kh{i}") for i in range(2)]
                Vr = [[ar.alloc([S // 128, 128], BF16, f"V{r}_{i}") for i in range(2)] for r in PATTERNS]
                num = ar.alloc([S], F32, "num")
                den = ar.alloc([S], F32, "den")
                oh = [ar.alloc([S], BF16, f"oh{i}") for i in range(2)]
                NPE, NPM = 4, 10
                pet = [ar.alloc([384], BF16, f"pet{i}") for i in range(NPE)]
                pmt = [ar.alloc([384], BF16, f"pmt{i}") for i in range(NPM)]
                cstg = [ar.alloc([PC], F32, f"cstg{i}") for i in range(5)]
                cstb = [ar.alloc([PC], BF16, f"cstb{i}") for i in range(3)]
                return dict(locals())
            nsB = bufs("B1", allocB1)
            qh, kh, Vr, num, den, oh, NPE, NPM, pet, pmt, cstg, cstb = (
                nsB[k] for k in ("qh", "kh", "Vr", "num", "den", "oh", "NPE", "NPM", "pet", "pmt", "cstg", "cstb"))
            cast_upto = max(cast_state["next"], cast_mark_R[l])
            cast_start = cast_state["next"]
            cast_n = cast_upto - cast_start
            PS_S = banks[0:3]
            PS_O = banks[3:5]
            PS_D = banks[5:7]
            LA = 5

            def load_head(h):
                i = h % 2
                sch.dma(qh[i].ap, qT[h * 128:(h + 1) * 128, :], qh[i], True)
                sch.dma(kh[i].ap, kT[h * 128:(h + 1) * 128, :], kh[i], True)
                for pi_, r in enumerate(PATTERNS):
                    vb = Vr[pi_][i]
                    nch = S // (128 * r)
                    if r == 1:
                        src = vS.rearrange("(c p) f -> p c f", p=128)[:, :, h * 128:(h + 1) * 128]
                        sch.dma(vb.ap, src, vb, True)
                    else:
                        for res in range(r):
                            src = vS.rearrange("(c p r) f -> p r c f", p=128, r=r)[:, res, :, h * 128:(h + 1) * 128]
                            sch.dma(vb.ap[:, res * nch:(res + 1) * nch, :], src, vb, True)

            load_head(0)
            if os.environ.get("DBG_BAR"):
                sch.barrier()
            sct = 0
            for h in range(H):
                i = h % 2
                if h + 1 < H and not os.environ.get("DBG_NOPF"):
                    load_head(h + 1)
                qb, kb = qh[i], kh[i]
                qk_jobs = []
                pv_jobs = []
                order = []
                for pi_, r in enumerate(PATTERNS):
                    nch = S // (128 * r)
                    nb = min(4, nch)
                    for res in range(r):
                        for cc in range(nch):
                            qk_jobs.append((pi_, r, res, cc, nch))
                            order.append(("qk", len(qk_jobs) - 1))
                            if cc >= 1:
                                pv_jobs.append((pi_, r, res, cc - 1, nch, nb))
                                order.append(("pv", len(pv_jobs) - 1))
                        pv_jobs.append((pi_, r, res, nch - 1, nch, nb))
                        order.append(("pv", len(pv_jobs) - 1))
                pm_of = {}
                qk_done = 0

                def emit_qk(idx):
                    nonlocal sct
                    pi_, r, res, cc, nch = qk_jobs[idx]
                    blo = max(0, cc - 1)
                    bhi = min(nch, cc + 2)
                    n = (bhi - blo) * 128
                    boff = (blo - (cc - 1)) * 128
                    ps = PS_S[sct % 3]
                    pe_t = pet[sct % NPE]
                    pm_t = pmt[sct % NPM]
                    sct += 1
                    k0 = 128 * cc * r + res
                    q0 = 128 * blo * r + res
                    sch.op("pe", lambda e, ps=ps, n=n, k0=k0, q0=q0, r=r: e.matmul(
                        ps.ap[:, 0:n], lhsT=kb.ap[:, k0:k0 + 127 * r + 1:r], rhs=qb.ap[:, q0:q0 + (n - 1) * r + 1:r],
                        start=True, stop=True), reads=[kb, qb], writes=[ps])
                    sch.op("act", lambda e, ps=ps, n=n, pe_t=pe_t: e.activation(
                        out=pe_t.ap[:, 0:n], in_=ps.ap[:, 0:n], func=AF.Exp, scale=scale_qk),
                        reads=[ps], writes=[pe_t])
                    sch.op("pool" if sct % 3 == 0 else "dve", lambda e, n=n, pe_t=pe_t, pm_t=pm_t, boff=boff: e.tensor_tensor(
                        out=pm_t.ap[:, 0:n], in0=pe_t.ap[:, 0:n], in1=band_b.ap[:, boff:boff + n], op=ALU.mult),
                        reads=[pe_t, band_b], writes=[pm_t])
                    pm_of[(pi_, res, cc)] = (pm_t, blo, sct)

                oct_ = [0]

                def emit_pv(idx):
                    pi_, r, res, b, nch, nb = pv_jobs[idx]
                    vb = Vr[pi_][i]
                    bslot = b % nb
                    if bslot == 0:
                        oct_[0] += 1
                    po = PS_O[oct_[0] % 2]
                    pd = PS_D[oct_[0] % 2]
                    cs = [cc for cc in (b - 1, b, b + 1) if 0 <= cc < nch]
                    for ci_, cc in enumerate(cs):
                        pm_t, blo, sct_at = pm_of[(pi_, res, cc)]
                        assert sct - sct_at < NPM - 1, "pm ring too small"
                        col = (b - blo) * 128
                        first = (ci_ == 0)
                        last = (ci_ == len(cs) - 1)
                        vidx = res * nch + cc if r > 1 else cc
                        sch.op("pe", lambda e, po=po, pm_t=pm_t, col=col, vidx=vidx, first=first, last=last, bslot=bslot, vb=vb: e.matmul(
                            po.ap[:, bslot * 128:(bslot + 1) * 128], lhsT=vb.ap[:, vidx, :], rhs=pm_t.ap[:, col:col + 128],
                            start=first, stop=last), reads=[vb, pm_t], writes=[po], signal=False)
                        sch.op("pe", lambda e, pd=pd, pm_t=pm_t, col=col, first=first, last=last, bslot=bslot: e.matmul(
                            pd.ap[:, bslot * 128:(bslot + 1) * 128], lhsT=ones_b.ap, rhs=pm_t.ap[:, col:col + 128],
                            start=first, stop=last), reads=[ones_b, pm_t], writes=[pd], signal=True)
                    if bslot == nb - 1:
                        b0 = b - (nb - 1)
                        w = nb * 128
                        if r == 1:
                            nv = num.ap[:, b0 * 128:b0 * 128 + w]
                            dv = den.ap[:, b0 * 128:b0 * 128 + w]
                        else:
                            nv = num.ap.rearrange("p (m r) -> p r m", r=r)[:, res, b0 * 128:b0 * 128 + w]
                            dv = den.ap.rearrange("p (m r) -> p r m", r=r)[:, res, b0 * 128:b0 * 128 + w]
                        if pi_ == 0:
                            sch.op("dve", lambda e, nv=nv, po=po, w=w: e.tensor_copy(out=nv, in_=po.ap[:, 0:w]),
                                   reads=[po], writes=[num])
                            sch.op("dve", lambda e, dv=dv, pd=pd, w=w: e.tensor_copy(out=dv, in_=pd.ap[:, 0:w]),
                                   reads=[pd], writes=[den])
                        else:
                            sch.op("dve", lambda e, nv=nv, po=po, w=w: e.tensor_tensor(
                                out=nv, in0=po.ap[:, 0:w], in1=nv, op=ALU.add), reads=[po, num], writes=[num])
                            sch.op("dve", lambda e, dv=dv, pd=pd, w=w: e.tensor_tensor(
                                out=dv, in0=pd.ap[:, 0:w], in1=dv, op=ALU.add), reads=[pd, den], writes=[den])

                qk_positions = [k for k, (kind, _) in enumerate(order) if kind == "qk"]
                for pos, (kind, idx) in enumerate(order):
                    if kind == "qk":
                        while qk_done <= min(idx + LA, len(qk_jobs) - 1):
                            emit_qk(qk_done)
                            qk_done += 1
                    else:
                        emit_pv(idx)
                    tgt = cast_start + (cast_n * (h * len(order) + pos + 1)) // (H * len(order))
                    emit_cast(min(tgt, cast_upto), cstg, cstb)
                ob = oh[i]
                sch.op("dve", lambda e: e.reciprocal(out=den.ap, in_=den.ap), reads=[den], writes=[den])
                sch.op("pool", lambda e, ob=ob: e.tensor_tensor(out=ob.ap, in0=num.ap, in1=den.ap, op=ALU.mult),
                       reads=[num, den], writes=[ob])
                sch.dma(aT[h * 128:(h + 1) * 128, :], ob.ap, ob, False)
                if h == H - 1:
                    emit_cast(cast_upto, cstg, cstb, flush=True)
                if h + 1 < H and os.environ.get("DBG_NOPF"):
                    load_head(h + 1)
                if debug_outputs and h == 0 and l == 0:
                    for nm, bb, dt in (("d_num", num, F32), ("d_rden", den, F32), ("d_qh", qb, BF16), ("d_kh", kb, BF16),
                                       ("d_V1", Vr[0][0], BF16), ("d_V4", Vr[min(1, len(Vr) - 1)][0], BF16), ("d_V16", Vr[-1][0], BF16)):
                        dd = nc.dram_tensor(nm, [128, S], dt, kind="ExternalOutput").ap()
                        src_ap = bb.ap if len(bb.ap.shape) == 2 else bb.ap.rearrange("p a b -> p (a b)")
                        sch.dma(dd, src_ap, bb, False)

            phase_begin()
            def allocB2():
                aall = ar.alloc([H, T], BF16, "aall")
                mall = ar.alloc([G, T], BF16, "mall")
                mixa = [ar.alloc([T], BF16, f"mx{h}") for h in range(H)]
                sqr = [ar.alloc([T], BF16, f"sq{i}") for i in range(2)]
                rstd = ar.alloc([T], F32, "rstd")
                sqt = ar.alloc([T], F32, "sqt")
                wfm = [ar.alloc([TA], BF16, f"wfm{i}") for i in range(3)]
                xold = [ar.alloc([T], F32, f"xo{i}") for i in range(3)]
                return dict(locals())
            nsC = bufs("B2", allocB2)
            aall, mall, mixa, sqr, rstd, sqt, wfm, xold = (
                nsC[k] for k in ("aall", "mall", "mixa", "sqr", "rstd", "sqt", "wfm", "xold"))
            PA = [banks[0:2], banks[2:4], banks[4:6]]
            PST = banks[6:8]
            xsrc = xT_in if l == 0 else xT
            wfm_r = Ring(wfm)
            xo_r = Ring(xold)
            pctr = {"pa": 0}

            def next_pp3():
                pp = PA[pctr["pa"] % 3]
                pctr["pa"] += 1
                return pp

            jobsB = []
            for s in range(NST):
                t0 = s * T

                def ld_am(t0=t0):
                    sch.dma(aall.ap, aT[:, t0:t0 + T].rearrange("(h p) t -> p h t", p=128), aall, True)
                    sch.dma(mall.ap, mgT[:, t0:t0 + T].rearrange("(g p) t -> p g t", p=128), mall, True)

                def cp_am():
                    rms_stats([(aall, aall.ap[:, h, :]) for h in range(H)], sqr, PST, rstd, sqt, DA)
                    for h in range(H):
                        sch.op("dve", lambda e: e.scalar_tensor_tensor(
                            out=mixa[h].ap, in0=aall.ap[:, h, :], scalar=ga_b.ap[:, l * H + h:l * H + h + 1],
                            in1=rstd.ap, op0=ALU.mult, op1=ALU.mult), reads=[aall, ga_b, rstd], writes=[mixa[h]])
                jobsB.append((ld_am, cp_am, False))
                mix = [(mixa[h], mixa[h].ap) for h in range(H)] + [(mall, mall.ap[:, g, :]) for g in range(G)]
                for j in range(KD):
                    stx = {}

                    def ev_o(pp, j=j, stx=stx, t0=t0):
                        xo = stx["xo"]
                        for hi in range(NH):
                            ps = pp[hi]
                            sl = slice(hi * 512, (hi + 1) * 512)
                            sch.op("dve", lambda e: e.tensor_tensor(
                                out=xo.ap[:, sl], in0=ps.ap, in1=xo.ap[:, sl], op=ALU.add), reads=[ps, xo], writes=[xo])
                        sch.dma(xT[j * 128:(j + 1) * 128, t0:t0 + T], xo.ap, xo, False)
                        xo_r.done(xo)
                    tj2 = fm_tile_job(wfm_r, wA[l * NA + 2 * H + G + j], next_pp3, mix, KD, ev_o)

                    def ld_o(j=j, stx=stx, tj2=tj2, t0=t0):
                        tj2[0]()
                        xo = xo_r.take()
                        stx["xo"] = xo
                        sch.dma(xo.ap, xsrc[j * 128:(j + 1) * 128, t0:t0 + T], xo, True)
                    jobsB.append((ld_o, tj2[1], True))
            run_jobs(jobsB, 1)

            phase_begin()
            def allocC():
                hT = [ar.alloc([T], BF16, f"h2T{k}") for k in range(KD)]
                ffT = [ar.alloc([T], BF16, f"ffT{f}") for f in range(KF)]
                xst = [ar.alloc([T], F32, f"xst{i}") for i in range(2)]
                xo2 = [ar.alloc([T], F32, f"xo{i}") for i in range(2)]
                sqr = [ar.alloc([T], BF16, f"sq{i}") for i in range(2)]
                sqt = ar.alloc([T], F32, "sqt")
                rstd = sqt
                cstgC = [ar.alloc([PC], F32, f"cstgC{i}") for i in range(4)]
                cstbC = [ar.alloc([PC], BF16, f"cstbC{i}") for i in range(2)]
                wgs = [ar.alloc([TA], BF16, f"wg{i}") for i in range(2)]
                wus = [ar.alloc([TA], BF16, f"wu{i}") for i in range(2)]
                wds = [ar.alloc([TC], BF16, f"wd{i}") for i in range(2)]
                sg = [ar.alloc([T], BF16, f"sg{i}") for i in range(2)]
                return dict(locals())
            nsD = bufs("C", allocC)
            hT, ffT, xst, xo2, sqr, rstd, sqt, wgs, wus, wds, sg, cstgC, cstbC = (
                nsD[k] for k in ("hT", "ffT", "xst", "xo2", "sqr", "rstd", "sqt", "wgs", "wus", "wds", "sg",
                                 "cstgC", "cstbC"))
            PGt = [banks[0:2], banks[2:4]]
            PUp = [banks[4:6], banks[6:8]]
            PD = [banks[0:2], banks[2:4], banks[4:6]]
            PST = banks[6:8]
            xring = Ring(xst)
            xo_r = Ring(xo2)
            wg_r, wu_r, wd_r = Ring(wgs), Ring(wus), Ring(wds)
            pctr = {"pd": 0}

            def next_pd():
                pp = PD[pctr["pd"] % 3]
                pctr["pd"] += 1
                return pp

            def gu_job(f):
                st_ = {}

                def ld():
                    st_["wg"] = wg_r.take()
                    st_["wu"] = wu_r.take()
                    sch.dma(st_["wg"].ap, wA[l * NA + 2 * H + G + KD + f], st_["wg"], True)
                    sch.dma(st_["wu"].ap, wA[l * NA + 2 * H + G + KD + KF + f], st_["wu"], True)

                def cp():
                    wg_, wu_ = st_["wg"], st_["wu"]
                    pg, pu = PGt[f % 2], PUp[f % 2]
                    for w, pp in ((wg_, pg), (wu_, pu)):
                        for hi in range(NH):
                            ps = pp[hi]
                            for k in range(KD):
                                last = (k == KD - 1)
                                sch.op("pe", lambda e: e.matmul(
                                    ps.ap, lhsT=w.ap[:, k * 128:(k + 1) * 128], rhs=hT[k].ap[:, hi * 512:(hi + 1) * 512],
                                    start=(k == 0), stop=last), reads=[w, hT[k]], writes=[ps], signal=last)
                    wg_r.done(wg_)
                    wu_r.done(wu_)
                    sgb = sg[f % 2]
                    for hi in range(NH):
                        sl = slice(hi * 512, (hi + 1) * 512)
                        sch.op("act", lambda e: e.activation(
                            out=sgb.ap[:, sl], in_=pg[hi].ap, func=AF.Silu), reads=[pg[hi]], writes=[sgb])
                        sch.op("dve", lambda e: e.tensor_tensor(
                            out=ffT[f].ap[:, sl], in0=pu[hi].ap, in1=sgb.ap[:, sl], op=ALU.mult),
                            reads=[pu[hi], sgb], writes=[ffT[f]])
                return (ld, cp, True)

            def down_job(j, t0):
                stx = {}
                fb = [(ffT[k], ffT[k].ap) for k in range(KF)]

                def ev(pp):
                    xo = stx["xo"]
                    for hi in range(NH):
                        ps = pp[hi]
                        sl = slice(hi * 512, (hi + 1) * 512)
                        sch.op("dve", lambda e: e.tensor_tensor(
                            out=xo.ap[:, sl], in0=ps.ap, in1=xo.ap[:, sl], op=ALU.add), reads=[ps, xo], writes=[xo])
                    sch.dma(xT[j * 128:(j + 1) * 128, t0:t0 + T], xo.ap, xo, False)
                    xo_r.done(xo)
                tj = fm_tile_job(wd_r, wC[l * KD + j], next_pd, fb, KF, ev)

                def ld():
                    tj[0]()
                    xo = xo_r.take()
                    stx["xo"] = xo
                    sch.dma(xo.ap, xT[j * 128:(j + 1) * 128, t0:t0 + T], xo, True)
                return (ld, tj[1], True)

            jobsC = []
            nj = [norm_jobs(l, s, xT, g2_b, hT, xring, sqr, PST, rstd, sqt) for s in range(NST)]
            jobsC += nj[0]
            for s in range(NST):
                t0 = s * T
                for f in range(KF):
                    jobsC.append(gu_job(f))
                for j in range(KD):
                    jobsC.append(down_job(j, t0))
                    if s + 1 < NST:
                        jobsC.append(nj[s + 1][j])
                if s + 1 < NST:
                    jobsC += nj[s + 1][KD:]
            if l + 1 < L:
                c_start = cast_state["next"]
                c_n = cast_mark_R[l + 1] - c_start
                njc = len(jobsC)

                def wrap(jb, i):
                    def cp():
                        jb[1]()
                        emit_cast(c_start + (c_n * (i + 1)) // njc, cstgC, cstbC, flush=(i == njc - 1))
                    return (jb[0], cp, jb[2])
                jobsC = [wrap(jb, i) for i, jb in enumerate(jobsC)]
            run_jobs(jobsC, 1)

        phase_begin()
        xall = ar.alloc([KD, T], F32, "xall")
        sqr = [ar.alloc([T], BF16, f"sq{i}") for i in range(2)]
        rstd = ar.alloc([T], F32, "rstd")
        sqt = ar.alloc([T], F32, "sqt")
        PST = banks[0:2]
        for s in range(NST):
            t0 = s * T
            sch.dma(xall.ap, xT[:, t0:t0 + T].rearrange("(k p) t -> p k t", p=128), xall, True)
            rms_stats([(xall, xall.ap[:, kc, :]) for kc in range(KD)], sqr, PST, rstd, sqt, D)
            for kc in range(KD):
                eng = "dve"
                sch.op(eng, lambda e, kc=kc: e.scalar_tensor_tensor(
                    out=xall.ap[:, kc, :], in0=xall.ap[:, kc, :], scalar=gf_b.ap[:, kc:kc + 1], in1=rstd.ap,
                    op0=ALU.mult, op1=ALU.mult), reads=[xall, gf_b, rstd], writes=[xall])
            sch.dma(outT[:, t0:t0 + T].rearrange("(k p) t -> p k t", p=128), xall.ap, xall, False)
        if debug_outputs:
            phase_begin()
            for nm, src, dt in (("d_xT", xT, F32), ("d_qT", qT, BF16), ("d_kT", kT, BF16), ("d_v", vS, BF16),
                                ("d_aT", aT, BF16), ("d_mgT", mgT, BF16)):
                rows, cols = src.shape
                dst = nc.dram_tensor(nm, [rows, cols], dt, kind="ExternalOutput").ap()
                ar.off = const_end
                dbuf = [ar.alloc([cols], dt, f"dbg{nm}{i}") for i in range(2)]
                for rr in range(rows // 128):
                    b = dbuf[rr % 2]
                    sch.dma(b.ap, src[rr * 128:(rr + 1) * 128, :], b, True)
                    sch.dma(dst[rr * 128:(rr + 1) * 128, :], b.ap, b, False)
                sch.barrier()
        sch.barrier()

        with nc.Block() as block:
            @block.sync
            def _(e):
                sch.replay("sp", e)

            @block.tensor
            def _(e):
                sch.replay("pe", e)

            @block.scalar
            def _(e):
                sch.replay("act", e)

            @block.vector
            def _(e):
                sch.replay("dve", e)

            @block.gpsimd
            def _(e):
                sch.replay("pool", e)
    return nc, {k: len(v) for k, v in sch.streams.items()}


def tile_w(W, cw):
    K, N = W.shape
    return np.ascontiguousarray(
        W.reshape(K // 128, 128, N // cw, cw).transpose(2, 1, 0, 3)).reshape(N // cw, 128, (K // 128) * cw)


def host_prep(cfg, p):
    c = derive(cfg)
    S, D, L = c["S"], c["D"], c["DEPTH"]
    DA, DG, H, G, KD, KF, CWV = c["DA"], c["DG"], c["H"], c["G"], c["KD"], c["KF"], c["CWV"]
    wA, wB, wC = [], [], []
    for l in range(L):
        w_in = p["w_in"][l]
        tiles = [tile_w(w_in[:, 0:2 * DA], 128),
                 tile_w(w_in[:, 3 * DA:3 * DA + DG], 128),
                 tile_w(p["w_out"][l], 128),
                 tile_w(p["w_gate"][l], 128),
                 tile_w(p["w_up"][l], 128)]
        wA.append(np.concatenate(tiles, 0))
        wB.append(np.concatenate([tile_w(w_in[:, 2 * DA:3 * DA], CWV),
                                  tile_w(w_in[:, 3 * DA + DG:3 * DA + 2 * DG], CWV)], 0))
        wC.append(tile_w(p["w_down"][l], 128))
    shared = {
        "wA": np.concatenate(wA, 0), "wB": np.concatenate(wB, 0), "wC": np.concatenate(wC, 0),
    }

    def pp(v, n):
        return np.ascontiguousarray(v.reshape(L, n, 128).transpose(2, 0, 1)).reshape(128, L * n)

    shared["g1"] = pp(p["norm1_g"], KD)
    shared["g2"] = pp(p["norm2_g"], KD)
    shared["ga"] = pp(p["mix_norm_attn_g"], H)
    shared["gg"] = pp(p["mix_norm_gmlp_g"], G)
    shared["lng"] = pp(p["gmlp_ln_g"], G)
    shared["gf"] = np.ascontiguousarray(p["final_g"].reshape(KD, 128).T)
    shared["wsT"] = np.ascontiguousarray(p["w_spatial"].transpose(3, 0, 1, 2)).reshape(128, L * G * 128)
    shared["bs"] = np.ascontiguousarray(np.broadcast_to(p["b_spatial"].reshape(1, L * G * 128), (128, L * G * 128)))
    pos = np.arange(S, dtype=np.float32)
    inv = (np.float32(10000.0) ** (-np.arange(0, 128, 2, dtype=np.float32) / np.float32(128))).astype(np.float32)
    ang = (pos[None, :] * inv[:, None]).astype(np.float32)
    cosv = np.cos(ang).astype(np.float32)
    sinv = np.sin(ang).astype(np.float32)
    shared["cosT"] = np.ascontiguousarray(np.concatenate([cosv, cosv], 0))
    shared["sinS"] = np.ascontiguousarray(np.concatenate([-sinv, sinv], 0))
    kk = np.arange(128)[:, None]
    jj = np.arange(384)[None, :]
    shared["band"] = (np.abs(kk + 128 - jj) <= 64).astype(np.float32)
    return shared


FULL_CFG = dict(S=4096, D=2048, DEPTH=4, T=1024)
_CACHE = {}


def run(cfg, inputs, n_cores):
    p = {k: np.asarray(v, dtype=np.float32) for k, v in inputs.items()}
    shared = host_prep(cfg, p)
    key = tuple(sorted(cfg.items()))
    if key not in _CACHE:
        _CACHE[key] = build(cfg)[0]
    nc = _CACHE[key]
    x = p["x"]
    in_maps = []
    for b in range(n_cores):
        m = dict(shared)
        m["xT"] = np.ascontiguousarray(x[b].T)
        in_maps.append(m)
    res = run_bass_kernel_spmd(nc, in_maps, core_ids=list(range(n_cores)))
    out = np.stack([np.ascontiguousarray(res.results[b]["outT"].T) for b in range(n_cores)], 0)
    return out.astype(np.float32)


def kernel(**inputs):
    return run(FULL_CFG, inputs, 8)
```
